# Optimizing a Trainium2 kernel written in Bass

```python
import math
import jax
import jax.numpy as jnp
from jax import lax
import numpy as np

D_MODEL = 1024
BATCH = 16
SEQ = 256
DEPTH = 2
DEC_BATCH = 4
DEC_SEQ = 4096
PAST_LEN = 256

GRID_W = 64
ROPE_THETA = 10000.0
EPS = 1e-6
CHUNK = 64
Q_BLOCK = 128
CONV_K = 5
FFN_CONV_K = 3

GDN_HEADS = 4
GDN_DK = 64
GDN_DV = 64
GDN_WIDTH = GDN_HEADS * GDN_DV

MLA_HEADS = 8
MLA_Q_LORA = 256
MLA_KV_LORA = 128
MLA_NOPE = 64
MLA_ROPE = 32
MLA_V = 64
MLA_WIDTH = MLA_HEADS * MLA_V

SSD_HEADS = 4
SSD_HEAD_DIM = 64
SSD_INNER = SSD_HEADS * SSD_HEAD_DIM
SSD_GROUPS = 2
SSD_STATE = 128
SSD_CONV_CH = SSD_INNER + 2 * SSD_GROUPS * SSD_STATE

MIX_WIDTH = GDN_WIDTH + MLA_WIDTH + SSD_INNER
D_FF = 128 * ((8 * D_MODEL // 3 + 127) // 128)

IN_SIZES = (3 * GDN_WIDTH, GDN_WIDTH, 2 * GDN_HEADS, 2 * GDN_HEADS,
            MLA_Q_LORA, MLA_KV_LORA, MLA_ROPE,
            SSD_INNER, SSD_CONV_CH, 2 * SSD_HEADS)
IN_COLS = sum(IN_SIZES)
IN_OFFSETS = tuple(int(o) for o in np.cumsum(IN_SIZES)[:-1])

kernel_name = 'hybrid_gdn_mla_ssd_prefix_diffusion_step'


def rmsnorm(x, g):
    xf = x.astype(jnp.float32)
    y = xf * lax.rsqrt(jnp.mean(jnp.square(xf), axis=-1, keepdims=True) + EPS)
    return (y * g.astype(jnp.float32)).astype(x.dtype)


def l2norm(x):
    xf = x.astype(jnp.float32)
    return (xf * lax.rsqrt(jnp.sum(xf * xf, axis=-1, keepdims=True) + EPS)).astype(x.dtype)


def dwconv(x, w):
    k = w.shape[0]
    return lax.conv_general_dilated(
        x, w[:, None, :].astype(x.dtype), window_strides=(1,),
        padding=[(k // 2, k // 2)], dimension_numbers=('NWC', 'WIO', 'NWC'),
        feature_group_count=x.shape[-1])


def axial_rope(n_tokens):
    rows = n_tokens // GRID_W
    r, col = jnp.meshgrid(jnp.arange(rows, dtype=jnp.float32),
                          jnp.arange(GRID_W, dtype=jnp.float32), indexing='ij')
    r, col = r.reshape(-1), col.reshape(-1)
    nf = MLA_ROPE // 4
    inv = ROPE_THETA ** (-jnp.arange(nf, dtype=jnp.float32) / nf)
    ang = jnp.concatenate([r[:, None] * inv, col[:, None] * inv], axis=-1)
    return jnp.cos(ang), jnp.sin(ang)


def apply_rope(x, cos, sin):
    x1, x2 = jnp.split(x.astype(jnp.float32), 2, axis=-1)
    return jnp.concatenate([x1 * cos - x2 * sin, x1 * sin + x2 * cos], axis=-1).astype(x.dtype)


def gated_delta_chunked(q, k, v, log_a, beta, s0):
    bsz, length, heads, _ = k.shape
    dv = v.shape[-1]
    n = length // CHUNK
    f32 = jnp.float32

    def chunks(t):
        return t.astype(f32).reshape(bsz, n, CHUNK, heads, t.shape[-1]).transpose(1, 0, 3, 2, 4)

    qc, kc, vc = chunks(q), chunks(k), chunks(v)
    g = jnp.cumsum(chunks(log_a[..., None])[..., 0], axis=-1)
    bc = chunks(beta[..., None])
    incl = jnp.tril(jnp.ones((CHUNK, CHUNK), bool))
    strict = jnp.tril(jnp.ones((CHUNK, CHUNK), bool), -1)
    diff = g[..., :, None] - g[..., None, :]
    decay = jnp.where(incl, jnp.exp(jnp.where(incl, diff, 0.0)), 0.0)
    kb = kc * bc
    lmat = jnp.where(strict, jnp.einsum('nbhid,nbhjd->nbhij', kb, kc) * decay, 0.0)
    eye = jnp.eye(CHUNK, dtype=f32)
    t_inv = lax.linalg.triangular_solve(eye + lmat, jnp.broadcast_to(eye, lmat.shape),
                                        left_side=True, lower=True)
    u = jnp.einsum('nbhij,nbhjd->nbhid', t_inv, vc * bc)
    w = jnp.einsum('nbhij,nbhjd->nbhid', t_inv, kb * jnp.exp(g)[..., None])
    attn = jnp.where(incl, jnp.einsum('nbhid,nbhjd->nbhij', qc, kc) * decay, 0.0)
    q_dec = qc * jnp.exp(g)[..., None]
    g_last = g[..., -1]
    k_dec = kc * jnp.exp(g_last[..., None] - g)[..., None]

    def step(s, inp):
        u_i, w_i, attn_i, q_i, k_i, gl_i = inp
        v_new = u_i - jnp.einsum('bhcd,bhde->bhce', w_i, s)
        o = jnp.einsum('bhcd,bhde->bhce', q_i, s) + jnp.einsum('bhij,bhje->bhie', attn_i, v_new)
        s = s * jnp.exp(gl_i)[..., None, None] + jnp.einsum('bhcd,bhce->bhde', k_i, v_new)
        return s, o

    s_fin, o = lax.scan(step, s0.astype(f32), (u, w, attn, q_dec, k_dec, g_last))
    o = o.transpose(1, 0, 3, 2, 4).reshape(bsz, length, heads, dv)
    return o.astype(v.dtype), s_fin.astype(s0.dtype)


def ssd_chunked(x, dt, a, bm, cm, s0):
    bsz, length, heads, p = x.shape
    n = length // CHUNK
    f32 = jnp.float32

    def chunks(t):
        return t.astype(f32).reshape(bsz, n, CHUNK, *t.shape[2:])

    dtc = chunks(dt)
    xc = chunks(x) * dtc[..., None]
    bc, cc = chunks(bm), chunks(cm)
    acum = jnp.cumsum(dtc * a.astype(f32), axis=2)
    incl = jnp.tril(jnp.ones((CHUNK, CHUNK), bool))[:, :, None]
    seg = acum[:, :, :, None, :] - acum[:, :, None, :, :]
    lmat = jnp.where(incl, jnp.exp(jnp.where(incl, seg, 0.0)), 0.0)
    scores = jnp.einsum('bclhn,bcshn->bclsh', cc, bc) * lmat
    y_diag = jnp.einsum('bclsh,bcshp->bclhp', scores, xc)
    decay_states = jnp.exp(acum[:, :, -1:, :] - acum)
    chunk_states = jnp.einsum('bclhn,bclh,bclhp->bchpn', bc, decay_states, xc)
    chunk_decay = jnp.exp(acum[:, :, -1, :])

    def step(s, inp):
        st_i, dec_i = inp
        return s * dec_i[..., None, None] + st_i, s

    s_fin, s_prev = lax.scan(step, s0.astype(f32),
                             (chunk_states.swapaxes(0, 1), chunk_decay.swapaxes(0, 1)))
    s_prev = s_prev.swapaxes(0, 1)
    y_off = jnp.einsum('bclhn,bchpn,bclh->bclhp', cc, s_prev, jnp.exp(acum))
    y = (y_diag + y_off).reshape(bsz, length, heads, p)
    return y.astype(x.dtype), s_fin.astype(s0.dtype)


def block_attention(q_nope, q_rope, k_nope, k_rope, v):
    bsz, lq, heads, _ = q_nope.shape
    nb = lq // Q_BLOCK
    scale = (MLA_NOPE + MLA_ROPE) ** -0.5

    def one_block(blk):
        qn, qr = blk
        s = jnp.einsum('bqhd,bkhd->bhqk', qn, k_nope) + jnp.einsum('bqhr,bkr->bhqk', qr, k_rope)
        pr = jax.nn.softmax(s.astype(jnp.float32) * scale, axis=-1).astype(v.dtype)
        return jnp.einsum('bhqk,bkhd->bqhd', pr, v)

    qn_b = q_nope.reshape(bsz, nb, Q_BLOCK, heads, q_nope.shape[-1]).swapaxes(0, 1)
    qr_b = q_rope.reshape(bsz, nb, Q_BLOCK, heads, q_rope.shape[-1]).swapaxes(0, 1)
    out = lax.map(one_block, (qn_b, qr_b))
    return out.swapaxes(0, 1).reshape(bsz, lq, heads, v.shape[-1])


def token_mixers(h, lp, rope, ctx):
    bsz, length, _ = h.shape
    (gdn_qkv, gdn_gate, gdn_a, gdn_b, mla_cq, mla_ckv, mla_kr,
     ssd_z, ssd_xbc, ssd_dt) = jnp.split(h @ lp['w_in'], IN_OFFSETS, axis=-1)
    if ctx is None:
        s_gdn = jnp.zeros((bsz, 2, GDN_HEADS, GDN_DK, GDN_DV), h.dtype)
        s_ssd = jnp.zeros((bsz, 2, SSD_HEADS, SSD_HEAD_DIM, SSD_STATE), h.dtype)
    else:
        ctx_ckv, ctx_kr, s_gdn, s_ssd = ctx

    qkv = jax.nn.silu(dwconv(gdn_qkv, lp['gdn_conv']))
    q, k, v = jnp.split(qkv, 3, axis=-1)
    q = l2norm(q.reshape(bsz, length, GDN_HEADS, GDN_DK)) * (GDN_DK ** -0.5)
    k = l2norm(k.reshape(bsz, length, GDN_HEADS, GDN_DK))
    v = v.reshape(bsz, length, GDN_HEADS, GDN_DV)
    log_a = -jnp.exp(lp['gdn_a_log']) * jax.nn.softplus(
        gdn_a.reshape(bsz, length, 2, GDN_HEADS) + lp['gdn_dt_bias'])
    beta = jax.nn.sigmoid(gdn_b.reshape(bsz, length, 2, GDN_HEADS))
    o_f, sg_f = gated_delta_chunked(q, k, v, log_a[:, :, 0], beta[:, :, 0], s_gdn[:, 0])
    o_b, sg_b = gated_delta_chunked(q[:, ::-1], k[:, ::-1], v[:, ::-1],
                                    log_a[:, ::-1, 1], beta[:, ::-1, 1], s_gdn[:, 1])
    o_gdn = rmsnorm(o_f + o_b[:, ::-1], lp['gdn_norm']) * jax.nn.silu(
        gdn_gate.reshape(bsz, length, GDN_HEADS, GDN_DV))

    qm = (rmsnorm(mla_cq, lp['mla_q_norm']) @ lp['mla_w_uq']).reshape(
        bsz, length, MLA_HEADS, MLA_NOPE + MLA_ROPE)
    q_nope, q_rope = qm[..., :MLA_NOPE], qm[..., MLA_NOPE:]
    ckv = rmsnorm(mla_ckv, lp['mla_kv_norm'])
    if ctx is None:
        ckv_all, kr_all = ckv, mla_kr
    else:
        cos, sin = rope
        q_rope = apply_rope(q_rope, cos[:, None], sin[:, None])
        ckv_all = jnp.concatenate([ckv, ctx_ckv], axis=1)
        kr_all = jnp.concatenate([apply_rope(mla_kr, cos, sin), ctx_kr], axis=1)
    kv = (ckv_all @ lp['mla_w_ukv']).reshape(bsz, -1, MLA_HEADS, MLA_NOPE + MLA_V)
    o_mla = block_attention(q_nope, q_rope, kv[..., :MLA_NOPE], kr_all, kv[..., MLA_NOPE:])

    xbc = jax.nn.silu(dwconv(ssd_xbc, lp['ssd_conv']) + lp['ssd_conv_b'])
    xs, bm, cm = jnp.split(xbc, (SSD_INNER, SSD_INNER + SSD_GROUPS * SSD_STATE), axis=-1)
    xs = xs.reshape(bsz, length, SSD_HEADS, SSD_HEAD_DIM)
    rep = SSD_HEADS // SSD_GROUPS
    bm = jnp.repeat(bm.reshape(bsz, length, SSD_GROUPS, SSD_STATE), rep, axis=2)
    cm = jnp.repeat(cm.reshape(bsz, length, SSD_GROUPS, SSD_STATE), rep, axis=2)
    dt = jax.nn.softplus(ssd_dt.reshape(bsz, length, 2, SSD_HEADS) + lp['ssd_dt_bias'])
    a = -jnp.exp(lp['ssd_a_log'])
    y_f, ss_f = ssd_chunked(xs, dt[:, :, 0], a[0], bm, cm, s_ssd[:, 0])
    y_b, ss_b = ssd_chunked(xs[:, ::-1], dt[:, ::-1, 1], a[1], bm[:, ::-1], cm[:, ::-1], s_ssd[:, 1])
    y = y_f + y_b[:, ::-1] + xs * lp['ssd_d'][:, None]
    o_ssd = rmsnorm(y.reshape(bsz, length, SSD_INNER) * jax.nn.silu(ssd_z), lp['ssd_norm'])

    mixed = jnp.concatenate([o_gdn.reshape(bsz, length, GDN_WIDTH),
                             o_mla.reshape(bsz, length, MLA_WIDTH), o_ssd], axis=-1) @ lp['w_out']
    if ctx is None:
        return mixed, (ckv, mla_kr, jnp.stack([sg_f, sg_b], axis=1), jnp.stack([ss_f, ss_b], axis=1))
    return mixed, None


def conv_ffn(h, w_up, conv_w, w_down):
    gu = dwconv(h @ w_up, conv_w)
    g, u = jnp.split(gu, 2, axis=-1)
    return (jax.nn.silu(g) * u) @ w_down


def trunk_layer(x, cond, lp, rope, ctx):
    mod = (jax.nn.silu(cond) @ lp['w_ada'] + lp['b_ada']).reshape(-1, 1, 6 * D_MODEL)
    sh1, sc1, g1, sh2, sc2, g2 = jnp.split(mod, 6, axis=-1)
    h = rmsnorm(x, lp['norm_mix_pre']) * (1.0 + sc1) + sh1
    mixed, ctx_out = token_mixers(h, lp, rope, ctx)
    x = x + g1 * rmsnorm(mixed, lp['norm_mix_post'])
    h = rmsnorm(x, lp['norm_ffn_pre']) * (1.0 + sc2) + sh2
    f = conv_ffn(h, lp['ffn_w_up'], lp['ffn_conv'], lp['ffn_w_down'])
    x = x + g2 * rmsnorm(f, lp['norm_ffn_post'])
    return x, ctx_out


def setup_inputs(seed: int = 0) -> dict:
    key = jax.random.key(seed)
    ks = jax.random.split(key, 33)
    f32 = jnp.float32

    def nrm(i, shape, scale):
        return scale * jax.random.normal(ks[i], shape, f32)

    def gain(i, shape):
        return 1.0 + nrm(i, shape, 0.05)

    def dt_bias(i, shape):
        dt = jnp.exp(jax.random.uniform(ks[i], shape, f32, math.log(1e-3), math.log(1e-1)))
        return dt + jnp.log(-jnp.expm1(-dt))

    def a_log(i, shape):
        return jnp.log(jax.random.uniform(ks[i], shape, f32, 1.0, 16.0))

    return {
        'x_prompt': nrm(0, (BATCH, SEQ, D_MODEL), 1.0),
        'x_sample': nrm(1, (DEC_BATCH, DEC_SEQ, D_MODEL), 1.0),
        'cache_mla_ckv': nrm(2, (DEC_BATCH, DEPTH, PAST_LEN, MLA_KV_LORA), 1.0),
        'cache_mla_krope': nrm(3, (DEC_BATCH, DEPTH, PAST_LEN, MLA_ROPE), 1.0),
        'state_gdn': nrm(4, (DEC_BATCH, DEPTH, 2, GDN_HEADS, GDN_DK, GDN_DV), 0.3),
        'state_ssd': nrm(5, (DEC_BATCH, DEPTH, 2, SSD_HEADS, SSD_HEAD_DIM, SSD_STATE), 0.3),
        'c': nrm(6, (DEC_BATCH, D_MODEL), 1.0),
        'c_ctx': nrm(7, (D_MODEL,), 1.0),
        'w_ada': nrm(8, (DEPTH, D_MODEL, 6 * D_MODEL), 0.5 * D_MODEL ** -0.5),
        'b_ada': nrm(9, (DEPTH, 6 * D_MODEL), 0.02),
        'norm_mix_pre': gain(10, (DEPTH, D_MODEL)),
        'norm_mix_post': gain(11, (DEPTH, D_MODEL)),
        'norm_ffn_pre': gain(12, (DEPTH, D_MODEL)),
        'norm_ffn_post': gain(13, (DEPTH, D_MODEL)),
        'w_in': nrm(14, (DEPTH, D_MODEL, IN_COLS), D_MODEL ** -0.5),
        'gdn_conv': nrm(15, (DEPTH, CONV_K, 3 * GDN_WIDTH), CONV_K ** -0.5),
        'gdn_a_log': a_log(16, (DEPTH, 2, GDN_HEADS)),
        'gdn_dt_bias': dt_bias(17, (DEPTH, 2, GDN_HEADS)),
        'gdn_norm': gain(18, (DEPTH, GDN_DV)),
        'mla_q_norm': gain(19, (DEPTH, MLA_Q_LORA)),
        'mla_w_uq': nrm(20, (DEPTH, MLA_Q_LORA, MLA_HEADS * (MLA_NOPE + MLA_ROPE)), MLA_Q_LORA ** -0.5),
        'mla_kv_norm': gain(21, (DEPTH, MLA_KV_LORA)),
        'mla_w_ukv': nrm(22, (DEPTH, MLA_KV_LORA, MLA_HEADS * (MLA_NOPE + MLA_V)), MLA_KV_LORA ** -0.5),
        'ssd_conv': nrm(23, (DEPTH, CONV_K, SSD_CONV_CH), CONV_K ** -0.5),
        'ssd_conv_b': nrm(24, (DEPTH, SSD_CONV_CH), 0.02),
        'ssd_a_log': a_log(25, (DEPTH, 2, SSD_HEADS)),
        'ssd_dt_bias': dt_bias(26, (DEPTH, 2, SSD_HEADS)),
        'ssd_d': gain(27, (DEPTH, SSD_HEADS)),
        'ssd_norm': gain(28, (DEPTH, SSD_INNER)),
        'w_out': nrm(29, (DEPTH, MIX_WIDTH, D_MODEL), MIX_WIDTH ** -0.5),
        'ffn_w_up': nrm(30, (DEPTH, D_MODEL, 2 * D_FF), D_MODEL ** -0.5),
        'ffn_conv': nrm(31, (DEPTH, FFN_CONV_K, 2 * D_FF), FFN_CONV_K ** -0.5),
        'ffn_w_down': nrm(32, (DEPTH, D_FF, D_MODEL), D_FF ** -0.5),
    }


def reference(x_prompt, x_sample, cache_mla_ckv, cache_mla_krope, state_gdn, state_ssd, c, c_ctx,
              w_ada, b_ada, norm_mix_pre, norm_mix_post, norm_ffn_pre, norm_ffn_post, w_in,
              gdn_conv, gdn_a_log, gdn_dt_bias, gdn_norm, mla_q_norm, mla_w_uq, mla_kv_norm,
              mla_w_ukv, ssd_conv, ssd_conv_b, ssd_a_log, ssd_dt_bias, ssd_d, ssd_norm, w_out,
              ffn_w_up, ffn_conv, ffn_w_down):
    rope = axial_rope(x_sample.shape[1])
    y_prompt, y_sample = x_prompt, x_sample
    ckv_l, kr_l, sg_l, ss_l = [], [], [], []
    for l in range(DEPTH):
        lp = {
            'w_ada': w_ada[l], 'b_ada': b_ada[l],
            'norm_mix_pre': norm_mix_pre[l], 'norm_mix_post': norm_mix_post[l],
            'norm_ffn_pre': norm_ffn_pre[l], 'norm_ffn_post': norm_ffn_post[l],
            'w_in': w_in[l],
            'gdn_conv': gdn_conv[l], 'gdn_a_log': gdn_a_log[l], 'gdn_dt_bias': gdn_dt_bias[l],
            'gdn_norm': gdn_norm[l],
            'mla_q_norm': mla_q_norm[l], 'mla_w_uq': mla_w_uq[l],
            'mla_kv_norm': mla_kv_norm[l], 'mla_w_ukv': mla_w_ukv[l],
            'ssd_conv': ssd_conv[l], 'ssd_conv_b': ssd_conv_b[l], 'ssd_a_log': ssd_a_log[l],
            'ssd_dt_bias': ssd_dt_bias[l], 'ssd_d': ssd_d[l], 'ssd_norm': ssd_norm[l],
            'w_out': w_out[l],
            'ffn_w_up': ffn_w_up[l], 'ffn_conv': ffn_conv[l], 'ffn_w_down': ffn_w_down[l],
        }
        y_prompt, (ckv, kr, sg, ss) = trunk_layer(y_prompt, c_ctx, lp, None, None)
        ckv_l.append(ckv)
        kr_l.append(kr)
        sg_l.append(sg)
        ss_l.append(ss)
        y_sample, _ = trunk_layer(
            y_sample, c, lp, rope,
            (cache_mla_ckv[:, l], cache_mla_krope[:, l], state_gdn[:, l], state_ssd[:, l]))
    new_mla_ckv = jnp.stack(ckv_l, axis=1)
    new_mla_krope = jnp.stack(kr_l, axis=1)
    new_state_gdn = jnp.stack(sg_l, axis=1)
    new_state_ssd = jnp.stack(ss_l, axis=1)
    return (y_prompt, y_sample, new_mla_ckv, new_mla_krope, new_state_gdn, new_state_ssd)
```

```python
import threading
import numpy as np
import concourse.bass as bass
import concourse.mybir as mybir
from concourse.bass_utils import run_bass_kernel_spmd

F32 = mybir.dt.float32
BF16 = mybir.dt.bfloat16
F32R = mybir.dt.float32r
AF = mybir.ActivationFunctionType
ALU = mybir.AluOpType

D = 1024
DEPTH = 2
BATCH = 16
SEQ = 256
DEC_BATCH = 4
DEC_SEQ = 4096
PAST = 256
GRID_W = 64
EPS = 1e-6
DFF = 2816
NCORES = 8
NPR = BATCH // NCORES
NG_IN = 23
G_Q, G_K, G_V, G_GATE, G_CQ, G_CKV, G_KR, G_KRB, G_AB, G_Z, G_XS, G_BM, G_CM = 0, 2, 4, 6, 8, 10, 11, 12, 13, 14, 16, 18, 20
C_ID, C_TRIF, C_TRIB, C_TRISF, C_TRISB, C_NEGTF, C_NEGTB, C_POSSF, C_POSSB, C_ONESBD, C_ONES = range(11)
NCONST = 11
PK_NMP, PK_NMO, PK_NFP, PK_NFO, PK_BADA = 0, 8, 16, 24, 32
PK_GCONV = 80
PK_SCONV = 110
PK_SCB = 140
PK_FCONV = 146
PK_QN = 278
PK_KVN = 280
PK_GDNN = 281
PK_SSDN = 282
PK_SSDD = 284
PK_ALOG = 286
PK_DTB = 287
NPK = 288
OVERLAP_PROMPTS = True


class Buf:
    __slots__ = ("w", "r", "excl")

    def __init__(self, excl=False):
        self.w = None
        self.r = {}
        self.excl = excl


class V:
    __slots__ = ("ap", "buf")

    def __init__(self, ap, buf):
        self.ap = ap
        self.buf = buf

    def bitcast(self, dt):
        return V(self.ap.bitcast(dt), self.buf)

    def bc(self, shape):
        return V(self.ap.to_broadcast(shape), self.buf)


class Tl:
    def __init__(self, t, buf=None):
        self.t = t
        self.buf = buf if buf is not None else Buf()

    def __getitem__(self, idx):
        return V(self.t[idx], self.buf)


class Prog:
    CE = ("pe", "dve", "act", "pool")

    def __init__(self, nc):
        self.nc = nc
        self.eng = {"pe": nc.tensor, "dve": nc.vector, "act": nc.scalar, "pool": nc.gpsimd, "sp": nc.sync}
        self.sem = {}
        self.cnt = {}
        self.sid = 0
        for e in self.CE:
            self.sem[e] = self._newsem(e)
        self.dsem = {"sp": [self._newsem("dsp%d" % i) for i in range(20)],
                     "pool": [self._newsem("dpl%d" % i) for i in range(8)]}
        self.drr = {"sp": 0, "pool": 0}
        self.waited = {e: {} for e in self.eng}
        self.ninst = 0
        self.nwait = 0
        self.on_barrier = None
        self.on_op = None
        self.tok = None

    def _newsem(self, name):
        h = self.nc.alloc_semaphore(name)
        s = (self.sid, h)
        self.cnt[self.sid] = 0
        self.sid += 1
        return s

    def _wait(self, e, deps):
        best = {}
        for (s, v) in deps:
            if best.get(s, (None, 0))[1] < v:
                best[s] = (s, v)
        for s, v in best.values():
            if self.waited[e].get(s[0], 0) < v:
                self.eng[e].wait_ge(s[1], v)
                self.waited[e][s[0]] = v
                self.nwait += 1

    def _deps(self, e, reads, writes, pe_acc=False):
        deps = []
        for b in reads:
            if b.w is not None:
                deps.append(b.w)
            if b.excl and e in self.sem:
                me = self.sem[e][0]
                for sid, tok in b.r.items():
                    if sid != me:
                        deps.append(tok)
        for b in writes:
            if b.w is not None:
                if not (pe_acc and b.w[0] is self.sem["pe"]):
                    deps.append(b.w)
            for s, v in b.r.values():
                deps.append((s, v))
        return deps

    def _commit(self, tok, reads, writes):
        if self.tok is not None:
            self.tok[tok[0][0]] = tok
        for b in reads:
            b.r[tok[0][0]] = tok
        for b in writes:
            b.w = tok
            b.r = {}

    def op(self, e, fn, reads, writes, pe_acc=False):
        reads = [v.buf for v in reads if v is not None]
        writes = [v.buf for v in writes if v is not None]
        self._wait(e, self._deps(e, reads, writes, pe_acc))
        inst = fn(self.eng[e])
        s = self.sem[e]
        self.cnt[s[0]] += 1
        inst.then_inc(s[1], 1)
        self.ninst += 1
        self._commit((s, self.cnt[s[0]]), reads, writes)
        if self.on_op is not None:
            self.on_op()

    def dma(self, out, in_, q="sp", slow=False):
        reads = [in_.buf]
        writes = [out.buf]
        sl = self.dsem[q]
        s = sl[self.drr[q] % len(sl)]
        self.drr[q] += 1
        deps = self._deps(q, reads, writes)
        if self.cnt[s[0]] > 0:
            deps.append((s, self.cnt[s[0]]))
        self._wait(q, deps)
        kw = {"allow_slow_non_contiguous": True} if slow else {}
        inst = self.eng[q].dma_start(out=out.ap, in_=in_.ap, **kw)
        self.cnt[s[0]] += 16
        inst.then_inc(s[1], 16)
        self.ninst += 1
        self._commit((s, self.cnt[s[0]]), reads, writes)
        if self.on_op is not None:
            self.on_op()

    def barrier(self, local=False):
        if local and self.tok is not None:
            deps = list(self.tok.values())
        else:
            allsems = [self.sem[e] for e in self.CE] + self.dsem["sp"] + self.dsem["pool"]
            deps = [(s, self.cnt[s[0]]) for s in allsems if self.cnt[s[0]] > 0]
        for e in self.eng:
            self._wait(e, deps)
        if self.on_barrier is not None:
            self.on_barrier(local)

    def mm(self, out, lhsT, rhs, start=True, stop=True):
        self.op("pe", lambda E: E.matmul(out.ap, lhsT=lhsT.ap, rhs=rhs.ap, start=start, stop=stop),
                [lhsT, rhs], [out], pe_acc=not start)

    def tr(self, out, in_, ident):
        self.op("pe", lambda E: E.transpose(out.ap, in_.ap, ident.ap), [in_, ident], [out])

    def act(self, out, in_, func, bias=None, scale=None, accum=None, after=()):
        kw = {}
        rd = [in_] + list(after)
        if bias is not None:
            if isinstance(bias, V):
                kw["bias"] = bias.ap
                rd.append(bias)
            else:
                kw["bias"] = float(bias)
        if scale is not None:
            if isinstance(scale, V):
                kw["scale"] = scale.ap
                rd.append(scale)
            else:
                kw["scale"] = float(scale)
        wr = [out]
        if accum is not None:
            kw["accum_out"] = accum.ap
            wr.append(accum)
        self.op("act", lambda E: E.activation(out=out.ap, in_=in_.ap, func=func, **kw), rd, wr)

    def tt(self, out, in0, in1, op, e="dve"):
        self.op(e, lambda E: E.tensor_tensor(out=out.ap, in0=in0.ap, in1=in1.ap, op=op), [in0, in1], [out])

    def ts(self, out, in0, s1, s2=None, op0=ALU.mult, op1=None, e="dve"):
        rd = [in0]
        a1 = s1
        if isinstance(s1, V):
            a1 = s1.ap
            rd.append(s1)
        a2 = s2
        if isinstance(s2, V):
            a2 = s2.ap
            rd.append(s2)
        kw = {}
        if op1 is not None:
            kw["op1"] = op1
        self.op(e, lambda E: E.tensor_scalar(out=out.ap, in0=in0.ap, scalar1=a1, scalar2=a2, op0=op0, **kw),
                rd, [out])

    def stt(self, out, in0, sc, in1, op0, op1):
        rd = [in0, in1]
        a = sc
        if isinstance(sc, V):
            a = sc.ap
            rd.append(sc)
        self.op("dve", lambda E: E.scalar_tensor_tensor(out=out.ap, in0=in0.ap, scalar=a, in1=in1.ap,
                                                       op0=op0, op1=op1), rd, [out])

    def cp(self, out, in_, e="dve"):
        if e == "act":
            self.op("act", lambda E: E.copy(out=out.ap, in_=in_.ap), [in_], [out])
        else:
            self.op(e, lambda E: E.tensor_copy(out=out.ap, in_=in_.ap), [in_], [out])

    def memset(self, v, val, e="pool"):
        self.op(e, lambda E: E.memset(v.ap, val), [], [v])

    def recip(self, out, in_):
        self.op("dve", lambda E: E.reciprocal(out=out.ap, in_=in_.ap), [in_], [out])


class Arena:
    def __init__(self, nc, lo, hi):
        self.nc = nc
        self.lo = lo
        self.hi = hi
        self.p = lo
        self.n = 0
        self.peak = lo
        self.reg = []

    def mark(self):
        return self.p

    def release(self, m):
        self.p = m

    def tile_at(self, off, shape, dt, name="t"):
        self.n += 1
        t = self.nc.alloc_sbuf_tensor_at("%s_%d" % (name, self.n), list(shape), dt, offset=off)
        tl = Tl(t)
        nb = self.nbytes(shape, dt)
        keep = []
        for (o, n, old) in self.reg:
            if o < off + nb and off < o + n:
                toks = list(old.buf.r.values())
                if old.buf.w is not None:
                    toks.append(old.buf.w)
                for tok in toks:
                    sid = tok[0][0]
                    if tl.buf.r.get(sid, (None, 0))[1] < tok[1]:
                        tl.buf.r[sid] = tok
                if not (off <= o and o + n <= off + nb):
                    keep.append((o, n, old))
            else:
                keep.append((o, n, old))
        keep.append((off, nb, tl))
        self.reg = keep
        return tl

    @staticmethod
    def nbytes(shape, dt):
        n = 1
        for s in shape[1:]:
            n *= s
        return (n * (2 if dt == BF16 else 4) + 31) // 32 * 32

    def tile(self, shape, dt, name="t"):
        nb = self.nbytes(shape, dt)
        off = self.p
        assert off + nb <= self.hi, "SBUF arena overflow %s %d+%d>%d" % (name, off, nb, self.hi)
        self.p += nb
        self.peak = max(self.peak, self.p)
        self.peak_since = max(getattr(self, "peak_since", 0), self.p)
        return self.tile_at(off, shape, dt, name)


class PsumPool:
    def __init__(self, nc=None, banks=None):
        if banks is None:
            banks = [Tl(nc.alloc_psum_tensor("psb%d" % i, [128, 512], F32), Buf(excl=True)) for i in range(8)]
        self.banks = banks
        self.res = set()
        self.i = 0

    def get(self):
        while True:
            k = self.i % len(self.banks)
            self.i += 1
            if k not in self.res:
                return self.banks[k]

    def reserve(self):
        b = self.get()
        self.res.add(self.banks.index(b))
        return b

    def free(self, b):
        self.res.discard(self.banks.index(b))


def dv(ap):
    return V(ap, Buf())


def sub(tl, ap):
    return V(ap, tl.buf)


class Ctx:
    pass


CTX_NAMES = ("A", "ps", "GWT", "GWTMP", "gw_key", "pa", "qt", "ckvd", "krd", "catd", "actT",
             "_oi", "_wi", "_obi", "_ny", "dI", "hl")


class Coop:
    def __init__(self):
        self.evs = []
        self.alive = []
        self.cur = 0
        self.exc = None
        self.on_resume = None

    def run(self, fns):
        n = len(fns)
        self.evs = [threading.Event() for _ in range(n)]
        self.alive = [True] * n
        done = threading.Event()

        def wrap(i, fn):
            self.evs[i].wait()
            try:
                fn()
            except BaseException as e:
                self.exc = e
            self.alive[i] = False
            nxt = self._next(i)
            if nxt is None:
                done.set()
            else:
                self.cur = nxt
                self.evs[nxt].set()

        ths = [threading.Thread(target=wrap, args=(i, f)) for i, f in enumerate(fns)]
        for t in ths:
            t.start()
        self.cur = 0
        self.evs[0].set()
        done.wait()
        for t in ths:
            t.join()
        self.evs = []
        if self.exc is not None:
            raise self.exc

    def _next(self, i):
        n = len(self.alive)
        for k in range(1, n + 1):
            j = (i + k) % n
            if self.alive[j] and j != i:
                return j
        return None

    def switch(self):
        if not self.evs:
            return
        i = self.cur
        nxt = self._next(i)
        if nxt is None:
            return
        self.evs[i].clear()
        self.cur = nxt
        self.evs[nxt].set()
        self.evs[i].wait()
        if self.on_resume is not None:
            self.on_resume()


class Builder:
    def __getattr__(self, name):
        if name in CTX_NAMES:
            return getattr(self.__dict__["_tls"].ctx, name)
        raise AttributeError(name)

    def __setattr__(self, name, val):
        if name in CTX_NAMES:
            setattr(self.__dict__["_tls"].ctx, name, val)
        else:
            self.__dict__[name] = val

    def use_ctx(self, ctx):
        self._tls.ctx = ctx
        self.P.tok = ctx.tokens

    def barrier(self):
        ctx = self._tls.ctx
        self.P.tok = ctx.tokens
        if self.coop.evs:
            self.P.barrier(local=True)
        else:
            self.P.barrier()

    def cswitch(self):
        self.coop.switch()

    def _tick(self):
        self._tick_n += 1
        if self._tick_n % 24 == 0:
            self.coop.switch()

    def __init__(self, depth=DEPTH, seq=SEQ, dec_seq=DEC_SEQ, past=PAST, npr=NPR, debug=None, stop=99, only=None):
        self.depth, self.seq, self.dec_seq, self.past, self.npr = depth, seq, dec_seq, past, npr
        self.stop, self.only = stop, only
        nc = bass.Bass("TRN2", target_bir_lowering=False)
        self.nc = nc
        self.P = Prog(nc)
        dt = nc.dram_tensor
        L, S = dec_seq, seq
        I = lambda name, shape, d=F32: dt(name, list(shape), d, kind="ExternalInput").ap()
        O = lambda name, shape, d=F32: dt(name, list(shape), d, kind="ExternalOutput").ap()
        X = lambda name, shape, d=F32: dt(name, list(shape), d, kind="Internal").ap()
        self.xp = I("xp", [npr, S, D])
        self.xs = I("xs", [L, D])
        self.cckv = I("cckv", [depth, past, 128])
        self.ckr = I("ckr", [depth, past, 48])
        self.stg = I("stg", [depth, 2, 4, 64, 64])
        self.sts = I("sts", [depth, 2, 4, 64, 128])
        self.condT = I("condT", [128, 8, 2])
        self.wada = I("wada", [depth, 48, 128, 8 * 128])
        self.win = I("win", [depth, NG_IN, 128, 8 * 128])
        self.pk = I("pk", [depth, 128, NPK])
        self.wuq = I("wuq", [depth, 256, 8 * 128])
        self.wuqb = I("wuqb", [depth, 256, 8 * 64])
        self.wuk = I("wuk", [depth, 128, 8 * 128])
        self.wuv = I("wuv", [depth, 128, 8 * 64])
        self.wout = I("wout", [depth, D, D])
        self.wup = I("wup", [depth, 44, 128, 8 * 128])
        self.wdn = I("wdn", [depth, DFF, D])
        self.consts = I("consts", [128, NCONST * 128])
        self.rope = I("rope", [48, 2, L])
        self.yp = O("yp", [npr, S, D])
        self.ys = O("ys", [L, D])
        self.ockv = O("ockv", [npr, depth, S, 128])
        self.okr = O("okr", [npr, depth, S, 32])
        self.osg = O("osg", [npr, depth, 2, 4, 64, 64])
        self.oss = O("oss", [npr, depth, 2, 4, 64, 128])
        self._tls = threading.local()
        self.coop = Coop()
        self.xres = X("xres", [L, D])

        def scratch(tag, Ls):
            return {"pa": X("pa" + tag, [2048, Ls], BF16),
                    "qt": X("qt" + tag, [1024, Ls], BF16),
                    "ckvd": X("ckvd" + tag, [128, Ls + past], BF16),
                    "krd": X("krd" + tag, [48, Ls + past], BF16),
                    "catd": X("catd" + tag, [1024, Ls], BF16),
                    "actT": X("actT" + tag, [DFF, Ls], BF16)}
        self.scr_main = scratch("", L)
        self.scr_p = [scratch("_p%d" % i, S) for i in range(npr)]
        self.dbg = {}
        if debug:
            self.dbg = {k: O("dbg_" + k, shp) for k, shp in debug.items()}
        lo = (nc.sbuf_base + 31) // 32 * 32
        self.arenas = []
        self.main_ctx = self.make_ctx(Arena(nc, lo, nc.sbuf_top // 32 * 32), PsumPool(nc), self.scr_main)
        self.use_ctx(self.main_ctx)
        self.P.on_barrier = lambda local: ([self._tls.ctx.A.reg.clear()] if local else [a.reg.clear() for a in self.arenas])
        self._tick_n = 0
        self.P.on_op = self._tick
        self.coop.on_resume = lambda: setattr(self.P, "tok", self._tls.ctx.tokens)
        self.build()

    def make_ctx(self, arena, ps, scr):
        c = Ctx()
        c.A, c.ps = arena, ps
        self.arenas.append(arena)
        for k, v in scr.items():
            setattr(c, k, v)
        c.tokens = {}
        c.gw_key = None
        c.GWT = c.GWTMP = None
        c._oi = c._wi = c._obi = c._ny = 0
        c.dI, c.hl = None, False
        return c

    def cst(self, blk, r0=0, r1=128, c0=0, c1=128):
        return self.CON[r0:r1, blk * 128 + c0:blk * 128 + c1]

    def pkc(self, col, r0=0, r1=128):
        return self.PK[r0:r1, col:col + 1]

    def rstd_from(self, out, in_, n, rows=128):
        P = self.P
        P.act(out, in_, AF.Ln, bias=self.EPSB[0:rows, :], scale=1.0 / n)
        P.act(out, out, AF.Exp, scale=-0.5)

    def build(self):
        P, A = self.P, self.A
        self.CON = A.tile([128, NCONST * 128], F32, "con")
        P.dma(self.CON[:, :], dv(self.consts[:, :]))
        self.IDB = A.tile([128, 128], BF16, "idb")
        P.cp(self.IDB[:, :], self.cst(C_ID))
        self.ONESB = A.tile([128, 64], BF16, "onesb")
        P.memset(self.ONESB[:, :], 1.0)
        self.EPSB = A.tile([128, 1], F32, "epsb")
        P.memset(self.EPSB[:, :], EPS)
        self.ONE1 = A.tile([128, 1], F32, "one1")
        P.memset(self.ONE1[:, :], 1.0)
        self.ZERO1 = A.tile([128, 1], F32, "zero1")
        P.memset(self.ZERO1[:, :], 0.0)
        self.CONDS = A.tile([128, 8, 2], F32, "conds")
        ct = A.tile([128, 8, 2], F32, "condraw")
        P.dma(ct[:, :, :], dv(self.condT[:, :, :]))
        P.act(self.CONDS[:, :, :], ct[:, :, :], AF.Silu)
        self.PK = A.tile([128, NPK], F32, "pk")
        self.MOD = A.tile([128, 48, 2], F32, "mod")
        self.MODV = A.tile([128, 6, 8, 2], F32, "modv")
        self.GWT = A.tile([128, D], F32, "gwt")
        self.GWTMP = A.tile([128, 128], F32, "gwtmp")
        self.gw_key = None
        base_mark = A.mark()
        npr = self.npr
        span = (A.hi - base_mark) // npr // 32 * 32
        banks = self.main_ctx.ps.banks
        nb = 8 // npr
        pctx = []
        for i in range(npr):
            c = self.make_ctx(Arena(self.nc, base_mark + i * span, base_mark + (i + 1) * span),
                              PsumPool(banks=banks[i * nb:(i + 1) * nb]), self.scr_p[i])
            pctx.append(c)
        for c in pctx:
            self.use_ctx(c)
            self.GWT = c.A.tile([128, D], F32, "gwt")
            self.GWTMP = c.A.tile([128, 128], F32, "gwtmp")
            c.base = c.A.mark()
        ATTN_SPAN = 104 * 1024
        full_ps = self.main_ctx.ps
        octx = self.make_ctx(Arena(self.nc, (base_mark + ATTN_SPAN) // 32 * 32, A.hi), PsumPool(banks=banks[5:8]), self.scr_p[0])
        self.use_ctx(octx)
        self.GWT = octx.A.tile([128, D], F32, "gwt")
        self.GWTMP = octx.A.tile([128, 128], F32, "gwtmp")
        octx.base = octx.A.mark()
        self.attn_limit = (base_mark + ATTN_SPAN) // 32 * 32
        self.use_ctx(self.main_ctx)
        for l in range(self.depth):
            A.release(base_mark)
            P.dma(self.PK[:, :], dv(self.pk[l]))
            self.adaln(l)
            for c in [self.main_ctx] + pctx:
                c.gw_key = None
            last = (l == self.depth - 1)
            if self.only is not None or not OVERLAP_PROMPTS:
                if self.only != "dec":
                    def mk(i):
                        def fn():
                            self.use_ctx(pctx[i])
                            pctx[i].A.release(pctx[i].base)
                            self.layer_seq(l, i, self.seq, False, last)
                        return fn
                    self.coop.run([mk(i) for i in range(npr)])
                    self.use_ctx(self.main_ctx)
                    self.barrier()
                if self.only == "pr":
                    continue
                m = A.mark()
                self.layer_seq(l, -1, self.dec_seq, True, last)
                self.barrier()
                A.release(m)
                continue

            def wrap(attn_fn, l=l, last=last):
                def prompts():
                    self.use_ctx(octx)
                    for i in range(npr):
                        for k, v in self.scr_p[i].items():
                            setattr(octx, k, v)
                        octx.A.release(octx.base)
                        octx.gw_key = None
                        self.layer_seq(l, i, self.seq, False, last)
                self.P.barrier()
                self.main_ctx.ps = PsumPool(banks=banks[0:5])
                self.coop.run([attn_fn, prompts])
                self.use_ctx(self.main_ctx)
                self.main_ctx.ps = full_ps
                self.P.barrier()
            m = A.mark()
            self.layer_seq(l, -1, self.dec_seq, True, last, attn_wrap=wrap)
            self.barrier()
            A.release(m)
        self.barrier()

    def adaln(self, l):
        P, A = self.P, self.A
        m = A.mark()
        pt = self.ps.get()
        wts = [A.tile([128, 8, 128], F32, "wada") for _ in range(3)]
        for c in range(48):
            wt = wts[c % 3]
            P.dma(sub(wt, wt.t[:, :, :].rearrange("p k c -> p (k c)")), dv(self.wada[l, c]))
            for k in range(8):
                P.mm(pt[:, c * 2:c * 2 + 2], wt[:, k, :], self.CONDS[:, k, :], start=(k == 0), stop=(k == 7))
        P.tt(self.MOD[:, :, :], sub(pt, pt.t[:, 0:96].rearrange("p (c j) -> p c j", j=2)),
             sub(self.PK, self.PK.t[:, PK_BADA:PK_BADA + 48].unsqueeze(2).to_broadcast([128, 48, 2])), ALU.add)
        for half, (nw, nwo) in enumerate(((PK_NMP, PK_NMO), (PK_NFP, PK_NFO))):
            sh = self.MOD[:, (half * 3 + 0) * 8:(half * 3 + 1) * 8, :]
            sc = self.MOD[:, (half * 3 + 1) * 8:(half * 3 + 2) * 8, :]
            g = self.MOD[:, (half * 3 + 2) * 8:(half * 3 + 3) * 8, :]
            nwb = sub(self.PK, self.PK.t[:, nw:nw + 8].unsqueeze(2).to_broadcast([128, 8, 2]))
            nwob = sub(self.PK, self.PK.t[:, nwo:nwo + 8].unsqueeze(2).to_broadcast([128, 8, 2]))
            P.stt(self.MODV[:, half * 3 + 0, :, :], sc, 1.0, nwb, ALU.add, ALU.mult)
            P.cp(self.MODV[:, half * 3 + 1, :, :], sh)
            P.tt(self.MODV[:, half * 3 + 2, :, :], g, nwob, ALU.mult)
        self.gw_key = None
        self.barrier()
        A.release(m)

    def layer_seq(self, l, sq, L, is_dec, last, attn_wrap=None):
        P, A = self.P, self.A
        who = 0 if is_dec else 1
        TW = min(512, L)
        NT = L // TW
        NB = L // 128
        NK = L + (self.past if is_dec else 0)
        if is_dec:
            x_in = self.xs if l == 0 else self.xres
            x_mid = self.xres
            x_out = self.ys if last else self.xres
        else:
            x_in = self.xp[sq] if l == 0 else self.yp[sq]
            x_mid = self.yp[sq]
            x_out = self.yp[sq]
        hreg = A.mark()
        hT = [A.tile([128, 8, TW], BF16, "hT") for _ in range(NT)]
        OG = A.tile_at(hreg, [128, 2, L], F32, "og")
        OY = A.tile_at(hreg + A.nbytes([128, 2, L], F32), [128, 2, L], F32, "oy")
        ABT = A.tile([128, NB, 80], F32, "abt")

        if self.stop < 0:
            return
        m0 = A.mark()
        xt = [A.tile([128, D], F32, "xt") for _ in range(2)]
        for b in range(NB):
            x = xt[b % 2]
            P.dma(x[:, :], dv(x_in[b * 128:(b + 1) * 128, :]))
            self.norm_to_hT(x, hT, b, TW, 0, who)
        self.barrier()
        A.release(m0)
        if self.stop < 1:
            return
        m0 = A.mark()
        self.phase_a(l, sq, L, is_dec, hT, TW, NT, NB, NK, ABT)
        self.barrier()
        A.release(m0)
        if self.stop < 2:
            return
        m0 = A.mark()
        self.phase_scan(l, sq, L, is_dec, TW, NT, NB, ABT, OG, OY)
        self.barrier()
        A.release(m0)
        if self.stop < 3:
            return
        ctx_ = self._tls.ctx

        def do_attn():
            self.use_ctx(ctx_)
            m1 = A.mark()
            if attn_wrap is not None:
                A.release(hreg)
            self.phase_attn(l, L, is_dec, TW, NT, NK)
            assert attn_wrap is None or A.peak_since <= self.attn_limit, "attention tiles overlap the prompt region"
            self.barrier()
            A.release(m1)
        if attn_wrap is not None:
            A.peak_since = 0
            attn_wrap(do_attn)
            self.use_ctx(ctx_)
        else:
            do_attn()
        if self.stop < 4:
            return
        m0 = A.mark()
        wo = A.tile([128, 8, D], BF16, "wout")
        self.load_w_bf16(wo, self.wout[l], 8, D)
        xt = [A.tile([128, D], F32, "xt") for _ in range(2)]
        ct = [A.tile([128, 8, 128], BF16, "ct") for _ in range(2)]
        for b in range(NB):
            x = xt[b % 2]
            c = ct[b % 2]
            P.dma(x[:, :], dv(x_in[b * 128:(b + 1) * 128, :]))
            P.dma(c[:, :, :], dv(self.catd[:, b * 128:(b + 1) * 128].rearrange("(k p) t -> p k t", p=128)))
            import os
            cut2 = int(os.environ.get("K_CUT2", "99"))
            self.gw_rows(2, who)
            pss = [self.ps.get(), self.ps.get()]
            for hh in range(2):
                for k in range(8):
                    P.mm(pss[hh][:, :], c[:, k, :], wo[:, k, hh * 512:(hh + 1) * 512], start=(k == 0), stop=(k == 7))
            if cut2 < 1:
                continue
            self.residual(pss, x, 2, who, x_mid, b)
            if cut2 < 5:
                continue
            self.norm_to_hT(x, hT, b, TW, 3, who)
        self.barrier()
        A.release(m0)
        if self.stop < 5:
            return
        m0 = A.mark()
        self.phase_ffn_up(l, L, hT, TW, NT)
        self.barrier()
        A.release(m0)
        if self.stop < 6:
            return
        m0 = A.mark()
        wd = A.tile([128, 22, D], BF16, "wdn")
        self.load_w_bf16(wd, self.wdn[l], 22, D)
        xt = [A.tile([128, D], F32, "xt") for _ in range(2)]
        at = [A.tile([128, 22, 128], BF16, "at") for _ in range(2)]
        for b in range(NB):
            x = xt[b % 2]
            a = at[b % 2]
            P.dma(x[:, :], dv(x_mid[b * 128:(b + 1) * 128, :]))
            P.dma(a[:, :, :], dv(self.actT[:, b * 128:(b + 1) * 128].rearrange("(j p) t -> p j t", p=128)))
            self.gw_rows(5, who)
            pss = [self.ps.get(), self.ps.get()]
            for hh in range(2):
                for j in range(22):
                    P.mm(pss[hh][:, :], a[:, j, :], wd[:, j, hh * 512:(hh + 1) * 512], start=(j == 0), stop=(j == 21))
            self.residual(pss, x, 5, who, x_out, b)
        self.barrier()
        A.release(m0)
        A.release(hreg)

    def load_w_bf16(self, dst, src, nk, ncols, engs=("dve", "act")):
        P, A = self.P, self.A
        cw = min(ncols, 512)
        st = [A.tile([128, cw], F32, "wst") for _ in range(3)]
        i = 0
        for k in range(nk):
            for c0 in range(0, ncols, cw):
                s = st[i % 3]
                P.dma(s[:, :], dv(src[k * 128:(k + 1) * 128, c0:c0 + cw]))
                P.cp(dst[:, k, c0:c0 + cw], s[:, :], e=engs[i % len(engs)])
                i += 1

    def load_wg(self, dst, src, stage, e):
        P = self.P
        P.dma(sub(stage, stage.t[:, :, :].rearrange("p k c -> p (k c)")), dv(src))
        P.cp(dst[:, :, :], stage[:, :, :], e=e)

    def norm_to_hT(self, x, hT, b, TW, mv, who):
        P, A = self.P, self.A
        m = A.mark()
        junk = A.tile([128, D], BF16, "junk")
        ssq = A.tile([128, 1], F32, "ssq")
        rstd = A.tile([128, 1], F32, "rstd")
        xb = A.tile([128, D], BF16, "xb")
        import os
        cut = int(os.environ.get("K_CUT", "99"))
        P.act(junk[:, :], x[:, :], AF.Square, scale=float(D) ** -0.5, accum=ssq[:, :])
        if cut < 1:
            A.release(m); return
        self.rstd_from(rstd[:, :], ssq[:, :], 1.0)
        if cut < 2:
            A.release(m); return
        P.ts(xb[:, :], x[:, :], rstd[:, :], None, op0=ALU.mult)
        if cut < 3:
            A.release(m); return
        pt = self.ps.get()
        ptb = pt.t[:, :].bitcast(BF16)
        for k in range(8):
            P.tr(sub(pt, ptb[:, k * 128:(k + 1) * 128]), xb[:, k * 128:(k + 1) * 128], self.IDB[:, :])
        tt_, off = b * 128 // TW, (b * 128) % TW
        if cut < 4:
            A.release(m); return
        for k in range(8):
            src = sub(pt, ptb[:, k * 128:(k + 1) * 128])
            dst = hT[tt_][:, k, off:off + 128]
            if (k % 2 == 0 or cut == 4) and cut != 5:
                P.ts(dst, src, self.MODV[:, mv, k, who:who + 1], self.MODV[:, mv + 1, k, who:who + 1],
                     op0=ALU.mult, op1=ALU.add)
            else:
                P.act(dst, src, AF.Identity, bias=self.MODV[:, mv + 1, k, who:who + 1],
                      scale=self.MODV[:, mv, k, who:who + 1])
        A.release(m)

    def residual(self, pss, x, mv, who, x_dst, b):
        P, A = self.P, self.A
        m = A.mark()
        GW = self.gw_rows(mv, who)
        junk = A.tile([128, 512], F32, "junk")
        ss = A.tile([128, 2], F32, "ss")
        rstd = A.tile([128, 1], F32, "rstd")
        t = A.tile([128, D], F32, "t")
        import os
        cut2 = int(os.environ.get("K_CUT2", "99"))
        for hh in range(2):
            P.act(junk[:, :], pss[hh][:, :], AF.Square, scale=float(D) ** -0.5, accum=ss[:, hh:hh + 1])
        if cut2 < 2:
            A.release(m); return
        P.tt(rstd[:, :], ss[:, 0:1], ss[:, 1:2], ALU.add)
        self.rstd_from(rstd[:, :], rstd[:, :], 1.0)
        for hh in range(2):
            P.tt(t[:, hh * 512:(hh + 1) * 512], pss[hh][:, :], GW[:, hh * 512:(hh + 1) * 512], ALU.mult)
        if cut2 < 3:
            A.release(m); return
        P.stt(x[:, :], t[:, :], rstd[:, :], x[:, :], ALU.mult, ALU.add)
        if cut2 < 4:
            A.release(m); return
        P.dma(dv(x_dst[b * 128:(b + 1) * 128, :]), x[:, :], q="pool")
        A.release(m)

    def gw_rows(self, mv, who):
        if self.gw_key == (mv, who):
            return self.GWT
        P = self.P
        for k in range(8):
            P.ts(self.GWTMP[:, :], self.cst(C_ONES), self.MODV[:, mv, k, who:who + 1], None, op0=ALU.mult)
            pt = self.ps.get()
            P.mm(pt[:, 0:128], self.GWTMP[:, :], self.cst(C_ID))
            P.cp(self.GWT[:, k * 128:(k + 1) * 128], pt[:, 0:128])
        self.gw_key = (mv, who)
        return self.GWT
    def phase_a(self, l, sq, L, is_dec, hT, TW, NT, NB, NK, ABT):
        P, A = self.P, self.A
        stg = [A.tile([128, 8, 128], F32, "wstg") for _ in range(2)]
        wgs = [A.tile([128, 8, 128], BF16, "wg") for _ in range(3)]
        RAW = [A.tile([128, L + 4], BF16, "raw") for _ in range(2)]
        for r in RAW:
            P.memset(r[:, 0:2], 0.0)
            P.memset(r[:, L + 2:L + 4], 0.0)
        DG = A.tile([128, 5, 128], BF16, "dg")
        OUTS = [A.tile([128, TW], F32, "outs") for _ in range(3)]
        TMP = [A.tile([128, TW], F32, "tmpa") for _ in range(3)]
        CQRAW = A.tile([128, 2, L], F32, "cqraw")
        self._oi = 0
        self._wi = 0

        OUTB = [A.tile([128, TW], BF16, "outb") for _ in range(3)]
        self._obi = 0

        def nxt_out():
            self._oi += 1
            return OUTS[self._oi % 3]

        def nxt_outb():
            self._obi += 1
            return OUTB[self._obi % 3]

        def load_group(g, ncol=128):
            self._wi += 1
            w = wgs[self._wi % 3]
            self.load_wg(w, self.win[l, g], stg[self._wi % 2], "act" if self._wi % 2 else "dve")
            return w

        def proj(w, t, ncol=128):
            pt = self.ps.get()
            for k in range(8):
                P.mm(pt[0:ncol, 0:TW], w[:, k, 0:ncol], hT[t][:, k, :], start=(k == 0), stop=(k == 7))
            return pt

        def store_pa(row0, t, src):
            P.dma(dv(self.pa[row0:row0 + 128, t * TW:(t + 1) * TW]), src, q="pool")

        convs = []
        for i in range(6):
            kind = "qk" if i < 4 else "plain"
            convs.append((G_Q + i, PK_GCONV + i * 5, None, kind, i * 128, 0.125 if i < 2 else 1.0))
        for i in range(6):
            convs.append((G_XS + i, PK_SCONV + i * 5, PK_SCB + i, "plain", 1280 + i * 128, 1.0))
        for ci, (g, ccol, bcol, kind, row0, qs) in enumerate(convs):
            w = load_group(g)
            raw = RAW[ci % 2]
            for t in range(NT):
                pt = proj(w, t)
                if t % 2 == 0:
                    P.cp(raw[:, 2 + t * TW:2 + (t + 1) * TW], pt[:, 0:TW], e="act")
                else:
                    P.cp(raw[:, 2 + t * TW:2 + (t + 1) * TW], pt[:, 0:TW], e="dve")
            for j in range(5):
                P.ts(DG[:, j, :], self.IDB[:, :], self.pkc(ccol + j), None, op0=ALU.mult)
            for t in range(NT):
                pt = self.ps.get()
                for j in range(5):
                    P.mm(pt[:, 0:TW], DG[:, j, :], raw[:, t * TW + j:t * TW + j + TW], start=(j == 0), stop=(j == 4))
                ob_ = nxt_outb()
                if kind == "qk":
                    o = nxt_out()
                    P.act(o[:, :], pt[:, 0:TW], AF.Silu)
                    sqt = TMP[0]
                    P.tt(sqt[:, :], o[:, :], o[:, :], ALU.mult)
                    p2 = self.ps.get()
                    P.mm(p2[:, 0:TW], self.cst(C_ONESBD), sqt[:, :])
                    rs = TMP[1]
                    P.act(rs[:, :], p2[:, 0:TW], AF.Ln, bias=self.EPSB[:, :])
                    P.act(rs[:, :], rs[:, :], AF.Exp, scale=-0.5)
                    P.stt(ob_[:, :], o[:, :], qs, rs[:, :], ALU.mult, ALU.mult)
                elif bcol is None:
                    P.act(ob_[:, :], pt[:, 0:TW], AF.Silu)
                else:
                    P.act(ob_[:, :], pt[:, 0:TW], AF.Silu, bias=self.pkc(bcol))
                store_pa(row0, t, ob_[:, :])
        import os
        cut3 = int(os.environ.get("K_CUT3", "99"))
        if cut3 < 1:
            return
        for (g, row0) in ((G_GATE, 768), (G_GATE + 1, 896), (G_Z, 1024), (G_Z + 1, 1152)):
            w = load_group(g)
            for t in range(NT):
                pt = proj(w, t)
                ob_ = nxt_outb()
                P.act(ob_[:, :], pt[:, 0:TW], AF.Silu)
                store_pa(row0, t, ob_[:, :])
        if cut3 < 2:
            return
        w = load_group(G_AB)
        NEA = A.tile([16, 1], F32, "nea")
        P.act(NEA[:, :], self.pkc(PK_ALOG, 0, 16), AF.Exp)
        P.ts(NEA[:, :], NEA[:, :], -1.0, None, op0=ALU.mult)
        ABF = A.tile([128, TW], F32, "abf")
        P.memset(ABF[:, :], 0.0)
        for t in range(NT):
            pt = proj(w, t)
            e1 = TMP[0]
            P.act(e1[0:16, :], pt[0:16, 0:TW], AF.Exp, bias=self.pkc(PK_DTB, 0, 16))
            P.act(ABF[64:80, :], e1[0:16, :], AF.Ln, bias=self.ONE1[0:16, :])
            P.act(e1[0:16, :], e1[0:16, :], AF.Ln, bias=self.ONE1[0:16, :])
            P.ts(ABF[0:16, :], e1[0:16, :], NEA[:, :], None, op0=ALU.mult)
            P.act(ABF[32:40, :], pt[32:40, 0:TW], AF.Sigmoid)
            for s in range(TW // 128):
                b = t * (TW // 128) + s
                p2 = self.ps.get()
                P.tr(p2[:, 0:80], ABF[0:80, s * 128:(s + 1) * 128], self.cst(C_ID, 0, 80, 0, 80))
                P.cp(ABT[:, b, :], p2[:, 0:80])
        if cut3 < 3:
            return
        w = load_group(G_CKV)
        for t in range(NT):
            pt = proj(w, t)
            sqt = TMP[0]
            P.act(sqt[:, :], pt[:, 0:TW], AF.Square)
            p2 = self.ps.get()
            P.mm(p2[:, 0:TW], self.cst(C_ONES), sqt[:, :])
            rs = TMP[1]
            self.rstd_from(rs[:, :], p2[:, 0:TW], 128.0)
            o = nxt_out()
            P.stt(o[:, :], pt[:, 0:TW], self.pkc(PK_KVN), rs[:, :], ALU.mult, ALU.mult)
            ob = TMP[2]
            obb = sub(ob, ob.t[:, 0:TW // 2].bitcast(BF16))
            P.cp(obb, o[:, :], e="act")
            P.dma(dv(self.ckvd[:, t * TW:(t + 1) * TW]), obb, q="pool")
            if not is_dec:
                for s in range(TW // 128):
                    b = t * (TW // 128) + s
                    p3 = self.ps.get()
                    P.tr(p3[:, 0:128], o[:, s * 128:(s + 1) * 128], self.cst(C_ID))
                    o3 = nxt_out()
                    P.cp(o3[:, 0:128], p3[:, 0:128])
                    P.dma(dv(self.ockv[sq, l, b * 128:(b + 1) * 128, :]), o3[:, 0:128], q="pool")
        if is_dec:
            for s in range(self.past // 128):
                c = TMP[0]
                P.dma(c[:, 0:128], dv(self.cckv[l, s * 128:(s + 1) * 128, :]))
                p3 = self.ps.get()
                P.tr(p3[:, 0:128], c[:, 0:128], self.cst(C_ID))
                ob = TMP[2]
                obb = sub(ob, ob.t[:, 0:64].bitcast(BF16))
                P.cp(obb, p3[:, 0:128])
                P.dma(dv(self.ckvd[:, L + s * 128:L + (s + 1) * 128]), obb, q="pool")
        if cut3 < 4:
            return
        wa = load_group(G_KR, 48)
        wb = load_group(G_KRB, 48)
        ROP = [A.tile([48, 2, TW], F32, "rop") for _ in range(2)]
        for t in range(NT):
            pa_ = proj(wa, t, 48)
            ob = TMP[2]
            obb = sub(ob, ob.t[0:48, 0:TW // 2].bitcast(BF16))
            if is_dec:
                pb_ = proj(wb, t, 48)
                rp = ROP[t % 2]
                P.dma(rp[:, :, :], dv(self.rope[:, :, t * TW:(t + 1) * TW]))
                t1 = TMP[0]
                t2 = TMP[1]
                P.tt(t1[0:48, :], pa_[0:48, 0:TW], rp[:, 0, :], ALU.mult)
                P.tt(t2[0:48, :], pb_[0:48, 0:TW], rp[:, 1, :], ALU.mult)
                P.tt(obb, t1[0:48, :], t2[0:48, :], ALU.add)
            else:
                o = nxt_out()
                P.cp(o[0:48, :], pa_[0:48, 0:TW])
                P.cp(obb, o[0:48, :], e="act")
                for s in range(TW // 128):
                    b = t * (TW // 128) + s
                    p3 = self.ps.get()
                    P.tr(p3[:, 0:48], o[0:48, s * 128:(s + 1) * 128], self.cst(C_ID, 0, 48, 0, 48))
                    o3 = nxt_out()
                    P.cp(o3[:, 0:48], p3[:, 0:48])
                    P.dma(dv(self.okr[sq, l, b * 128:(b + 1) * 128, 0:16]), o3[:, 0:16], q="pool")
                    P.dma(dv(self.okr[sq, l, b * 128:(b + 1) * 128, 16:32]), o3[:, 32:48], q="pool")
            P.dma(dv(self.krd[:, t * TW:(t + 1) * TW]), obb, q="pool")
        skip = os.environ.get('K_SKIP', '').split(',')
        if is_dec and 'ctxkr' not in skip:
            for s in range(self.past // 128):
                c = TMP[0]
                P.dma(c[:, 0:48], dv(self.ckr[l, s * 128:(s + 1) * 128, :]))
                p3 = self.ps.get()
                P.tr(p3[0:64, 0:128], c[:, 0:64], self.cst(C_ID))
                ob = TMP[2]
                obb = sub(ob, ob.t[0:48, 0:64].bitcast(BF16))
                P.cp(obb, p3[0:48, 0:128])
                P.dma(dv(self.krd[:, L + s * 128:L + (s + 1) * 128]), obb, q="pool")
        if cut3 < 5:
            return
        for i in range(2):
            w = load_group(G_CQ + i)
            for t in range(NT):
                pt = proj(w, t)
                P.cp(CQRAW[:, i, t * TW:(t + 1) * TW], pt[:, 0:TW], e=("act" if t % 2 else "dve"))
        WQ = A.tile([128, 2, 1024], BF16, "wq")
        WQB = A.tile([128, 2, 512], BF16, "wqb")
        self.load_w_bf16(WQ, self.wuq[l], 2, 1024)
        self.load_w_bf16(WQB, self.wuqb[l], 2, 512)
        CQN = A.tile([128, 2, TW], BF16, "cqn")
        QO = [A.tile([128, TW], BF16, "qo") for _ in range(2)]
        for t in range(NT):
            p2 = self.ps.get()
            for i in range(2):
                sqt = TMP[i]
                P.act(sqt[:, :], CQRAW[:, i, t * TW:(t + 1) * TW], AF.Square)
                P.mm(p2[:, 0:TW], self.cst(C_ONES), sqt[:, :], start=(i == 0), stop=(i == 1))
            rs = TMP[2]
            self.rstd_from(rs[:, :], p2[:, 0:TW], 256.0)
            for i in range(2):
                P.stt(CQN[:, i, :], CQRAW[:, i, t * TW:(t + 1) * TW], self.pkc(PK_QN + i), rs[:, :], ALU.mult, ALU.mult)
            if is_dec:
                rp = ROP[t % 2]
                P.dma(rp[:, :, :], dv(self.rope[:, :, t * TW:(t + 1) * TW]))
            for h in range(8):
                pa_ = self.ps.get()
                for i in range(2):
                    P.mm(pa_[:, 0:TW], WQ[:, i, h * 128:(h + 1) * 128], CQN[:, i, :], start=(i == 0), stop=(i == 1))
                qo = QO[h % 2]
                P.cp(qo[:, :], pa_[:, 0:TW], e="dve")
                if is_dec:
                    skip = os.environ.get('K_SKIP', '').split(',')
                    if 'qpb' in skip:
                        pb_ = pa_
                    else:
                        pb_ = self.ps.get()
                        for i in range(2):
                            P.mm(pb_[0:48, 0:TW], WQB[:, i, h * 64:h * 64 + 48], CQN[:, i, :], start=(i == 0), stop=(i == 1))
                    t1 = TMP[0]
                    t2 = TMP[1]
                    if 'qtt' not in skip:
                        P.tt(t1[0:48, :], pa_[0:48, 0:TW], rp[:, 0, :], ALU.mult)
                        P.tt(t2[0:48, :], pb_[0:48, 0:TW], rp[:, 1, :], ALU.mult)
                        P.tt(qo[0:48, :], t1[0:48, :], t2[0:48, :], ALU.add)
                P.dma(dv(self.qt[h * 128:(h + 1) * 128, t * TW:(t + 1) * TW]), qo[:, :], q="pool")

    def phase_ffn_up(self, l, L, hT, TW, NT):
        P, A = self.P, self.A
        stg = [A.tile([128, 8, 128], F32, "wstg") for _ in range(2)]
        WG = [A.tile([128, 8, 128], BF16, "wg") for _ in range(2)]
        WU = [A.tile([128, 8, 128], BF16, "wu") for _ in range(2)]
        GB = [A.tile([128, L + 2], BF16, "gb") for _ in range(2)]
        UB = [A.tile([128, L + 2], BF16, "ub") for _ in range(2)]
        for r in GB + UB:
            P.memset(r[:, 0:1], 0.0)
            P.memset(r[:, L + 1:L + 2], 0.0)
        DGG = A.tile([128, 3, 128], BF16, "dgg")
        DGU = A.tile([128, 3, 128], BF16, "dgu")
        SGT = [A.tile([128, TW], F32, "sgt") for _ in range(2)]
        AO = [A.tile([128, TW], BF16, "ao") for _ in range(3)]
        n = 0
        for cg in range(22):
            wg, wu, gb, ub = WG[cg % 2], WU[cg % 2], GB[cg % 2], UB[cg % 2]
            self.load_wg(wg, self.wup[l, cg], stg[0], "dve")
            self.load_wg(wu, self.wup[l, 22 + cg], stg[1], "act")
            for t in range(NT):
                pg = self.ps.get()
                pu = self.ps.get()
                for k in range(8):
                    P.mm(pg[:, 0:TW], wg[:, k, :], hT[t][:, k, :], start=(k == 0), stop=(k == 7))
                for k in range(8):
                    P.mm(pu[:, 0:TW], wu[:, k, :], hT[t][:, k, :], start=(k == 0), stop=(k == 7))
                P.cp(gb[:, 1 + t * TW:1 + (t + 1) * TW], pg[:, 0:TW], e="act")
                P.cp(ub[:, 1 + t * TW:1 + (t + 1) * TW], pu[:, 0:TW], e="dve")
            for j in range(3):
                P.ts(DGG[:, j, :], self.IDB[:, :], self.pkc(PK_FCONV + cg * 3 + j), None, op0=ALU.mult)
                P.ts(DGU[:, j, :], self.IDB[:, :], self.pkc(PK_FCONV + (22 + cg) * 3 + j), None, op0=ALU.mult)
            for t in range(NT):
                pg = self.ps.get()
                pu = self.ps.get()
                for j in range(3):
                    P.mm(pg[:, 0:TW], DGG[:, j, :], gb[:, t * TW + j:t * TW + j + TW], start=(j == 0), stop=(j == 2))
                for j in range(3):
                    P.mm(pu[:, 0:TW], DGU[:, j, :], ub[:, t * TW + j:t * TW + j + TW], start=(j == 0), stop=(j == 2))
                sg = SGT[n % 2]
                ao = AO[n % 3]
                n += 1
                P.act(sg[:, :], pg[:, 0:TW], AF.Silu)
                P.tt(ao[:, :], pu[:, 0:TW], sg[:, :], ALU.mult)
                P.dma(dv(self.actT[cg * 128:(cg + 1) * 128, t * TW:(t + 1) * TW]), ao[:, :], q="pool")

    def phase_attn(self, l, L, is_dec, TW, NT, NK):
        P, A = self.P, self.A
        NKT = NK // 128
        scale = 96.0 ** -0.5
        CKVB = A.tile([128, NK], BF16, "ckvb")
        KRB = A.tile([48, NK], BF16, "krb")
        P.dma(CKVB[:, :], dv(self.ckvd[:, 0:NK]))
        P.dma(KRB[:, :], dv(self.krd[:, 0:NK]))
        WUK = A.tile([128, 1, 1024], BF16, "wuk")
        WUV = A.tile([128, 1, 512], BF16, "wuv")
        self.load_w_bf16(WUK, self.wuk[l], 1, 1024)
        self.load_w_bf16(WUV, self.wuv[l], 1, 512)
        KT = [A.tile([128, NK], BF16, "kt") for _ in range(2)]
        VA = [A.tile([128, NKT, 128], BF16, "va") for _ in range(2)]
        for va in VA:
            P.cp(va[:, :, 64:128], sub(self.ONESB, self.ONESB.t[:, :].unsqueeze(1).to_broadcast([128, NKT, 64])), e="pool")
        QT = [A.tile([128, L], BF16, "qth") for _ in range(2)]
        PT = [A.tile([128, TW], BF16, "pt") for _ in range(8)]
        REC = [A.tile([64, TW], F32, "rec") for _ in range(2)]
        OT = [A.tile([64, TW], BF16, "ot") for _ in range(2)]
        npt = 0
        pend = []
        LAG = 4

        def drain(n):
            while len(pend) > n:
                pend.pop(0)()

        for h in range(8):
            kt, va, qt = KT[h % 2], VA[h % 2], QT[h % 2]
            while pend and pend[0].head <= h - 2:
                pend.pop(0)()
            P.dma(qt[:, :], dv(self.qt[h * 128:(h + 1) * 128, 0:L]))
            for c0 in range(0, NK, 512):
                cw = min(512, NK - c0)
                pt = self.ps.get()
                P.mm(pt[:, 0:cw], WUK[:, 0, h * 128:(h + 1) * 128], CKVB[:, c0:c0 + cw], start=True, stop=False)
                P.mm(pt[:, 0:cw], self.IDB[0:48, :], KRB[:, c0:c0 + cw], start=False, stop=True)
                P.cp(kt[:, c0:c0 + cw], pt[:, 0:cw], e="dve")
            for k0 in range(0, NKT, 8):
                kn = min(8, NKT - k0)
                pt = self.ps.get()
                for kk in range(kn):
                    P.mm(pt[:, kk * 64:(kk + 1) * 64], CKVB[:, (k0 + kk) * 128:(k0 + kk + 1) * 128],
                         WUV[:, 0, h * 64:(h + 1) * 64])
                P.cp(va[:, k0:k0 + kn, 0:64], sub(pt, pt.t[:, 0:kn * 64].rearrange("p (k v) -> p k v", v=64)), e="dve")
            for t in range(NT):
                po = self.ps.reserve()
                for kti in range(NKT):
                    ps_ = self.ps.get()
                    P.mm(ps_[:, 0:TW], kt[:, kti * 128:(kti + 1) * 128], qt[:, t * TW:(t + 1) * TW])
                    pt_ = PT[npt % len(PT)]
                    npt += 1
                    P.act(pt_[:, :], ps_[:, 0:TW], AF.Exp, scale=scale)

                    def pv(po=po, kti=kti, pt_=pt_, va=va, t=t, h=h):
                        P.mm(po[:, 0:TW], va[:, kti, :], pt_[:, :], start=(kti == 0), stop=(kti == NKT - 1))
                        if kti == NKT - 1:
                            rec = REC[(h * NT + t) % 2]
                            P.recip(rec[:, :], po[64:128, 0:TW])
                            ot = OT[(h * NT + t) % 2]
                            P.tt(ot[:, :], po[0:64, 0:TW], rec[:, :], ALU.mult)
                            self.ps.free(po)
                            P.dma(dv(self.catd[256 + h * 64:256 + (h + 1) * 64, t * TW:(t + 1) * TW]), ot[:, :], q="pool")

                    pv.head = h
                    pend.append(pv)
                    drain(LAG)
        drain(0)

    def scan_tiles(self):
        import os
        A = self.A
        f = lambda shape, name: A.tile(shape, F32, name)
        h = lambda shape, name: A.tile(shape, BF16, name)
        mode = os.environ.get("K_DI", "r32")
        self.dI = {"f32": F32, "r32": F32R}.get(mode, BF16)
        self.hl = (mode == "hl")
        i_ = lambda shape, name: A.tile(shape, self.dI, name)
        T = {}
        T["FMS"] = [{k: [h([128, 128], "fm" + k) for _ in range(2)] for k in ("q", "k", "v", "x", "b", "c")} for _ in range(2)]
        T["TM"] = h([128, 4, 256], "tm")
        T["GC"], T["EG"], T["EGL"] = f([128, 8], "gc"), f([128, 8], "eg"), f([128, 8], "egl")
        T["NB"], T["BEG"] = f([128, 4], "nbeta"), f([128, 4], "beg")
        T["LAB"] = [f([128, 128], "lab") for _ in range(8)]
        T["EGRW"] = [f([128, 4, 128], "egrw") for _ in range(2)]
        T["DECT"] = [f([128, 4, 128], "dect") for _ in range(2)]
        T["DECS"] = f([128, 4, 128], "decs")
        T["MP"] = [i_([128, 4, 3 if self.hl else 2, 128], "mp") for _ in range(2)]
        T["AT"] = [i_([128, 4, 128], "at") for _ in range(2)]
        T["R"] = f([128, 4, 128], "r") if (self.hl or self.dI == F32R) else i_([128, 4, 128], "r")
        if self.hl:
            T["PF"] = f([128, 4, 128], "pf")
            T["PHL"] = h([128, 4, 2, 128], "phl")
        kv_ = f if (self.hl or self.dI == F32R) else i_
        T["KBG"], T["VB"] = kv_([128, 4, 64], "kbg"), kv_([128, 4, 64], "vb")
        T["WT"] = h([64, 4, 128], "wt")
        T["U"] = f([64, 2, 4, 64], "u")
        T["ATT"] = h([64, 2, 4, 64], "att")
        T["QDT"] = h([64, 4, 128], "qdt")
        T["KDEC"] = h([64, 2, 4, 64], "kdec")
        T["SCT"] = h([64, 2, 4, 64], "sct")
        T["XDT"] = h([64, 2, 4, 64], "xdt")
        T["BDEC"] = h([64, 2, 4, 128], "bdec")
        T["CDT"] = h([128, 4, 128], "cdt")
        T["VN"] = h([64, 4, 64], "vn")
        T["SG"] = [f([64, 4, 64], "sg") for _ in range(2)]
        T["SGB"] = [h([64, 4, 64], "sgb") for _ in range(2)]
        T["SS"] = [f([128, 4, 64], "ss") for _ in range(2)]
        T["SSB"] = [h([128, 4, 64], "ssb") for _ in range(2)]
        T["STMP"] = f([64, 4, 128], "stmp")
        return T

    def scan_dir(self, d, T, l, sq, L, is_dec, NB, ABT, OGt, OYt, obufs, touched):
        P = self.P
        ID = self.cst(C_ID)
        IDB = self.IDB
        rows = {"q": 0, "k": 256, "v": 512, "x": 1280, "b": 1536, "c": 1792}
        TRI = self.cst(C_TRIF + d)
        TRIS = self.cst(C_TRISF + d)
        b4 = lambda blk: sub(self.CON, self.CON.t[:, blk * 128:(blk + 1) * 128].unsqueeze(1).to_broadcast([128, 4, 128]))
        TRI4, TRIS4 = b4(C_TRIF + d), b4(C_TRISF + d)
        IDB4 = sub(IDB, IDB.t[:, :].unsqueeze(1).to_broadcast([128, 4, 128]))
        ID4 = b4(C_ID)
        FMS, TM = T["FMS"], T["TM"]
        GC, EG, EGL, NB_, BEG, LAB = T["GC"], T["EG"], T["EGL"], T["NB"], T["BEG"], T["LAB"]
        EGRW, DECT, DECS, MP, AT, R = T["EGRW"], T["DECT"], T["DECS"], T["MP"], T["AT"], T["R"]
        KBG, VB, WT, U, ATT, QDT, KDEC = T["KBG"], T["VB"], T["WT"], T["U"], T["ATT"], T["QDT"], T["KDEC"]
        SCT, XDT, BDEC, CDT, VN = T["SCT"], T["XDT"], T["BDEC"], T["CDT"], T["VN"]
        SG, SGB, SS, SSB, STMP = T["SG"], T["SGB"], T["SS"], T["SSB"], T["STMP"]
        h4 = lambda pt, n=128: sub(pt, pt.t[:, 0:4 * n].rearrange("p (h i) -> p h i", h=4))
        f32v = (lambda v: V(v.ap.bitcast(F32), v.buf)) if self.dI == F32R else (lambda v: v)
        sgi, ssi = 0, 0
        import os
        scut = int(os.environ.get("K_SCUT", "100000"))
        MASKE = os.environ.get('K_MASKE', 'dve')
        self._ny = 0
        if is_dec:
            P.dma(SG[0][:, :, :], dv(self.stg[l, d].rearrange("h k v -> k h v")))
            P.dma(STMP[:, :, :], dv(self.sts[l, d].rearrange("h p n -> p h n")))
            pt = self.ps.get()
            for h in range(4):
                P.tr(pt[:, h * 64:(h + 1) * 64], STMP[:, h, :], self.cst(C_ID, 0, 64, 0, 64))
            P.cp(SS[0][:, :, :], h4(pt, 64))
        else:
            P.memset(SG[0][:, :, :], 0.0)
            P.memset(SS[0][:, :, :], 0.0)
        P.cp(SGB[0][:, :, :], SG[0][:, :, :], e="pool")
        P.cp(SSB[0][:, :, :], SS[0][:, :, :], e="pool")
        self._ny += 1
        if self._ny >= scut:
            return
        yield
        blocks = list(range(NB)) if d == 0 else list(range(NB - 1, -1, -1))

        def load_fm(bi):
            fm = FMS[bi % 2]
            tk = slice(blocks[bi] * 128, (blocks[bi] + 1) * 128)
            for k in fm:
                for g in range(2):
                    P.dma(fm[k][g][:, :], dv(self.pa[rows[k] + g * 128:rows[k] + (g + 1) * 128, tk]))

        load_fm(0)
        for bi, b in enumerate(blocks):
            tok = slice(b * 128, (b + 1) * 128)
            FM = FMS[bi % 2]
            if bi + 1 < len(blocks):
                load_fm(bi + 1)
            skip = os.environ.get('K_SKIP', '')
            for half, keys in enumerate((("k", "v"), ("x", "b"))):
                if 'tm' in skip:
                    break
                pt = self.ps.get()
                ptb = pt.t[:, :].bitcast(BF16)
                for i, k in enumerate(keys):
                    for g in range(2):
                        c0 = i * 256 + g * 128
                        P.tr(sub(pt, ptb[:, c0:c0 + 128]), FM[k][g][:, :], IDB[:, :])
                P.ts(sub(TM, TM.t[:, half * 2:half * 2 + 2, :].rearrange("p a c -> p (a c)")), sub(pt, ptb[:, 0:512]),
                     1.0, None, op0=ALU.mult)
            KT4 = sub(TM, TM.t[:, 0, :].rearrange("p (h v) -> p h v", h=4))
            VT4 = sub(TM, TM.t[:, 1, :].rearrange("p (h v) -> p h v", h=4))
            XT4 = sub(TM, TM.t[:, 2, :].rearrange("p (h v) -> p h v", h=4))
            BT = sub(TM, TM.t[:, 3, :])
            lasel = sub(ABT, ABT.t[:, b, 0:16].rearrange("p (t d h) -> p t d h", t=2, d=2)[:, :, d, :])
            pt = self.ps.get()
            P.mm(sub(pt, pt.t[:, 0:8].rearrange("p (t h) -> p t h", t=2)), TRI, lasel)
            P.mm(sub(pt, pt.t[:, 8:16].rearrange("p (t h) -> p t h", t=2)), TRIS, lasel)
            P.cp(GC[:, :], pt[:, 0:8])
            P.act(EG[:, :], pt[:, 0:8], AF.Exp)
            P.act(EGL[:, :], pt[:, 8:16], AF.Exp)
            beta = ABT[:, b, 32 + d * 4:36 + d * 4]
            dtc = ABT[:, b, 72 + d * 4:76 + d * 4]
            P.ts(NB_[:, :], beta, -1.0, None, op0=ALU.mult)
            P.tt(BEG[:, :], beta, EG[:, 0:4], ALU.mult)
            for ty in range(2):
                if 'lab' in skip:
                    break
                for h in range(4):
                    P.ts(LAB[ty * 4 + h][:, :], self.cst(C_ONES), ABT[:, b, ty * 8 + d * 4 + h:ty * 8 + d * 4 + h + 1],
                         None, op0=ALU.mult)
            self._ny += 1
            if self._ny >= scut:
                return
            yield
            pg = [self.ps.get(), self.ps.get()]
            for ty in range(2):
                for h in range(4):
                    P.mm(pg[ty][:, h * 128:(h + 1) * 128], LAB[ty * 4 + h][:, :], TRI)
            for h in range(4):
                P.ts(DECS[:, h, :], pg[0][:, h * 128:(h + 1) * 128], GC[:, h:h + 1], self.ZERO1[:, :], op0=ALU.subtract, op1=ALU.max)
            P.act(DECS[:, :, :], DECS[:, :, :], AF.Exp, scale=-1.0)
            P.tt(DECS[:, :, :], DECS[:, :, :], TRIS4, ALU.mult, e=MASKE)
            for ty in range(2):
                for h in range(4):
                    P.ts(DECT[ty][:, h, :], pg[ty][:, h * 128:(h + 1) * 128], GC[:, ty * 4 + h:ty * 4 + h + 1], self.ZERO1[:, :],
                         op0=ALU.subtract, op1=ALU.min)
                P.act(EGRW[ty][:, :, :], h4(pg[ty]), AF.Exp, after=[DECT[ty][:, :, :]] + ([DECS[:, :, :]] if ty == 0 else []))
                P.act(DECT[ty][:, :, :], DECT[ty][:, :, :], AF.Exp)
                P.tt(DECT[ty][:, :, :], DECT[ty][:, :, :], TRI4, ALU.mult, e=MASKE)
            self._ny += 1
            if self._ny >= scut:
                return
            yield
            mp, at = MP[0], AT[0]
            pkk = self.ps.get()
            for h in range(4):
                kf = FM["k"][h // 2][(h % 2) * 64:(h % 2) * 64 + 64, :]
                P.mm(pkk[:, h * 128:(h + 1) * 128], kf, kf)
            for h in range(4):
                P.stt(mp[:, h, 0, :], pkk[:, h * 128:(h + 1) * 128], NB_[:, h:h + 1], DECS[:, h, :], ALU.mult, ALU.mult)
            if self.dI == F32R:
                P.cp(mp[:, :, 1, :], ID4)
            else:
                P.cp(mp[:, :, 1, :], IDB4, e="pool")
            if self.hl:
                P.memset(mp[:, :, 2, :], 0.0)
                P.cp(T["PF"][:, :, :], ID4, e="pool")
            pt = self.ps.get()
            if self.dI == BF16:
                ptb = pt.t[:, :].bitcast(BF16)
                for h in range(4):
                    P.tr(sub(pt, ptb[:, h * 128:(h + 1) * 128]), mp[:, h, 0, :], IDB[:, :])
                P.cp(at[:, :, :], sub(pt, ptb[:, 0:512].rearrange("p (h i) -> p h i", h=4)), e="act")
            else:
                for h in range(4):
                    P.tr(pt[:, h * 128:(h + 1) * 128], f32v(mp[:, h, 0, :]), ID)
                P.cp(at[:, :, :], h4(pt), e="act")
            pqk = self.ps.get()
            for h in range(4):
                r0 = (h % 2) * 64
                P.mm(pqk[:, h * 128:(h + 1) * 128], FM["k"][h // 2][r0:r0 + 64, :], FM["q"][h // 2][r0:r0 + 64, :])
            pbc = self.ps.get()
            for gr in range(2):
                P.mm(pbc[:, gr * 128:(gr + 1) * 128], FM["b"][gr][:, :], FM["c"][gr][:, :])
            for c in range(2):
                cs = slice(c * 64, c * 64 + 64)
                P.tt(ATT[:, c, :, :], sub(pqk, pqk.t[cs, :].rearrange("p (h i) -> p h i", h=4)[:, :, cs]),
                     DECT[0][cs, :, cs], ALU.mult)
                for gr in range(2):
                    P.tt(SCT[:, c, gr * 2:gr * 2 + 2, :],
                         sub(pbc, pbc.t[cs, gr * 128 + c * 64:gr * 128 + c * 64 + 64].unsqueeze(1).to_broadcast([64, 2, 64])),
                         DECT[1][cs, gr * 2:gr * 2 + 2, cs], ALU.mult)
            self._ny += 1
            if self._ny >= scut:
                return
            yield
            cur = 0
            for lev in range(6):
                mp, at = MP[cur], AT[cur]
                mpn, atn = MP[1 - cur], AT[1 - cur]
                lastlev = (lev == 5)
                pM = [self.ps.get(), self.ps.get()]
                for h in range(4):
                    reg0 = (h % 2) * 256
                    if self.hl:
                        P.mm(pM[h // 2][:, reg0:reg0 + 256], at[:, h, :],
                             sub(mp, mp.t[:, h, 0:2, :].rearrange("p a i -> p (a i)")), start=True, stop=False)
                        P.mm(pM[h // 2][:, reg0 + 128:reg0 + 256], at[:, h, :], mp[:, h, 2, :], start=False, stop=True)
                    else:
                        P.mm(pM[h // 2][:, reg0:reg0 + 256], at[:, h, :],
                             sub(mp, mp.t[:, h, :, :].rearrange("p a i -> p (a i)")))
                if not lastlev:
                    pA = self.ps.get()
                    for h in range(4):
                        P.mm(pA[:, h * 128:(h + 1) * 128], mp[:, h, 0, :], at[:, h, :])
                    P.cp(atn[:, :, :], h4(pA), e="act")
                for hp in range(2):
                    src = pM[hp].t[:, :].rearrange("p (h a i) -> p h a i", h=2, a=2)
                    if not lastlev:
                        P.cp(mpn[:, hp * 2:hp * 2 + 2, 0, :], sub(pM[hp], src[:, :, 0, :]), e="act")
                    if self.hl:
                        PF = T["PF"]
                        P.tt(PF[:, hp * 2:hp * 2 + 2, :], sub(pM[hp], src[:, :, 1, :]), PF[:, hp * 2:hp * 2 + 2, :], ALU.add)
                    else:
                        P.tt(mpn[:, hp * 2:hp * 2 + 2, 1, :], sub(pM[hp], src[:, :, 1, :]), f32v(mp[:, hp * 2:hp * 2 + 2, 1, :]), ALU.add)
                if self.hl and not lastlev:
                    hi = mpn[:, :, 1, :]
                    lo = mpn[:, :, 2, :]
                    P.cp(hi, T["PF"][:, :, :], e="pool")
                    P.tt(lo, T["PF"][:, :, :], hi, ALU.subtract)
                cur = 1 - cur
                if lev == 0:
                    P.tt(KBG[:, :, :], KT4, sub(BEG, BEG.t[:, :].unsqueeze(2).to_broadcast([128, 4, 64])), ALU.mult)
                    P.tt(VB[:, :, :], VT4, sub(ABT, beta.ap.unsqueeze(2).to_broadcast([128, 4, 64])), ALU.mult)
                    for h in range(4):
                        r0 = (h % 2) * 64
                        P.tt(QDT[:, h, :], FM["q"][h // 2][r0:r0 + 64, :], EGRW[0][r0:r0 + 64, h, :], ALU.mult,
                             e=("pool" if h % 2 else "dve"))
                if lev == 1:
                    for c in range(2):
                        cs = slice(c * 64, c * 64 + 64)
                        P.tt(KDEC[:, c, :, :], sub(TM, KT4.ap[cs, :, :]),
                             sub(EGL, EGL.t[cs, 0:4].unsqueeze(2).to_broadcast([64, 4, 64])), ALU.mult)
                        P.tt(XDT[:, c, :, :], sub(TM, XT4.ap[cs, :, :]),
                             sub(ABT, dtc.ap[cs, :].unsqueeze(2).to_broadcast([64, 4, 64])), ALU.mult)
                if lev == 2:
                    for c in range(2):
                        cs = slice(c * 64, c * 64 + 64)
                        for h in range(4):
                            gr = h // 2
                            P.ts(BDEC[:, c, h, :], sub(TM, BT.ap[cs, gr * 128:(gr + 1) * 128]), EGL[cs, 4 + h:5 + h], None,
                                 op0=ALU.mult, e=("pool" if h % 2 else "dve"))
                if lev == 3:
                    for h in range(4):
                        P.tt(CDT[:, h, :], FM["c"][h // 2][:, :], EGRW[1][:, h, :], ALU.mult, e=("pool" if h % 2 else "dve"))
                self._ny += 1
                if self._ny >= scut:
                    return
                yield
            mp = MP[cur]
            pt = self.ps.get()
            if self.hl:
                for h in range(4):
                    P.tr(pt[:, h * 128:(h + 1) * 128], T["PF"][:, h, :], ID)
                P.cp(R[:, :, :], h4(pt), e="act")
            elif self.dI == BF16:
                ptb = pt.t[:, :].bitcast(BF16)
                for h in range(4):
                    P.tr(sub(pt, ptb[:, h * 128:(h + 1) * 128]), mp[:, h, 1, :], IDB[:, :])
                P.cp(R[:, :, :], sub(pt, ptb[:, 0:512].rearrange("p (h i) -> p h i", h=4)), e="act")
            else:
                for h in range(4):
                    P.tr(pt[:, h * 128:(h + 1) * 128], f32v(mp[:, h, 1, :]), ID)
                P.cp(R[:, :, :], h4(pt), e="act")
            self._ny += 1
            if self._ny >= scut:
                return
            yield
            pt = self.ps.get()
            for h in range(4):
                if False:
                    pass
                else:
                    P.mm(pt[0:64, h * 128:(h + 1) * 128], KBG[:, h, :], R[:, h, :])
            P.cp(WT[:, :, :], sub(pt, pt.t[0:64, :].rearrange("p (h i) -> p h i", h=4)), e="act")
            pt = self.ps.get()
            for c in range(2):
                cs = slice(c * 64, c * 64 + 64)
                for h in range(4):
                    oreg = pt[0:64, (c * 4 + h) * 64:(c * 4 + h + 1) * 64]
                    if False:
                        pass
                    else:
                        P.mm(oreg, R[cs, h, cs], VB[cs, h, :])
            P.cp(U[:, :, :, :], sub(pt, pt.t[0:64, :].rearrange("p (c h v) -> p c h v", c=2, h=4)))
            self._ny += 1
            if self._ny >= scut:
                return
            yield
            for c in ((0, 1) if d == 0 else (1, 0)):
                cs = slice(c * 64, c * 64 + 64)
                il = c * 64 + (63 if d == 0 else 0)
                ctok = slice(b * 128 + c * 64, b * 128 + c * 64 + 64)
                ck = b * 2 + c
                first = ck not in touched
                touched.add(ck)
                ogv = V(OGt.t[:, :, ctok], obufs[0][ck])
                oyv = V(OYt.t[:, :, ctok], obufs[1][ck])
                S, Sn, Sb, Sbn = SS[ssi], SS[1 - ssi], SSB[ssi], SSB[1 - ssi]
                ssi = 1 - ssi
                po = self.ps.get()
                for h in range(4):
                    r0 = (h % 2) * 64
                    oreg = po[r0:r0 + 64, (h // 2) * 64:(h // 2) * 64 + 64]
                    P.mm(oreg, Sb[:, h, :], CDT[:, h, cs], start=True, stop=False)
                    P.mm(oreg, XDT[:, c, h, :], SCT[:, c, h, :], start=False, stop=True)
                pk_ = self.ps.get()
                for h in range(4):
                    P.mm(pk_[:, h * 64:(h + 1) * 64], BDEC[:, c, h, :], XDT[:, c, h, :])
                for h in range(4):
                    P.stt(Sn[:, h, :], S[:, h, :], EGRW[1][:, h, il:il + 1], pk_[:, h * 64:(h + 1) * 64], ALU.mult, ALU.add)
                P.cp(Sbn[:, :, :], Sn[:, :, :], e="pool")
                posrc = sub(po, po.t[:, 0:128].rearrange("p (g i) -> p g i", g=2))
                if first:
                    P.cp(oyv, posrc, e="act")
                else:
                    P.tt(oyv, posrc, oyv, ALU.add)
                S, Sn, Sb, Sbn = SG[sgi], SG[1 - sgi], SGB[sgi], SGB[1 - sgi]
                sgi = 1 - sgi
                pw = self.ps.get()
                for h in range(4):
                    P.mm(pw[0:64, h * 64:(h + 1) * 64], WT[:, h, cs], Sb[:, h, :])
                P.tt(VN[:, :, :], U[:, c, :, :], sub(pw, pw.t[0:64, 0:256].rearrange("p (h v) -> p h v", h=4)), ALU.subtract)
                self._ny += 1
                if self._ny >= scut:
                    return
                yield
                pk_ = self.ps.get()
                for h in range(4):
                    P.mm(pk_[0:64, h * 64:(h + 1) * 64], KDEC[:, c, h, :], VN[:, h, :])
                po = self.ps.get()
                for h in range(4):
                    r0 = (h % 2) * 64
                    oreg = po[r0:r0 + 64, (h // 2) * 64:(h // 2) * 64 + 64]
                    P.mm(oreg, Sb[:, h, :], QDT[:, h, cs], start=True, stop=False)
                    P.mm(oreg, VN[:, h, :], ATT[:, c, h, :], start=False, stop=True)
                for h in range(4):
                    P.stt(Sn[:, h, :], S[:, h, :], EGRW[0][0:64, h, il:il + 1], pk_[0:64, h * 64:(h + 1) * 64], ALU.mult, ALU.add)
                P.cp(Sbn[:, :, :], Sn[:, :, :], e="pool")
                posrc = sub(po, po.t[:, 0:128].rearrange("p (g i) -> p g i", g=2))
                if first:
                    P.cp(ogv, posrc, e="act")
                else:
                    P.tt(ogv, posrc, ogv, ALU.add)
                self._ny += 1
                if self._ny >= scut:
                    return
                yield
        if not is_dec:
            P.dma(dv(self.osg[sq, l, d].rearrange("h k v -> k h v")), SG[sgi][:, :, :], q="pool")
            pt = self.ps.get()
            for h in range(4):
                P.tr(pt[0:64, h * 128:(h + 1) * 128], SS[ssi][:, h, :], ID)
            P.cp(STMP[:, :, :], sub(pt, pt.t[0:64, :].rearrange("p (h n) -> p h n", h=4)))
            P.dma(dv(self.oss[sq, l, d].rearrange("h p n -> p h n")), STMP[:, :, :], q="pool")

    def phase_scan(self, l, sq, L, is_dec, TW, NT, NB, ABT, OG, OY):
        P, A = self.P, self.A
        m_sc = A.mark()
        obufs = [[Buf() for _ in range(2 * NB)] for _ in range(2)]
        touched = set()
        if is_dec:
            tsets = [self.scan_tiles(), self.scan_tiles()]
            groups = [[0, 1]]
        else:
            ts1 = self.scan_tiles()
            tsets = [ts1, ts1]
            groups = [[0], [1]]
        for grp in groups:
            gens = [self.scan_dir(d, tsets[d], l, sq, L, is_dec, NB, ABT, OG, OY, obufs, touched) for d in grp]
            while gens:
                for g in list(gens):
                    try:
                        next(g)
                    except StopIteration:
                        gens.remove(g)
        self.barrier()
        A.release(m_sc)
        T_ = lambda shape, name: A.tile(shape, F32, name)
        GT = [A.tile([128, 2, TW], BF16, "gt") for _ in range(3)]
        TA = T_([128, 2, TW], "ta")
        TB = T_([128, 2, TW], "tb")
        RS = T_([128, TW], "rs")
        OB = [A.tile([128, 2, TW], BF16, "ob") for _ in range(2)]
        import os
        for t in range(NT):
            if 'epi' in os.environ.get('K_SKIP', ''):
                break
            ts_ = slice(t * TW, (t + 1) * TW)
            g = GT[0]
            P.dma(g[:, :, :], dv(self.pa[768:1024, ts_].rearrange("(g p) t -> p g t", p=128)))
            P.tt(TA[:, :, :], OG[:, :, ts_], OG[:, :, ts_], ALU.mult)
            ob = OB[0]
            for gi in range(2):
                pt = self.ps.get()
                P.mm(pt[:, 0:TW], self.cst(C_ONESBD), TA[:, gi, :])
                self.rstd_from(RS[:, :], pt[:, 0:TW], 64.0)
                P.stt(TB[:, gi, :], OG[:, gi, ts_], self.pkc(PK_GDNN), RS[:, :], ALU.mult, ALU.mult)
            P.tt(ob[:, :, :], TB[:, :, :], g[:, :, :], ALU.mult)
            P.dma(dv(self.catd[0:256, ts_].rearrange("(g p) t -> p g t", p=128)), ob[:, :, :], q="pool")
            z = GT[1]
            P.dma(z[:, :, :], dv(self.pa[1024:1280, ts_].rearrange("(g p) t -> p g t", p=128)))
            xs = GT[2]
            P.dma(xs[:, :, :], dv(self.pa[1280:1536, ts_].rearrange("(g p) t -> p g t", p=128)))
            for gi in range(2):
                P.stt(TA[:, gi, :], xs[:, gi, :], self.pkc(PK_SSDD + gi), OY[:, gi, ts_], ALU.mult, ALU.add)
            P.tt(TA[:, :, :], TA[:, :, :], z[:, :, :], ALU.mult)
            P.tt(TB[:, :, :], TA[:, :, :], TA[:, :, :], ALU.mult)
            pt = self.ps.get()
            for gi in range(2):
                P.mm(pt[:, 0:TW], self.cst(C_ONES), TB[:, gi, :], start=(gi == 0), stop=(gi == 1))
            self.rstd_from(RS[:, :], pt[:, 0:TW], 256.0)
            ob = OB[1]
            for gi in range(2):
                P.stt(ob[:, gi, :], TA[:, gi, :], self.pkc(PK_SSDN + gi), RS[:, :], ALU.mult, ALU.mult)
            P.dma(dv(self.catd[768:1024, ts_].rearrange("(g p) t -> p g t", p=128)), ob[:, :, :], q="pool")


def _consts():
    C = np.zeros((128, NCONST * 128), np.float32)
    idx = np.arange(128)
    sc = (idx[:, None] // 64) == (idx[None, :] // 64)
    t = idx[:, None]
    i = idx[None, :]

    def put(blk, m):
        C[:, blk * 128:(blk + 1) * 128] = m.astype(np.float32)

    put(C_ID, np.eye(128))
    for d in range(2):
        before = (t < i) if d == 0 else (t > i)
        after = (t > i) if d == 0 else (t < i)
        tri = sc & (before | (t == i))
        put(C_TRIF + d, tri)
        put(C_TRISF + d, sc & after)
        put(C_NEGTF + d, np.where(tri, 0.0, -30000.0))
        put(C_POSSF + d, np.where(sc & after, 0.0, 30000.0))
    put(C_ONESBD, sc)
    put(C_ONES, np.ones((128, 128)))
    return C


def _rope_table(L):
    tt = np.arange(L)
    r = (tt // GRID_W).astype(np.float32)
    col = (tt % GRID_W).astype(np.float32)
    nf = 8
    inv = (np.float32(10000.0) ** (-np.arange(nf, dtype=np.float32) / np.float32(nf))).astype(np.float32)
    ang = np.concatenate([r[:, None] * inv, col[:, None] * inv], axis=-1).astype(np.float32)
    cos, sin = np.cos(ang).astype(np.float32), np.sin(ang).astype(np.float32)
    R = np.zeros((48, 2, L), np.float32)
    R[0:16, 0] = cos.T
    R[32:48, 0] = cos.T
    R[0:16, 1] = -sin.T
    R[32:48, 1] = sin.T
    return R


def _fm(v, n):
    return np.ascontiguousarray(np.asarray(v, np.float32).reshape(n, 128).T)


def _prep_shared(inp, depth, L):
    f = lambda k: np.asarray(inp[k], np.float32)
    w_in = f("w_in")
    win = np.zeros((depth, D, NG_IN * 128), np.float32)
    win[:, :, 0:768] = w_in[:, :, 0:768]
    win[:, :, 768:1024] = w_in[:, :, 768:1024]
    win[:, :, G_CQ * 128:G_CQ * 128 + 256] = w_in[:, :, 1040:1296]
    win[:, :, G_CKV * 128:G_CKV * 128 + 128] = w_in[:, :, 1296:1424]
    win[:, :, G_KR * 128 + 0:G_KR * 128 + 16] = w_in[:, :, 1424:1440]
    win[:, :, G_KR * 128 + 32:G_KR * 128 + 48] = w_in[:, :, 1440:1456]
    win[:, :, G_KRB * 128 + 0:G_KRB * 128 + 16] = w_in[:, :, 1440:1456]
    win[:, :, G_KRB * 128 + 32:G_KRB * 128 + 48] = w_in[:, :, 1424:1440]
    win[:, :, G_AB * 128 + 0:G_AB * 128 + 8] = w_in[:, :, 1024:1032]
    win[:, :, G_AB * 128 + 8:G_AB * 128 + 16] = w_in[:, :, 2480:2488]
    win[:, :, G_AB * 128 + 32:G_AB * 128 + 40] = w_in[:, :, 1032:1040]
    win[:, :, G_Z * 128:G_Z * 128 + 256] = w_in[:, :, 1456:1712]
    win[:, :, G_XS * 128:G_XS * 128 + 768] = w_in[:, :, 1712:2480]
    pk = np.zeros((depth, 128, NPK), np.float32)
    p = np.arange(128)
    for l in range(depth):
        pk[l, :, PK_NMP:PK_NMP + 8] = _fm(f("norm_mix_pre")[l], 8)
        pk[l, :, PK_NMO:PK_NMO + 8] = _fm(f("norm_mix_post")[l], 8)
        pk[l, :, PK_NFP:PK_NFP + 8] = _fm(f("norm_ffn_pre")[l], 8)
        pk[l, :, PK_NFO:PK_NFO + 8] = _fm(f("norm_ffn_post")[l], 8)
        pk[l, :, PK_BADA:PK_BADA + 48] = _fm(f("b_ada")[l], 48)
        for i in range(6):
            pk[l, :, PK_GCONV + i * 5:PK_GCONV + i * 5 + 5] = f("gdn_conv")[l][:, i * 128:(i + 1) * 128].T
            pk[l, :, PK_SCONV + i * 5:PK_SCONV + i * 5 + 5] = f("ssd_conv")[l][:, i * 128:(i + 1) * 128].T
            pk[l, :, PK_SCB + i] = f("ssd_conv_b")[l][i * 128:(i + 1) * 128]
        for cg in range(44):
            pk[l, :, PK_FCONV + cg * 3:PK_FCONV + cg * 3 + 3] = f("ffn_conv")[l][:, cg * 128:(cg + 1) * 128].T
        pk[l, :, PK_QN:PK_QN + 2] = _fm(f("mla_q_norm")[l], 2)
        pk[l, :, PK_KVN] = f("mla_kv_norm")[l]
        pk[l, :, PK_GDNN] = f("gdn_norm")[l][p % 64]
        pk[l, :, PK_SSDN:PK_SSDN + 2] = _fm(f("ssd_norm")[l], 2)
        for gi in range(2):
            pk[l, :, PK_SSDD + gi] = f("ssd_d")[l][2 * gi + p // 64]
        pk[l, 0:8, PK_ALOG] = f("gdn_a_log")[l].reshape(8)
        pk[l, 8:16, PK_ALOG] = f("ssd_a_log")[l].reshape(8)
        pk[l, 0:8, PK_DTB] = f("gdn_dt_bias")[l].reshape(8)
        pk[l, 8:16, PK_DTB] = f("ssd_dt_bias")[l].reshape(8)
    w_uq = f("mla_w_uq")
    wuq = np.zeros((depth, 256, 8 * 128), np.float32)
    wuqb = np.zeros((depth, 256, 8 * 64), np.float32)
    w_ukv = f("mla_w_ukv")
    wuk = np.zeros((depth, 128, 8 * 128), np.float32)
    wuv = np.zeros((depth, 128, 8 * 64), np.float32)
    for h in range(8):
        x1 = w_uq[:, :, h * 96 + 64:h * 96 + 80]
        x2 = w_uq[:, :, h * 96 + 80:h * 96 + 96]
        wuq[:, :, h * 128 + 0:h * 128 + 16] = x1
        wuq[:, :, h * 128 + 32:h * 128 + 48] = x2
        wuq[:, :, h * 128 + 64:h * 128 + 128] = w_uq[:, :, h * 96:h * 96 + 64]
        wuqb[:, :, h * 64 + 0:h * 64 + 16] = x2
        wuqb[:, :, h * 64 + 32:h * 64 + 48] = x1
        wuk[:, :, h * 128 + 64:h * 128 + 128] = w_ukv[:, :, h * 128:h * 128 + 64]
        wuv[:, :, h * 64:(h + 1) * 64] = w_ukv[:, :, h * 128 + 64:h * 128 + 128]
    def grp(w):
        dd, _, gc = w.shape
        g = gc // 128
        return np.ascontiguousarray(w.reshape(dd, 8, 128, g, 128).transpose(0, 3, 2, 1, 4)).reshape(dd, g, 128, 1024)
    return {"wada": grp(f("w_ada")), "win": grp(win), "pk": pk, "wuq": wuq, "wuqb": wuqb, "wuk": wuk, "wuv": wuv,
            "wout": f("w_out"), "wup": grp(f("ffn_w_up")), "wdn": f("ffn_w_down"), "consts": _consts(),
            "rope": _rope_table(L)}


def _in_maps(inp, ncores, depth, npr, L):
    sh = _prep_shared(inp, depth, L)
    f = lambda k: np.asarray(inp[k], np.float32)
    ndec = f("x_sample").shape[0]
    ckr = f("cache_mla_krope")
    ckr48 = np.zeros(ckr.shape[:-1] + (48,), np.float32)
    ckr48[..., 0:16] = ckr[..., 0:16]
    ckr48[..., 32:48] = ckr[..., 16:32]
    maps = []
    for c in range(ncores):
        b = c % ndec
        m = dict(sh)
        m["xp"] = np.ascontiguousarray(f("x_prompt")[c * npr:(c + 1) * npr])
        m["xs"] = np.ascontiguousarray(f("x_sample")[b])
        m["cckv"] = np.ascontiguousarray(f("cache_mla_ckv")[b])
        m["ckr"] = np.ascontiguousarray(ckr48[b])
        m["stg"] = np.ascontiguousarray(f("state_gdn")[b])
        m["sts"] = np.ascontiguousarray(f("state_ssd")[b])
        ct = np.zeros((128, 8, 2), np.float32)
        ct[:, :, 0] = f("c")[b].reshape(8, 128).T
        ct[:, :, 1] = f("c_ctx").reshape(8, 128).T
        m["condT"] = ct
        maps.append(m)
    return maps


def run(inp, ncores=NCORES, depth=DEPTH, seq=SEQ, dec_seq=DEC_SEQ, past=PAST, npr=NPR, debug=None, stop=99, only=None):
    bld = Builder(depth=depth, seq=seq, dec_seq=dec_seq, past=past, npr=npr, debug=debug, stop=stop, only=only)
    maps = _in_maps(inp, ncores, depth, npr, dec_seq)
    res = run_bass_kernel_spmd(bld.nc, maps, core_ids=list(range(ncores)))
    return res.results, bld


def kernel(**inputs):
    res, _ = run(inputs)
    ndec = DEC_BATCH
    y_prompt = np.concatenate([res[c]["yp"] for c in range(NCORES)], axis=0).astype(np.float32)
    y_sample = np.stack([res[b]["ys"] for b in range(ndec)], axis=0).astype(np.float32)
    ckv = np.concatenate([res[c]["ockv"] for c in range(NCORES)], axis=0).astype(np.float32)
    kr = np.concatenate([res[c]["okr"] for c in range(NCORES)], axis=0).astype(np.float32)
    sg = np.concatenate([res[c]["osg"] for c in range(NCORES)], axis=0).astype(np.float32)
    ss = np.concatenate([res[c]["oss"] for c in range(NCORES)], axis=0).astype(np.float32)
    return (y_prompt, y_sample, ckv, kr, sg, ss)
```

```python
import threading
import numpy as np
import concourse.bass as bass
import concourse.mybir as mybir
from concourse.bass_utils import run_bass_kernel_spmd

F32 = mybir.dt.float32
BF16 = mybir.dt.bfloat16
F32R = mybir.dt.float32r
AF = mybir.ActivationFunctionType
ALU = mybir.AluOpType

D = 1024
DEPTH = 2
BATCH = 16
SEQ = 256
DEC_BATCH = 4
DEC_SEQ = 4096
PAST = 256
GRID_W = 64
EPS = 1e-6
DFF = 2816
NCORES = 8
NPR = BATCH // NCORES
NG_IN = 23
G_Q, G_K, G_V, G_GATE, G_CQ, G_CKV, G_KR, G_KRB, G_AB, G_Z, G_XS, G_BM, G_CM = 0, 2, 4, 6, 8, 10, 11, 12, 13, 14, 16, 18, 20
C_ID, C_TRIF, C_TRIB, C_TRISF, C_TRISB, C_NEGTF, C_NEGTB, C_POSSF, C_POSSB, C_ONESBD, C_ONES = range(11)
NCONST = 11
PK_NMP, PK_NMO, PK_NFP, PK_NFO, PK_BADA = 0, 8, 16, 24, 32
PK_GCONV = 80
PK_SCONV = 110
PK_SCB = 140
PK_FCONV = 146
PK_QN = 278
PK_KVN = 280
PK_GDNN = 281
PK_SSDN = 282
PK_SSDD = 284
PK_ALOG = 286
PK_DTB = 287
NPK = 288


class Buf:
    __slots__ = ("w", "r", "excl")

    def __init__(self, excl=False):
        self.w = None
        self.r = {}
        self.excl = excl


class V:
    __slots__ = ("ap", "buf")

    def __init__(self, ap, buf):
        self.ap = ap
        self.buf = buf

    def bitcast(self, dt):
        return V(self.ap.bitcast(dt), self.buf)

    def bc(self, shape):
        return V(self.ap.to_broadcast(shape), self.buf)


class Tl:
    def __init__(self, t, buf=None):
        self.t = t
        self.buf = buf if buf is not None else Buf()

    def __getitem__(self, idx):
        return V(self.t[idx], self.buf)


class Prog:
    CE = ("pe", "dve", "act", "pool")

    def __init__(self, nc):
        self.nc = nc
        self.eng = {"pe": nc.tensor, "dve": nc.vector, "act": nc.scalar, "pool": nc.gpsimd, "sp": nc.sync}
        self.sem = {}
        self.cnt = {}
        self.sid = 0
        for e in self.CE:
            self.sem[e] = self._newsem(e)
        self.dsem = {"sp": [self._newsem("dsp%d" % i) for i in range(20)],
                     "pool": [self._newsem("dpl%d" % i) for i in range(8)]}
        self.drr = {"sp": 0, "pool": 0}
        self.waited = {e: {} for e in self.eng}
        self.ninst = 0
        self.nwait = 0
        self.on_barrier = None
        self.on_op = None
        self.tok = None

    def _newsem(self, name):
        h = self.nc.alloc_semaphore(name)
        s = (self.sid, h)
        self.cnt[self.sid] = 0
        self.sid += 1
        return s

    def _wait(self, e, deps):
        best = {}
        for (s, v) in deps:
            if best.get(s, (None, 0))[1] < v:
                best[s] = (s, v)
        for s, v in best.values():
            if self.waited[e].get(s[0], 0) < v:
                self.eng[e].wait_ge(s[1], v)
                self.waited[e][s[0]] = v
                self.nwait += 1

    def _deps(self, e, reads, writes, pe_acc=False):
        deps = []
        for b in reads:
            if b.w is not None:
                deps.append(b.w)
            if b.excl and e in self.sem:
                me = self.sem[e][0]
                for sid, tok in b.r.items():
                    if sid != me:
                        deps.append(tok)
        for b in writes:
            if b.w is not None:
                if not (pe_acc and b.w[0] is self.sem["pe"]):
                    deps.append(b.w)
            for s, v in b.r.values():
                deps.append((s, v))
        return deps

    def _commit(self, tok, reads, writes):
        if self.tok is not None:
            self.tok[tok[0][0]] = tok
        for b in reads:
            b.r[tok[0][0]] = tok
        for b in writes:
            b.w = tok
            b.r = {}

    def op(self, e, fn, reads, writes, pe_acc=False):
        reads = [v.buf for v in reads if v is not None]
        writes = [v.buf for v in writes if v is not None]
        self._wait(e, self._deps(e, reads, writes, pe_acc))
        inst = fn(self.eng[e])
        s = self.sem[e]
        self.cnt[s[0]] += 1
        inst.then_inc(s[1], 1)
        self.ninst += 1
        self._commit((s, self.cnt[s[0]]), reads, writes)
        if self.on_op is not None:
            self.on_op()

    def dma(self, out, in_, q="sp", slow=False):
        reads = [in_.buf]
        writes = [out.buf]
        sl = self.dsem[q]
        s = sl[self.drr[q] % len(sl)]
        self.drr[q] += 1
        deps = self._deps(q, reads, writes)
        if self.cnt[s[0]] > 0:
            deps.append((s, self.cnt[s[0]]))
        self._wait(q, deps)
        kw = {"allow_slow_non_contiguous": True} if slow else {}
        inst = self.eng[q].dma_start(out=out.ap, in_=in_.ap, **kw)
        self.cnt[s[0]] += 16
        inst.then_inc(s[1], 16)
        self.ninst += 1
        self._commit((s, self.cnt[s[0]]), reads, writes)
        if self.on_op is not None:
            self.on_op()

    def barrier(self, local=False):
        if local and self.tok is not None:
            deps = list(self.tok.values())
        else:
            allsems = [self.sem[e] for e in self.CE] + self.dsem["sp"] + self.dsem["pool"]
            deps = [(s, self.cnt[s[0]]) for s in allsems if self.cnt[s[0]] > 0]
        for e in self.eng:
            self._wait(e, deps)
        if self.on_barrier is not None:
            self.on_barrier(local)

    def mm(self, out, lhsT, rhs, start=True, stop=True):
        self.op("pe", lambda E: E.matmul(out.ap, lhsT=lhsT.ap, rhs=rhs.ap, start=start, stop=stop),
                [lhsT, rhs], [out], pe_acc=not start)

    def tr(self, out, in_, ident):
        self.op("pe", lambda E: E.transpose(out.ap, in_.ap, ident.ap), [in_, ident], [out])

    def act(self, out, in_, func, bias=None, scale=None, accum=None, after=()):
        kw = {}
        rd = [in_] + list(after)
        if bias is not None:
            if isinstance(bias, V):
                kw["bias"] = bias.ap
                rd.append(bias)
            else:
                kw["bias"] = float(bias)
        if scale is not None:
            if isinstance(scale, V):
                kw["scale"] = scale.ap
                rd.append(scale)
            else:
                kw["scale"] = float(scale)
        wr = [out]
        if accum is not None:
            kw["accum_out"] = accum.ap
            wr.append(accum)
        self.op("act", lambda E: E.activation(out=out.ap, in_=in_.ap, func=func, **kw), rd, wr)

    def tt(self, out, in0, in1, op, e="dve"):
        self.op(e, lambda E: E.tensor_tensor(out=out.ap, in0=in0.ap, in1=in1.ap, op=op), [in0, in1], [out])

    def ts(self, out, in0, s1, s2=None, op0=ALU.mult, op1=None, e="dve"):
        rd = [in0]
        a1 = s1
        if isinstance(s1, V):
            a1 = s1.ap
            rd.append(s1)
        a2 = s2
        if isinstance(s2, V):
            a2 = s2.ap
            rd.append(s2)
        kw = {}
        if op1 is not None:
            kw["op1"] = op1
        self.op(e, lambda E: E.tensor_scalar(out=out.ap, in0=in0.ap, scalar1=a1, scalar2=a2, op0=op0, **kw),
                rd, [out])

    def stt(self, out, in0, sc, in1, op0, op1):
        rd = [in0, in1]
        a = sc
        if isinstance(sc, V):
            a = sc.ap
            rd.append(sc)
        self.op("dve", lambda E: E.scalar_tensor_tensor(out=out.ap, in0=in0.ap, scalar=a, in1=in1.ap,
                                                       op0=op0, op1=op1), rd, [out])

    def cp(self, out, in_, e="dve"):
        if e == "act":
            self.op("act", lambda E: E.copy(out=out.ap, in_=in_.ap), [in_], [out])
        else:
            self.op(e, lambda E: E.tensor_copy(out=out.ap, in_=in_.ap), [in_], [out])

    def memset(self, v, val, e="pool"):
        self.op(e, lambda E: E.memset(v.ap, val), [], [v])

    def recip(self, out, in_):
        self.op("dve", lambda E: E.reciprocal(out=out.ap, in_=in_.ap), [in_], [out])


class Arena:
    def __init__(self, nc, lo, hi):
        self.nc = nc
        self.lo = lo
        self.hi = hi
        self.p = lo
        self.n = 0
        self.peak = lo
        self.reg = []

    def mark(self):
        return self.p

    def release(self, m):
        self.p = m

    def tile_at(self, off, shape, dt, name="t"):
        self.n += 1
        t = self.nc.alloc_sbuf_tensor_at("%s_%d" % (name, self.n), list(shape), dt, offset=off)
        tl = Tl(t)
        nb = self.nbytes(shape, dt)
        keep = []
        for (o, n, old) in self.reg:
            if o < off + nb and off < o + n:
                toks = list(old.buf.r.values())
                if old.buf.w is not None:
                    toks.append(old.buf.w)
                for tok in toks:
                    sid = tok[0][0]
                    if tl.buf.r.get(sid, (None, 0))[1] < tok[1]:
                        tl.buf.r[sid] = tok
                if not (off <= o and o + n <= off + nb):
                    keep.append((o, n, old))
            else:
                keep.append((o, n, old))
        keep.append((off, nb, tl))
        self.reg = keep
        return tl

    @staticmethod
    def nbytes(shape, dt):
        n = 1
        for s in shape[1:]:
            n *= s
        return (n * (2 if dt == BF16 else 4) + 31) // 32 * 32

    def tile(self, shape, dt, name="t"):
        nb = self.nbytes(shape, dt)
        off = self.p
        assert off + nb <= self.hi, "SBUF arena overflow %s %d+%d>%d" % (name, off, nb, self.hi)
        self.p += nb
        self.peak = max(self.peak, self.p)
        return self.tile_at(off, shape, dt, name)


class PsumPool:
    def __init__(self, nc=None, banks=None):
        if banks is None:
            banks = [Tl(nc.alloc_psum_tensor("psb%d" % i, [128, 512], F32), Buf(excl=True)) for i in range(8)]
        self.banks = banks
        self.res = set()
        self.i = 0

    def get(self):
        while True:
            k = self.i % len(self.banks)
            self.i += 1
            if k not in self.res:
                return self.banks[k]

    def reserve(self):
        b = self.get()
        self.res.add(self.banks.index(b))
        return b

    def free(self, b):
        self.res.discard(self.banks.index(b))


def dv(ap):
    return V(ap, Buf())


def sub(tl, ap):
    return V(ap, tl.buf)


class Ctx:
    pass


CTX_NAMES = ("A", "ps", "GWT", "GWTMP", "gw_key", "pa", "qt", "ckvd", "krd", "catd", "actT",
             "_oi", "_wi", "_obi", "_ny", "dI", "hl")


class Coop:
    def __init__(self):
        self.evs = []
        self.alive = []
        self.cur = 0
        self.exc = None
        self.on_resume = None

    def run(self, fns):
        n = len(fns)
        self.evs = [threading.Event() for _ in range(n)]
        self.alive = [True] * n
        done = threading.Event()

        def wrap(i, fn):
            self.evs[i].wait()
            try:
                fn()
            except BaseException as e:
                self.exc = e
            self.alive[i] = False
            nxt = self._next(i)
            if nxt is None:
                done.set()
            else:
                self.cur = nxt
                self.evs[nxt].set()

        ths = [threading.Thread(target=wrap, args=(i, f)) for i, f in enumerate(fns)]
        for t in ths:
            t.start()
        self.cur = 0
        self.evs[0].set()
        done.wait()
        for t in ths:
            t.join()
        self.evs = []
        if self.exc is not None:
            raise self.exc

    def _next(self, i):
        n = len(self.alive)
        for k in range(1, n + 1):
            j = (i + k) % n
            if self.alive[j] and j != i:
                return j
        return None

    def switch(self):
        if not self.evs:
            return
        i = self.cur
        nxt = self._next(i)
        if nxt is None:
            return
        self.evs[i].clear()
        self.cur = nxt
        self.evs[nxt].set()
        self.evs[i].wait()
        if self.on_resume is not None:
            self.on_resume()


class Builder:
    def __getattr__(self, name):
        if name in CTX_NAMES:
            return getattr(self.__dict__["_tls"].ctx, name)
        raise AttributeError(name)

    def __setattr__(self, name, val):
        if name in CTX_NAMES:
            setattr(self.__dict__["_tls"].ctx, name, val)
        else:
            self.__dict__[name] = val

    def use_ctx(self, ctx):
        self._tls.ctx = ctx
        self.P.tok = ctx.tokens

    def barrier(self):
        ctx = self._tls.ctx
        self.P.tok = ctx.tokens
        if self.coop.evs:
            self.P.barrier(local=True)
        else:
            self.P.barrier()

    def cswitch(self):
        self.coop.switch()

    def _tick(self):
        self._tick_n += 1
        if self._tick_n % 24 == 0:
            self.coop.switch()

    def __init__(self, depth=DEPTH, seq=SEQ, dec_seq=DEC_SEQ, past=PAST, npr=NPR, debug=None, stop=99, only=None):
        self.depth, self.seq, self.dec_seq, self.past, self.npr = depth, seq, dec_seq, past, npr
        self.stop, self.only = stop, only
        nc = bass.Bass("TRN2", target_bir_lowering=False)
        self.nc = nc
        self.P = Prog(nc)
        dt = nc.dram_tensor
        L, S = dec_seq, seq
        I = lambda name, shape, d=F32: dt(name, list(shape), d, kind="ExternalInput").ap()
        O = lambda name, shape, d=F32: dt(name, list(shape), d, kind="ExternalOutput").ap()
        X = lambda name, shape, d=F32: dt(name, list(shape), d, kind="Internal").ap()
        self.xp = I("xp", [npr, S, D])
        self.xs = I("xs", [L, D])
        self.cckv = I("cckv", [depth, past, 128])
        self.ckr = I("ckr", [depth, past, 48])
        self.stg = I("stg", [depth, 2, 4, 64, 64])
        self.sts = I("sts", [depth, 2, 4, 64, 128])
        self.condT = I("condT", [128, 8, 2])
        self.wada = I("wada", [depth, 48, 128, 8 * 128])
        self.win = I("win", [depth, NG_IN, 128, 8 * 128])
        self.pk = I("pk", [depth, 128, NPK])
        self.wuq = I("wuq", [depth, 256, 8 * 128])
        self.wuqb = I("wuqb", [depth, 256, 8 * 64])
        self.wuk = I("wuk", [depth, 128, 8 * 128])
        self.wuv = I("wuv", [depth, 128, 8 * 64])
        self.wout = I("wout", [depth, D, D])
        self.wup = I("wup", [depth, 44, 128, 8 * 128])
        self.wdn = I("wdn", [depth, DFF, D])
        self.consts = I("consts", [128, NCONST * 128])
        self.rope = I("rope", [48, 2, L])
        self.yp = O("yp", [npr, S, D])
        self.ys = O("ys", [L, D])
        self.ockv = O("ockv", [npr, depth, S, 128])
        self.okr = O("okr", [npr, depth, S, 32])
        self.osg = O("osg", [npr, depth, 2, 4, 64, 64])
        self.oss = O("oss", [npr, depth, 2, 4, 64, 128])
        self._tls = threading.local()
        self.coop = Coop()
        self.xres = X("xres", [L, D])

        def scratch(tag, Ls):
            return {"pa": X("pa" + tag, [2048, Ls], BF16),
                    "qt": X("qt" + tag, [1024, Ls], BF16),
                    "ckvd": X("ckvd" + tag, [128, Ls + past], BF16),
                    "krd": X("krd" + tag, [48, Ls + past], BF16),
                    "catd": X("catd" + tag, [1024, Ls], BF16),
                    "actT": X("actT" + tag, [DFF, Ls], BF16)}
        self.scr_main = scratch("", L)
        self.scr_p = [scratch("_p%d" % i, S) for i in range(npr)]
        self.dbg = {}
        if debug:
            self.dbg = {k: O("dbg_" + k, shp) for k, shp in debug.items()}
        lo = (nc.sbuf_base + 31) // 32 * 32
        self.arenas = []
        self.main_ctx = self.make_ctx(Arena(nc, lo, nc.sbuf_top // 32 * 32), PsumPool(nc), self.scr_main)
        self.use_ctx(self.main_ctx)
        self.P.on_barrier = lambda local: ([self._tls.ctx.A.reg.clear()] if local else [a.reg.clear() for a in self.arenas])
        self._tick_n = 0
        self.P.on_op = self._tick
        self.coop.on_resume = lambda: setattr(self.P, "tok", self._tls.ctx.tokens)
        self.build()

    def make_ctx(self, arena, ps, scr):
        c = Ctx()
        c.A, c.ps = arena, ps
        self.arenas.append(arena)
        for k, v in scr.items():
            setattr(c, k, v)
        c.tokens = {}
        c.gw_key = None
        c.GWT = c.GWTMP = None
        c._oi = c._wi = c._obi = c._ny = 0
        c.dI, c.hl = None, False
        return c

    def cst(self, blk, r0=0, r1=128, c0=0, c1=128):
        return self.CON[r0:r1, blk * 128 + c0:blk * 128 + c1]

    def pkc(self, col, r0=0, r1=128):
        return self.PK[r0:r1, col:col + 1]

    def rstd_from(self, out, in_, n, rows=128):
        P = self.P
        P.act(out, in_, AF.Ln, bias=self.EPSB[0:rows, :], scale=1.0 / n)
        P.act(out, out, AF.Exp, scale=-0.5)

    def build(self):
        P, A = self.P, self.A
        self.CON = A.tile([128, NCONST * 128], F32, "con")
        P.dma(self.CON[:, :], dv(self.consts[:, :]))
        self.IDB = A.tile([128, 128], BF16, "idb")
        P.cp(self.IDB[:, :], self.cst(C_ID))
        self.ONESB = A.tile([128, 64], BF16, "onesb")
        P.memset(self.ONESB[:, :], 1.0)
        self.EPSB = A.tile([128, 1], F32, "epsb")
        P.memset(self.EPSB[:, :], EPS)
        self.ONE1 = A.tile([128, 1], F32, "one1")
        P.memset(self.ONE1[:, :], 1.0)
        self.ZERO1 = A.tile([128, 1], F32, "zero1")
        P.memset(self.ZERO1[:, :], 0.0)
        self.CONDS = A.tile([128, 8, 2], F32, "conds")
        ct = A.tile([128, 8, 2], F32, "condraw")
        P.dma(ct[:, :, :], dv(self.condT[:, :, :]))
        P.act(self.CONDS[:, :, :], ct[:, :, :], AF.Silu)
        self.PK = A.tile([128, NPK], F32, "pk")
        self.MOD = A.tile([128, 48, 2], F32, "mod")
        self.MODV = A.tile([128, 6, 8, 2], F32, "modv")
        self.GWT = A.tile([128, D], F32, "gwt")
        self.GWTMP = A.tile([128, 128], F32, "gwtmp")
        self.gw_key = None
        base_mark = A.mark()
        npr = self.npr
        span = (A.hi - base_mark) // npr // 32 * 32
        banks = self.main_ctx.ps.banks
        nb = 8 // npr
        pctx = []
        for i in range(npr):
            c = self.make_ctx(Arena(self.nc, base_mark + i * span, base_mark + (i + 1) * span),
                              PsumPool(banks=banks[i * nb:(i + 1) * nb]), self.scr_p[i])
            pctx.append(c)
        for c in pctx:
            self.use_ctx(c)
            self.GWT = c.A.tile([128, D], F32, "gwt")
            self.GWTMP = c.A.tile([128, 128], F32, "gwtmp")
            c.base = c.A.mark()
        self.use_ctx(self.main_ctx)
        for l in range(self.depth):
            A.release(base_mark)
            P.dma(self.PK[:, :], dv(self.pk[l]))
            self.adaln(l)
            for c in [self.main_ctx] + pctx:
                c.gw_key = None
            last = (l == self.depth - 1)
            if self.only != "dec":
                def mk(i):
                    def fn():
                        self.use_ctx(pctx[i])
                        pctx[i].A.release(pctx[i].base)
                        self.layer_seq(l, i, self.seq, False, last)
                    return fn
                self.coop.run([mk(i) for i in range(npr)])
                self.use_ctx(self.main_ctx)
                self.barrier()
            if self.only == "pr":
                continue
            m = A.mark()
            self.layer_seq(l, -1, self.dec_seq, True, last)
            self.barrier()
            A.release(m)
        self.barrier()

    def adaln(self, l):
        P, A = self.P, self.A
        m = A.mark()
        pt = self.ps.get()
        wts = [A.tile([128, 8, 128], F32, "wada") for _ in range(3)]
        for c in range(48):
            wt = wts[c % 3]
            P.dma(sub(wt, wt.t[:, :, :].rearrange("p k c -> p (k c)")), dv(self.wada[l, c]))
            for k in range(8):
                P.mm(pt[:, c * 2:c * 2 + 2], wt[:, k, :], self.CONDS[:, k, :], start=(k == 0), stop=(k == 7))
        P.tt(self.MOD[:, :, :], sub(pt, pt.t[:, 0:96].rearrange("p (c j) -> p c j", j=2)),
             sub(self.PK, self.PK.t[:, PK_BADA:PK_BADA + 48].unsqueeze(2).to_broadcast([128, 48, 2])), ALU.add)
        for half, (nw, nwo) in enumerate(((PK_NMP, PK_NMO), (PK_NFP, PK_NFO))):
            sh = self.MOD[:, (half * 3 + 0) * 8:(half * 3 + 1) * 8, :]
            sc = self.MOD[:, (half * 3 + 1) * 8:(half * 3 + 2) * 8, :]
            g = self.MOD[:, (half * 3 + 2) * 8:(half * 3 + 3) * 8, :]
            nwb = sub(self.PK, self.PK.t[:, nw:nw + 8].unsqueeze(2).to_broadcast([128, 8, 2]))
            nwob = sub(self.PK, self.PK.t[:, nwo:nwo + 8].unsqueeze(2).to_broadcast([128, 8, 2]))
            P.stt(self.MODV[:, half * 3 + 0, :, :], sc, 1.0, nwb, ALU.add, ALU.mult)
            P.cp(self.MODV[:, half * 3 + 1, :, :], sh)
            P.tt(self.MODV[:, half * 3 + 2, :, :], g, nwob, ALU.mult)
        self.gw_key = None
        self.barrier()
        A.release(m)

    def layer_seq(self, l, sq, L, is_dec, last):
        P, A = self.P, self.A
        who = 0 if is_dec else 1
        TW = min(512, L)
        NT = L // TW
        NB = L // 128
        NK = L + (self.past if is_dec else 0)
        if is_dec:
            x_in = self.xs if l == 0 else self.xres
            x_mid = self.xres
            x_out = self.ys if last else self.xres
        else:
            x_in = self.xp[sq] if l == 0 else self.yp[sq]
            x_mid = self.yp[sq]
            x_out = self.yp[sq]
        hreg = A.mark()
        hT = [A.tile([128, 8, TW], BF16, "hT") for _ in range(NT)]
        OG = A.tile_at(hreg, [128, 2, L], F32, "og")
        OY = A.tile_at(hreg + A.nbytes([128, 2, L], F32), [128, 2, L], F32, "oy")
        ABT = A.tile([128, NB, 80], F32, "abt")

        if self.stop < 0:
            return
        m0 = A.mark()
        xt = [A.tile([128, D], F32, "xt") for _ in range(2)]
        for b in range(NB):
            x = xt[b % 2]
            P.dma(x[:, :], dv(x_in[b * 128:(b + 1) * 128, :]))
            self.norm_to_hT(x, hT, b, TW, 0, who)
        self.barrier()
        A.release(m0)
        if self.stop < 1:
            return
        m0 = A.mark()
        self.phase_a(l, sq, L, is_dec, hT, TW, NT, NB, NK, ABT)
        self.barrier()
        A.release(m0)
        if self.stop < 2:
            return
        m0 = A.mark()
        self.phase_scan(l, sq, L, is_dec, TW, NT, NB, ABT, OG, OY)
        self.barrier()
        A.release(m0)
        if self.stop < 3:
            return
        m0 = A.mark()
        self.phase_attn(l, L, is_dec, TW, NT, NK)
        self.barrier()
        A.release(m0)
        if self.stop < 4:
            return
        m0 = A.mark()
        wo = A.tile([128, 8, D], BF16, "wout")
        self.load_w_bf16(wo, self.wout[l], 8, D)
        xt = [A.tile([128, D], F32, "xt") for _ in range(2)]
        ct = [A.tile([128, 8, 128], BF16, "ct") for _ in range(2)]
        for b in range(NB):
            x = xt[b % 2]
            c = ct[b % 2]
            P.dma(x[:, :], dv(x_in[b * 128:(b + 1) * 128, :]))
            P.dma(c[:, :, :], dv(self.catd[:, b * 128:(b + 1) * 128].rearrange("(k p) t -> p k t", p=128)))
            import os
            cut2 = int(os.environ.get("K_CUT2", "99"))
            self.gw_rows(2, who)
            pss = [self.ps.get(), self.ps.get()]
            for hh in range(2):
                for k in range(8):
                    P.mm(pss[hh][:, :], c[:, k, :], wo[:, k, hh * 512:(hh + 1) * 512], start=(k == 0), stop=(k == 7))
            if cut2 < 1:
                continue
            self.residual(pss, x, 2, who, x_mid, b)
            if cut2 < 5:
                continue
            self.norm_to_hT(x, hT, b, TW, 3, who)
        self.barrier()
        A.release(m0)
        if self.stop < 5:
            return
        m0 = A.mark()
        self.phase_ffn_up(l, L, hT, TW, NT)
        self.barrier()
        A.release(m0)
        if self.stop < 6:
            return
        m0 = A.mark()
        wd = A.tile([128, 22, D], BF16, "wdn")
        self.load_w_bf16(wd, self.wdn[l], 22, D)
        xt = [A.tile([128, D], F32, "xt") for _ in range(2)]
        at = [A.tile([128, 22, 128], BF16, "at") for _ in range(2)]
        for b in range(NB):
            x = xt[b % 2]
            a = at[b % 2]
            P.dma(x[:, :], dv(x_mid[b * 128:(b + 1) * 128, :]))
            P.dma(a[:, :, :], dv(self.actT[:, b * 128:(b + 1) * 128].rearrange("(j p) t -> p j t", p=128)))
            self.gw_rows(5, who)
            pss = [self.ps.get(), self.ps.get()]
            for hh in range(2):
                for j in range(22):
                    P.mm(pss[hh][:, :], a[:, j, :], wd[:, j, hh * 512:(hh + 1) * 512], start=(j == 0), stop=(j == 21))
            self.residual(pss, x, 5, who, x_out, b)
        self.barrier()
        A.release(m0)
        A.release(hreg)

    def load_w_bf16(self, dst, src, nk, ncols, engs=("dve", "act")):
        P, A = self.P, self.A
        cw = min(ncols, 512)
        st = [A.tile([128, cw], F32, "wst") for _ in range(3)]
        i = 0
        for k in range(nk):
            for c0 in range(0, ncols, cw):
                s = st[i % 3]
                P.dma(s[:, :], dv(src[k * 128:(k + 1) * 128, c0:c0 + cw]))
                P.cp(dst[:, k, c0:c0 + cw], s[:, :], e=engs[i % len(engs)])
                i += 1

    def load_wg(self, dst, src, stage, e):
        P = self.P
        P.dma(sub(stage, stage.t[:, :, :].rearrange("p k c -> p (k c)")), dv(src))
        P.cp(dst[:, :, :], stage[:, :, :], e=e)

    def norm_to_hT(self, x, hT, b, TW, mv, who):
        P, A = self.P, self.A
        m = A.mark()
        junk = A.tile([128, D], BF16, "junk")
        ssq = A.tile([128, 1], F32, "ssq")
        rstd = A.tile([128, 1], F32, "rstd")
        xb = A.tile([128, D], BF16, "xb")
        import os
        cut = int(os.environ.get("K_CUT", "99"))
        P.act(junk[:, :], x[:, :], AF.Square, scale=float(D) ** -0.5, accum=ssq[:, :])
        if cut < 1:
            A.release(m); return
        self.rstd_from(rstd[:, :], ssq[:, :], 1.0)
        if cut < 2:
            A.release(m); return
        P.ts(xb[:, :], x[:, :], rstd[:, :], None, op0=ALU.mult)
        if cut < 3:
            A.release(m); return
        pt = self.ps.get()
        ptb = pt.t[:, :].bitcast(BF16)
        for k in range(8):
            P.tr(sub(pt, ptb[:, k * 128:(k + 1) * 128]), xb[:, k * 128:(k + 1) * 128], self.IDB[:, :])
        tt_, off = b * 128 // TW, (b * 128) % TW
        if cut < 4:
            A.release(m); return
        for k in range(8):
            src = sub(pt, ptb[:, k * 128:(k + 1) * 128])
            dst = hT[tt_][:, k, off:off + 128]
            if (k % 2 == 0 or cut == 4) and cut != 5:
                P.ts(dst, src, self.MODV[:, mv, k, who:who + 1], self.MODV[:, mv + 1, k, who:who + 1],
                     op0=ALU.mult, op1=ALU.add)
            else:
                P.act(dst, src, AF.Identity, bias=self.MODV[:, mv + 1, k, who:who + 1],
                      scale=self.MODV[:, mv, k, who:who + 1])
        A.release(m)

    def residual(self, pss, x, mv, who, x_dst, b):
        P, A = self.P, self.A
        m = A.mark()
        GW = self.gw_rows(mv, who)
        junk = A.tile([128, 512], F32, "junk")
        ss = A.tile([128, 2], F32, "ss")
        rstd = A.tile([128, 1], F32, "rstd")
        t = A.tile([128, D], F32, "t")
        import os
        cut2 = int(os.environ.get("K_CUT2", "99"))
        for hh in range(2):
            P.act(junk[:, :], pss[hh][:, :], AF.Square, scale=float(D) ** -0.5, accum=ss[:, hh:hh + 1])
        if cut2 < 2:
            A.release(m); return
        P.tt(rstd[:, :], ss[:, 0:1], ss[:, 1:2], ALU.add)
        self.rstd_from(rstd[:, :], rstd[:, :], 1.0)
        for hh in range(2):
            P.tt(t[:, hh * 512:(hh + 1) * 512], pss[hh][:, :], GW[:, hh * 512:(hh + 1) * 512], ALU.mult)
        if cut2 < 3:
            A.release(m); return
        P.stt(x[:, :], t[:, :], rstd[:, :], x[:, :], ALU.mult, ALU.add)
        if cut2 < 4:
            A.release(m); return
        P.dma(dv(x_dst[b * 128:(b + 1) * 128, :]), x[:, :], q="pool")
        A.release(m)

    def gw_rows(self, mv, who):
        if self.gw_key == (mv, who):
            return self.GWT
        P = self.P
        for k in range(8):
            P.ts(self.GWTMP[:, :], self.cst(C_ONES), self.MODV[:, mv, k, who:who + 1], None, op0=ALU.mult)
            pt = self.ps.get()
            P.mm(pt[:, 0:128], self.GWTMP[:, :], self.cst(C_ID))
            P.cp(self.GWT[:, k * 128:(k + 1) * 128], pt[:, 0:128])
        self.gw_key = (mv, who)
        return self.GWT
    def phase_a(self, l, sq, L, is_dec, hT, TW, NT, NB, NK, ABT):
        P, A = self.P, self.A
        stg = [A.tile([128, 8, 128], F32, "wstg") for _ in range(2)]
        wgs = [A.tile([128, 8, 128], BF16, "wg") for _ in range(3)]
        RAW = [A.tile([128, L + 4], BF16, "raw") for _ in range(2)]
        for r in RAW:
            P.memset(r[:, 0:2], 0.0)
            P.memset(r[:, L + 2:L + 4], 0.0)
        DG = A.tile([128, 5, 128], BF16, "dg")
        OUTS = [A.tile([128, TW], F32, "outs") for _ in range(3)]
        TMP = [A.tile([128, TW], F32, "tmpa") for _ in range(3)]
        CQRAW = A.tile([128, 2, L], F32, "cqraw")
        self._oi = 0
        self._wi = 0

        OUTB = [A.tile([128, TW], BF16, "outb") for _ in range(3)]
        self._obi = 0

        def nxt_out():
            self._oi += 1
            return OUTS[self._oi % 3]

        def nxt_outb():
            self._obi += 1
            return OUTB[self._obi % 3]

        def load_group(g, ncol=128):
            self._wi += 1
            w = wgs[self._wi % 3]
            self.load_wg(w, self.win[l, g], stg[self._wi % 2], "act" if self._wi % 2 else "dve")
            return w

        def proj(w, t, ncol=128):
            pt = self.ps.get()
            for k in range(8):
                P.mm(pt[0:ncol, 0:TW], w[:, k, 0:ncol], hT[t][:, k, :], start=(k == 0), stop=(k == 7))
            return pt

        def store_pa(row0, t, src):
            P.dma(dv(self.pa[row0:row0 + 128, t * TW:(t + 1) * TW]), src, q="pool")

        convs = []
        for i in range(6):
            kind = "qk" if i < 4 else "plain"
            convs.append((G_Q + i, PK_GCONV + i * 5, None, kind, i * 128, 0.125 if i < 2 else 1.0))
        for i in range(6):
            convs.append((G_XS + i, PK_SCONV + i * 5, PK_SCB + i, "plain", 1280 + i * 128, 1.0))
        for ci, (g, ccol, bcol, kind, row0, qs) in enumerate(convs):
            w = load_group(g)
            raw = RAW[ci % 2]
            for t in range(NT):
                pt = proj(w, t)
                if t % 2 == 0:
                    P.cp(raw[:, 2 + t * TW:2 + (t + 1) * TW], pt[:, 0:TW], e="act")
                else:
                    P.cp(raw[:, 2 + t * TW:2 + (t + 1) * TW], pt[:, 0:TW], e="dve")
            for j in range(5):
                P.ts(DG[:, j, :], self.IDB[:, :], self.pkc(ccol + j), None, op0=ALU.mult)
            for t in range(NT):
                pt = self.ps.get()
                for j in range(5):
                    P.mm(pt[:, 0:TW], DG[:, j, :], raw[:, t * TW + j:t * TW + j + TW], start=(j == 0), stop=(j == 4))
                ob_ = nxt_outb()
                if kind == "qk":
                    o = nxt_out()
                    P.act(o[:, :], pt[:, 0:TW], AF.Silu)
                    sqt = TMP[0]
                    P.tt(sqt[:, :], o[:, :], o[:, :], ALU.mult)
                    p2 = self.ps.get()
                    P.mm(p2[:, 0:TW], self.cst(C_ONESBD), sqt[:, :])
                    rs = TMP[1]
                    P.act(rs[:, :], p2[:, 0:TW], AF.Ln, bias=self.EPSB[:, :])
                    P.act(rs[:, :], rs[:, :], AF.Exp, scale=-0.5)
                    P.stt(ob_[:, :], o[:, :], qs, rs[:, :], ALU.mult, ALU.mult)
                elif bcol is None:
                    P.act(ob_[:, :], pt[:, 0:TW], AF.Silu)
                else:
                    P.act(ob_[:, :], pt[:, 0:TW], AF.Silu, bias=self.pkc(bcol))
                store_pa(row0, t, ob_[:, :])
        import os
        cut3 = int(os.environ.get("K_CUT3", "99"))
        if cut3 < 1:
            return
        for (g, row0) in ((G_GATE, 768), (G_GATE + 1, 896), (G_Z, 1024), (G_Z + 1, 1152)):
            w = load_group(g)
            for t in range(NT):
                pt = proj(w, t)
                ob_ = nxt_outb()
                P.act(ob_[:, :], pt[:, 0:TW], AF.Silu)
                store_pa(row0, t, ob_[:, :])
        if cut3 < 2:
            return
        w = load_group(G_AB)
        NEA = A.tile([16, 1], F32, "nea")
        P.act(NEA[:, :], self.pkc(PK_ALOG, 0, 16), AF.Exp)
        P.ts(NEA[:, :], NEA[:, :], -1.0, None, op0=ALU.mult)
        ABF = A.tile([128, TW], F32, "abf")
        P.memset(ABF[:, :], 0.0)
        for t in range(NT):
            pt = proj(w, t)
            e1 = TMP[0]
            P.act(e1[0:16, :], pt[0:16, 0:TW], AF.Exp, bias=self.pkc(PK_DTB, 0, 16))
            P.act(ABF[64:80, :], e1[0:16, :], AF.Ln, bias=self.ONE1[0:16, :])
            P.act(e1[0:16, :], e1[0:16, :], AF.Ln, bias=self.ONE1[0:16, :])
            P.ts(ABF[0:16, :], e1[0:16, :], NEA[:, :], None, op0=ALU.mult)
            P.act(ABF[32:40, :], pt[32:40, 0:TW], AF.Sigmoid)
            for s in range(TW // 128):
                b = t * (TW // 128) + s
                p2 = self.ps.get()
                P.tr(p2[:, 0:80], ABF[0:80, s * 128:(s + 1) * 128], self.cst(C_ID, 0, 80, 0, 80))
                P.cp(ABT[:, b, :], p2[:, 0:80])
        if cut3 < 3:
            return
        w = load_group(G_CKV)
        for t in range(NT):
            pt = proj(w, t)
            sqt = TMP[0]
            P.act(sqt[:, :], pt[:, 0:TW], AF.Square)
            p2 = self.ps.get()
            P.mm(p2[:, 0:TW], self.cst(C_ONES), sqt[:, :])
            rs = TMP[1]
            self.rstd_from(rs[:, :], p2[:, 0:TW], 128.0)
            o = nxt_out()
            P.stt(o[:, :], pt[:, 0:TW], self.pkc(PK_KVN), rs[:, :], ALU.mult, ALU.mult)
            ob = TMP[2]
            obb = sub(ob, ob.t[:, 0:TW // 2].bitcast(BF16))
            P.cp(obb, o[:, :], e="act")
            P.dma(dv(self.ckvd[:, t * TW:(t + 1) * TW]), obb, q="pool")
            if not is_dec:
                for s in range(TW // 128):
                    b = t * (TW // 128) + s
                    p3 = self.ps.get()
                    P.tr(p3[:, 0:128], o[:, s * 128:(s + 1) * 128], self.cst(C_ID))
                    o3 = nxt_out()
                    P.cp(o3[:, 0:128], p3[:, 0:128])
                    P.dma(dv(self.ockv[sq, l, b * 128:(b + 1) * 128, :]), o3[:, 0:128], q="pool")
        if is_dec:
            for s in range(self.past // 128):
                c = TMP[0]
                P.dma(c[:, 0:128], dv(self.cckv[l, s * 128:(s + 1) * 128, :]))
                p3 = self.ps.get()
                P.tr(p3[:, 0:128], c[:, 0:128], self.cst(C_ID))
                ob = TMP[2]
                obb = sub(ob, ob.t[:, 0:64].bitcast(BF16))
                P.cp(obb, p3[:, 0:128])
                P.dma(dv(self.ckvd[:, L + s * 128:L + (s + 1) * 128]), obb, q="pool")
        if cut3 < 4:
            return
        wa = load_group(G_KR, 48)
        wb = load_group(G_KRB, 48)
        ROP = [A.tile([48, 2, TW], F32, "rop") for _ in range(2)]
        for t in range(NT):
            pa_ = proj(wa, t, 48)
            ob = TMP[2]
            obb = sub(ob, ob.t[0:48, 0:TW // 2].bitcast(BF16))
            if is_dec:
                pb_ = proj(wb, t, 48)
                rp = ROP[t % 2]
                P.dma(rp[:, :, :], dv(self.rope[:, :, t * TW:(t + 1) * TW]))
                t1 = TMP[0]
                t2 = TMP[1]
                P.tt(t1[0:48, :], pa_[0:48, 0:TW], rp[:, 0, :], ALU.mult)
                P.tt(t2[0:48, :], pb_[0:48, 0:TW], rp[:, 1, :], ALU.mult)
                P.tt(obb, t1[0:48, :], t2[0:48, :], ALU.add)
            else:
                o = nxt_out()
                P.cp(o[0:48, :], pa_[0:48, 0:TW])
                P.cp(obb, o[0:48, :], e="act")
                for s in range(TW // 128):
                    b = t * (TW // 128) + s
                    p3 = self.ps.get()
                    P.tr(p3[:, 0:48], o[0:48, s * 128:(s + 1) * 128], self.cst(C_ID, 0, 48, 0, 48))
                    o3 = nxt_out()
                    P.cp(o3[:, 0:48], p3[:, 0:48])
                    P.dma(dv(self.okr[sq, l, b * 128:(b + 1) * 128, 0:16]), o3[:, 0:16], q="pool")
                    P.dma(dv(self.okr[sq, l, b * 128:(b + 1) * 128, 16:32]), o3[:, 32:48], q="pool")
            P.dma(dv(self.krd[:, t * TW:(t + 1) * TW]), obb, q="pool")
        skip = os.environ.get('K_SKIP', '').split(',')
        if is_dec and 'ctxkr' not in skip:
            for s in range(self.past // 128):
                c = TMP[0]
                P.dma(c[:, 0:48], dv(self.ckr[l, s * 128:(s + 1) * 128, :]))
                p3 = self.ps.get()
                P.tr(p3[0:64, 0:128], c[:, 0:64], self.cst(C_ID))
                ob = TMP[2]
                obb = sub(ob, ob.t[0:48, 0:64].bitcast(BF16))
                P.cp(obb, p3[0:48, 0:128])
                P.dma(dv(self.krd[:, L + s * 128:L + (s + 1) * 128]), obb, q="pool")
        if cut3 < 5:
            return
        for i in range(2):
            w = load_group(G_CQ + i)
            for t in range(NT):
                pt = proj(w, t)
                P.cp(CQRAW[:, i, t * TW:(t + 1) * TW], pt[:, 0:TW], e=("act" if t % 2 else "dve"))
        WQ = A.tile([128, 2, 1024], BF16, "wq")
        WQB = A.tile([128, 2, 512], BF16, "wqb")
        self.load_w_bf16(WQ, self.wuq[l], 2, 1024)
        self.load_w_bf16(WQB, self.wuqb[l], 2, 512)
        CQN = A.tile([128, 2, TW], BF16, "cqn")
        QO = [A.tile([128, TW], BF16, "qo") for _ in range(2)]
        for t in range(NT):
            p2 = self.ps.get()
            for i in range(2):
                sqt = TMP[i]
                P.act(sqt[:, :], CQRAW[:, i, t * TW:(t + 1) * TW], AF.Square)
                P.mm(p2[:, 0:TW], self.cst(C_ONES), sqt[:, :], start=(i == 0), stop=(i == 1))
            rs = TMP[2]
            self.rstd_from(rs[:, :], p2[:, 0:TW], 256.0)
            for i in range(2):
                P.stt(CQN[:, i, :], CQRAW[:, i, t * TW:(t + 1) * TW], self.pkc(PK_QN + i), rs[:, :], ALU.mult, ALU.mult)
            if is_dec:
                rp = ROP[t % 2]
                P.dma(rp[:, :, :], dv(self.rope[:, :, t * TW:(t + 1) * TW]))
            for h in range(8):
                pa_ = self.ps.get()
                for i in range(2):
                    P.mm(pa_[:, 0:TW], WQ[:, i, h * 128:(h + 1) * 128], CQN[:, i, :], start=(i == 0), stop=(i == 1))
                qo = QO[h % 2]
                P.cp(qo[:, :], pa_[:, 0:TW], e="dve")
                if is_dec:
                    skip = os.environ.get('K_SKIP', '').split(',')
                    if 'qpb' in skip:
                        pb_ = pa_
                    else:
                        pb_ = self.ps.get()
                        for i in range(2):
                            P.mm(pb_[0:48, 0:TW], WQB[:, i, h * 64:h * 64 + 48], CQN[:, i, :], start=(i == 0), stop=(i == 1))
                    t1 = TMP[0]
                    t2 = TMP[1]
                    if 'qtt' not in skip:
                        P.tt(t1[0:48, :], pa_[0:48, 0:TW], rp[:, 0, :], ALU.mult)
                        P.tt(t2[0:48, :], pb_[0:48, 0:TW], rp[:, 1, :], ALU.mult)
                        P.tt(qo[0:48, :], t1[0:48, :], t2[0:48, :], ALU.add)
                P.dma(dv(self.qt[h * 128:(h + 1) * 128, t * TW:(t + 1) * TW]), qo[:, :], q="pool")

    def phase_ffn_up(self, l, L, hT, TW, NT):
        P, A = self.P, self.A
        stg = [A.tile([128, 8, 128], F32, "wstg") for _ in range(2)]
        WG = [A.tile([128, 8, 128], BF16, "wg") for _ in range(2)]
        WU = [A.tile([128, 8, 128], BF16, "wu") for _ in range(2)]
        GB = [A.tile([128, L + 2], BF16, "gb") for _ in range(2)]
        UB = [A.tile([128, L + 2], BF16, "ub") for _ in range(2)]
        for r in GB + UB:
            P.memset(r[:, 0:1], 0.0)
            P.memset(r[:, L + 1:L + 2], 0.0)
        DGG = A.tile([128, 3, 128], BF16, "dgg")
        DGU = A.tile([128, 3, 128], BF16, "dgu")
        SGT = [A.tile([128, TW], F32, "sgt") for _ in range(2)]
        AO = [A.tile([128, TW], BF16, "ao") for _ in range(3)]
        n = 0
        for cg in range(22):
            wg, wu, gb, ub = WG[cg % 2], WU[cg % 2], GB[cg % 2], UB[cg % 2]
            self.load_wg(wg, self.wup[l, cg], stg[0], "dve")
            self.load_wg(wu, self.wup[l, 22 + cg], stg[1], "act")
            for t in range(NT):
                pg = self.ps.get()
                pu = self.ps.get()
                for k in range(8):
                    P.mm(pg[:, 0:TW], wg[:, k, :], hT[t][:, k, :], start=(k == 0), stop=(k == 7))
                for k in range(8):
                    P.mm(pu[:, 0:TW], wu[:, k, :], hT[t][:, k, :], start=(k == 0), stop=(k == 7))
                P.cp(gb[:, 1 + t * TW:1 + (t + 1) * TW], pg[:, 0:TW], e="act")
                P.cp(ub[:, 1 + t * TW:1 + (t + 1) * TW], pu[:, 0:TW], e="dve")
            for j in range(3):
                P.ts(DGG[:, j, :], self.IDB[:, :], self.pkc(PK_FCONV + cg * 3 + j), None, op0=ALU.mult)
                P.ts(DGU[:, j, :], self.IDB[:, :], self.pkc(PK_FCONV + (22 + cg) * 3 + j), None, op0=ALU.mult)
            for t in range(NT):
                pg = self.ps.get()
                pu = self.ps.get()
                for j in range(3):
                    P.mm(pg[:, 0:TW], DGG[:, j, :], gb[:, t * TW + j:t * TW + j + TW], start=(j == 0), stop=(j == 2))
                for j in range(3):
                    P.mm(pu[:, 0:TW], DGU[:, j, :], ub[:, t * TW + j:t * TW + j + TW], start=(j == 0), stop=(j == 2))
                sg = SGT[n % 2]
                ao = AO[n % 3]
                n += 1
                P.act(sg[:, :], pg[:, 0:TW], AF.Silu)
                P.tt(ao[:, :], pu[:, 0:TW], sg[:, :], ALU.mult)
                P.dma(dv(self.actT[cg * 128:(cg + 1) * 128, t * TW:(t + 1) * TW]), ao[:, :], q="pool")

    def phase_attn(self, l, L, is_dec, TW, NT, NK):
        P, A = self.P, self.A
        NKT = NK // 128
        scale = 96.0 ** -0.5
        CKVB = A.tile([128, NK], BF16, "ckvb")
        KRB = A.tile([48, NK], BF16, "krb")
        P.dma(CKVB[:, :], dv(self.ckvd[:, 0:NK]))
        P.dma(KRB[:, :], dv(self.krd[:, 0:NK]))
        WUK = A.tile([128, 1, 1024], BF16, "wuk")
        WUV = A.tile([128, 1, 512], BF16, "wuv")
        self.load_w_bf16(WUK, self.wuk[l], 1, 1024)
        self.load_w_bf16(WUV, self.wuv[l], 1, 512)
        KT = [A.tile([128, NK], BF16, "kt") for _ in range(2)]
        VA = [A.tile([128, NKT, 128], BF16, "va") for _ in range(2)]
        for va in VA:
            P.cp(va[:, :, 64:128], sub(self.ONESB, self.ONESB.t[:, :].unsqueeze(1).to_broadcast([128, NKT, 64])), e="pool")
        QT = [A.tile([128, L], BF16, "qth") for _ in range(2)]
        PT = [A.tile([128, TW], BF16, "pt") for _ in range(8)]
        REC = [A.tile([64, TW], F32, "rec") for _ in range(2)]
        OT = [A.tile([64, TW], BF16, "ot") for _ in range(2)]
        npt = 0
        pend = []
        LAG = 4

        def drain(n):
            while len(pend) > n:
                pend.pop(0)()

        for h in range(8):
            kt, va, qt = KT[h % 2], VA[h % 2], QT[h % 2]
            while pend and pend[0].head <= h - 2:
                pend.pop(0)()
            P.dma(qt[:, :], dv(self.qt[h * 128:(h + 1) * 128, 0:L]))
            for c0 in range(0, NK, 512):
                cw = min(512, NK - c0)
                pt = self.ps.get()
                P.mm(pt[:, 0:cw], WUK[:, 0, h * 128:(h + 1) * 128], CKVB[:, c0:c0 + cw], start=True, stop=False)
                P.mm(pt[:, 0:cw], self.IDB[0:48, :], KRB[:, c0:c0 + cw], start=False, stop=True)
                P.cp(kt[:, c0:c0 + cw], pt[:, 0:cw], e="dve")
            for k0 in range(0, NKT, 8):
                kn = min(8, NKT - k0)
                pt = self.ps.get()
                for kk in range(kn):
                    P.mm(pt[:, kk * 64:(kk + 1) * 64], CKVB[:, (k0 + kk) * 128:(k0 + kk + 1) * 128],
                         WUV[:, 0, h * 64:(h + 1) * 64])
                P.cp(va[:, k0:k0 + kn, 0:64], sub(pt, pt.t[:, 0:kn * 64].rearrange("p (k v) -> p k v", v=64)), e="dve")
            for t in range(NT):
                po = self.ps.reserve()
                for kti in range(NKT):
                    ps_ = self.ps.get()
                    P.mm(ps_[:, 0:TW], kt[:, kti * 128:(kti + 1) * 128], qt[:, t * TW:(t + 1) * TW])
                    pt_ = PT[npt % len(PT)]
                    npt += 1
                    P.act(pt_[:, :], ps_[:, 0:TW], AF.Exp, scale=scale)

                    def pv(po=po, kti=kti, pt_=pt_, va=va, t=t, h=h):
                        P.mm(po[:, 0:TW], va[:, kti, :], pt_[:, :], start=(kti == 0), stop=(kti == NKT - 1))
                        if kti == NKT - 1:
                            rec = REC[(h * NT + t) % 2]
                            P.recip(rec[:, :], po[64:128, 0:TW])
                            ot = OT[(h * NT + t) % 2]
                            P.tt(ot[:, :], po[0:64, 0:TW], rec[:, :], ALU.mult)
                            self.ps.free(po)
                            P.dma(dv(self.catd[256 + h * 64:256 + (h + 1) * 64, t * TW:(t + 1) * TW]), ot[:, :], q="pool")

                    pv.head = h
                    pend.append(pv)
                    drain(LAG)
        drain(0)

    def scan_tiles(self):
        import os
        A = self.A
        f = lambda shape, name: A.tile(shape, F32, name)
        h = lambda shape, name: A.tile(shape, BF16, name)
        mode = os.environ.get("K_DI", "r32")
        self.dI = {"f32": F32, "r32": F32R}.get(mode, BF16)
        self.hl = (mode == "hl")
        i_ = lambda shape, name: A.tile(shape, self.dI, name)
        T = {}
        T["FMS"] = [{k: [h([128, 128], "fm" + k) for _ in range(2)] for k in ("q", "k", "v", "x", "b", "c")} for _ in range(2)]
        T["TM"] = h([128, 4, 256], "tm")
        T["GC"], T["EG"], T["EGL"] = f([128, 8], "gc"), f([128, 8], "eg"), f([128, 8], "egl")
        T["NB"], T["BEG"] = f([128, 4], "nbeta"), f([128, 4], "beg")
        T["LAB"] = [f([128, 128], "lab") for _ in range(8)]
        T["EGRW"] = [f([128, 4, 128], "egrw") for _ in range(2)]
        T["DECT"] = [f([128, 4, 128], "dect") for _ in range(2)]
        T["DECS"] = f([128, 4, 128], "decs")
        T["MP"] = [i_([128, 4, 3 if self.hl else 2, 128], "mp") for _ in range(2)]
        T["AT"] = [i_([128, 4, 128], "at") for _ in range(2)]
        T["R"] = f([128, 4, 128], "r") if (self.hl or self.dI == F32R) else i_([128, 4, 128], "r")
        if self.hl:
            T["PF"] = f([128, 4, 128], "pf")
            T["PHL"] = h([128, 4, 2, 128], "phl")
        kv_ = f if (self.hl or self.dI == F32R) else i_
        T["KBG"], T["VB"] = kv_([128, 4, 64], "kbg"), kv_([128, 4, 64], "vb")
        T["WT"] = h([64, 4, 128], "wt")
        T["U"] = f([64, 2, 4, 64], "u")
        T["ATT"] = h([64, 2, 4, 64], "att")
        T["QDT"] = h([64, 4, 128], "qdt")
        T["KDEC"] = h([64, 2, 4, 64], "kdec")
        T["SCT"] = h([64, 2, 4, 64], "sct")
        T["XDT"] = h([64, 2, 4, 64], "xdt")
        T["BDEC"] = h([64, 2, 4, 128], "bdec")
        T["CDT"] = h([128, 4, 128], "cdt")
        T["VN"] = h([64, 4, 64], "vn")
        T["SG"] = [f([64, 4, 64], "sg") for _ in range(2)]
        T["SGB"] = [h([64, 4, 64], "sgb") for _ in range(2)]
        T["SS"] = [f([128, 4, 64], "ss") for _ in range(2)]
        T["SSB"] = [h([128, 4, 64], "ssb") for _ in range(2)]
        T["STMP"] = f([64, 4, 128], "stmp")
        return T

    def scan_dir(self, d, T, l, sq, L, is_dec, NB, ABT, OGt, OYt, obufs, touched):
        P = self.P
        ID = self.cst(C_ID)
        IDB = self.IDB
        rows = {"q": 0, "k": 256, "v": 512, "x": 1280, "b": 1536, "c": 1792}
        TRI = self.cst(C_TRIF + d)
        TRIS = self.cst(C_TRISF + d)
        b4 = lambda blk: sub(self.CON, self.CON.t[:, blk * 128:(blk + 1) * 128].unsqueeze(1).to_broadcast([128, 4, 128]))
        TRI4, TRIS4 = b4(C_TRIF + d), b4(C_TRISF + d)
        IDB4 = sub(IDB, IDB.t[:, :].unsqueeze(1).to_broadcast([128, 4, 128]))
        ID4 = b4(C_ID)
        FMS, TM = T["FMS"], T["TM"]
        GC, EG, EGL, NB_, BEG, LAB = T["GC"], T["EG"], T["EGL"], T["NB"], T["BEG"], T["LAB"]
        EGRW, DECT, DECS, MP, AT, R = T["EGRW"], T["DECT"], T["DECS"], T["MP"], T["AT"], T["R"]
        KBG, VB, WT, U, ATT, QDT, KDEC = T["KBG"], T["VB"], T["WT"], T["U"], T["ATT"], T["QDT"], T["KDEC"]
        SCT, XDT, BDEC, CDT, VN = T["SCT"], T["XDT"], T["BDEC"], T["CDT"], T["VN"]
        SG, SGB, SS, SSB, STMP = T["SG"], T["SGB"], T["SS"], T["SSB"], T["STMP"]
        h4 = lambda pt, n=128: sub(pt, pt.t[:, 0:4 * n].rearrange("p (h i) -> p h i", h=4))
        f32v = (lambda v: V(v.ap.bitcast(F32), v.buf)) if self.dI == F32R else (lambda v: v)
        sgi, ssi = 0, 0
        st = {"ssi": 0}
        import os
        scut = int(os.environ.get("K_SCUT", "100000"))
        MASKE = os.environ.get('K_MASKE', 'dve')
        self._ny = 0
        if is_dec:
            P.dma(SG[0][:, :, :], dv(self.stg[l, d].rearrange("h k v -> k h v")))
            P.dma(STMP[:, :, :], dv(self.sts[l, d].rearrange("h p n -> p h n")))
            pt = self.ps.get()
            for h in range(4):
                P.tr(pt[:, h * 64:(h + 1) * 64], STMP[:, h, :], self.cst(C_ID, 0, 64, 0, 64))
            P.cp(SS[0][:, :, :], h4(pt, 64))
        else:
            P.memset(SG[0][:, :, :], 0.0)
            P.memset(SS[0][:, :, :], 0.0)
        P.cp(SGB[0][:, :, :], SG[0][:, :, :], e="pool")
        P.cp(SSB[0][:, :, :], SS[0][:, :, :], e="pool")
        self._ny += 1
        if self._ny >= scut:
            return
        yield
        blocks = list(range(NB)) if d == 0 else list(range(NB - 1, -1, -1))

        def load_fm(bi):
            fm = FMS[bi % 2]
            tk = slice(blocks[bi] * 128, (blocks[bi] + 1) * 128)
            for k in fm:
                for g in range(2):
                    P.dma(fm[k][g][:, :], dv(self.pa[rows[k] + g * 128:rows[k] + (g + 1) * 128, tk]))

        load_fm(0)
        for bi, b in enumerate(blocks):
            tok = slice(b * 128, (b + 1) * 128)
            FM = FMS[bi % 2]
            if bi + 1 < len(blocks):
                load_fm(bi + 1)
            skip = os.environ.get('K_SKIP', '')
            for half, keys in enumerate((("k", "v"), ("x", "b"))):
                if 'tm' in skip:
                    break
                pt = self.ps.get()
                ptb = pt.t[:, :].bitcast(BF16)
                for i, k in enumerate(keys):
                    for g in range(2):
                        c0 = i * 256 + g * 128
                        P.tr(sub(pt, ptb[:, c0:c0 + 128]), FM[k][g][:, :], IDB[:, :])
                P.ts(sub(TM, TM.t[:, half * 2:half * 2 + 2, :].rearrange("p a c -> p (a c)")), sub(pt, ptb[:, 0:512]),
                     1.0, None, op0=ALU.mult)
            KT4 = sub(TM, TM.t[:, 0, :].rearrange("p (h v) -> p h v", h=4))
            VT4 = sub(TM, TM.t[:, 1, :].rearrange("p (h v) -> p h v", h=4))
            XT4 = sub(TM, TM.t[:, 2, :].rearrange("p (h v) -> p h v", h=4))
            BT = sub(TM, TM.t[:, 3, :])
            lasel = sub(ABT, ABT.t[:, b, 0:16].rearrange("p (t d h) -> p t d h", t=2, d=2)[:, :, d, :])
            pt = self.ps.get()
            P.mm(sub(pt, pt.t[:, 0:8].rearrange("p (t h) -> p t h", t=2)), TRI, lasel)
            P.mm(sub(pt, pt.t[:, 8:16].rearrange("p (t h) -> p t h", t=2)), TRIS, lasel)
            P.cp(GC[:, :], pt[:, 0:8])
            P.act(EG[:, :], pt[:, 0:8], AF.Exp)
            P.act(EGL[:, :], pt[:, 8:16], AF.Exp)
            beta = ABT[:, b, 32 + d * 4:36 + d * 4]
            dtc = ABT[:, b, 72 + d * 4:76 + d * 4]
            P.ts(NB_[:, :], beta, -1.0, None, op0=ALU.mult)
            P.tt(BEG[:, :], beta, EG[:, 0:4], ALU.mult)
            for ty in range(2):
                if 'lab' in skip:
                    break
                for h in range(4):
                    P.ts(LAB[ty * 4 + h][:, :], self.cst(C_ONES), ABT[:, b, ty * 8 + d * 4 + h:ty * 8 + d * 4 + h + 1],
                         None, op0=ALU.mult)
            self._ny += 1
            if self._ny >= scut:
                return
            yield
            pg = [self.ps.get(), self.ps.get()]
            for ty in range(2):
                for h in range(4):
                    P.mm(pg[ty][:, h * 128:(h + 1) * 128], LAB[ty * 4 + h][:, :], TRI)
            for h in range(4):
                P.ts(DECS[:, h, :], pg[0][:, h * 128:(h + 1) * 128], GC[:, h:h + 1], self.ZERO1[:, :], op0=ALU.subtract, op1=ALU.max)
            P.act(DECS[:, :, :], DECS[:, :, :], AF.Exp, scale=-1.0)
            P.tt(DECS[:, :, :], DECS[:, :, :], TRIS4, ALU.mult, e=MASKE)
            for ty in range(2):
                for h in range(4):
                    P.ts(DECT[ty][:, h, :], pg[ty][:, h * 128:(h + 1) * 128], GC[:, ty * 4 + h:ty * 4 + h + 1], self.ZERO1[:, :],
                         op0=ALU.subtract, op1=ALU.min)
                P.act(EGRW[ty][:, :, :], h4(pg[ty]), AF.Exp, after=[DECT[ty][:, :, :]] + ([DECS[:, :, :]] if ty == 0 else []))
                P.act(DECT[ty][:, :, :], DECT[ty][:, :, :], AF.Exp)
                P.tt(DECT[ty][:, :, :], DECT[ty][:, :, :], TRI4, ALU.mult, e=MASKE)
            self._ny += 1
            if self._ny >= scut:
                return
            yield
            mp, at = MP[0], AT[0]
            pkk = self.ps.get()
            for h in range(4):
                kf = FM["k"][h // 2][(h % 2) * 64:(h % 2) * 64 + 64, :]
                P.mm(pkk[:, h * 128:(h + 1) * 128], kf, kf)
            for h in range(4):
                P.stt(mp[:, h, 0, :], pkk[:, h * 128:(h + 1) * 128], NB_[:, h:h + 1], DECS[:, h, :], ALU.mult, ALU.mult)
            if self.dI == F32R:
                P.cp(mp[:, :, 1, :], ID4)
            else:
                P.cp(mp[:, :, 1, :], IDB4, e="pool")
            if self.hl:
                P.memset(mp[:, :, 2, :], 0.0)
                P.cp(T["PF"][:, :, :], ID4, e="pool")
            pt = self.ps.get()
            if self.dI == BF16:
                ptb = pt.t[:, :].bitcast(BF16)
                for h in range(4):
                    P.tr(sub(pt, ptb[:, h * 128:(h + 1) * 128]), mp[:, h, 0, :], IDB[:, :])
                P.cp(at[:, :, :], sub(pt, ptb[:, 0:512].rearrange("p (h i) -> p h i", h=4)), e="act")
            else:
                for h in range(4):
                    P.tr(pt[:, h * 128:(h + 1) * 128], f32v(mp[:, h, 0, :]), ID)
                P.cp(at[:, :, :], h4(pt), e="act")
            pqk = self.ps.get()
            for h in range(4):
                r0 = (h % 2) * 64
                P.mm(pqk[:, h * 128:(h + 1) * 128], FM["k"][h // 2][r0:r0 + 64, :], FM["q"][h // 2][r0:r0 + 64, :])
            pbc = self.ps.get()
            for gr in range(2):
                P.mm(pbc[:, gr * 128:(gr + 1) * 128], FM["b"][gr][:, :], FM["c"][gr][:, :])
            for c in range(2):
                cs = slice(c * 64, c * 64 + 64)
                P.tt(ATT[:, c, :, :], sub(pqk, pqk.t[cs, :].rearrange("p (h i) -> p h i", h=4)[:, :, cs]),
                     DECT[0][cs, :, cs], ALU.mult)
                for gr in range(2):
                    P.tt(SCT[:, c, gr * 2:gr * 2 + 2, :],
                         sub(pbc, pbc.t[cs, gr * 128 + c * 64:gr * 128 + c * 64 + 64].unsqueeze(1).to_broadcast([64, 2, 64])),
                         DECT[1][cs, gr * 2:gr * 2 + 2, cs], ALU.mult)
            self._ny += 1
            if self._ny >= scut:
                return
            yield
            def ssd_step(c):
                cs = slice(c * 64, c * 64 + 64)
                il = c * 64 + (63 if d == 0 else 0)
                ctok = slice(b * 128 + c * 64, b * 128 + c * 64 + 64)
                ck = b * 2 + c
                first = ("y", ck) not in touched
                touched.add(("y", ck))
                oyv = V(OYt.t[:, :, ctok], obufs[1][ck])
                ssi_ = st["ssi"]
                S, Sn, Sb, Sbn = SS[ssi_], SS[1 - ssi_], SSB[ssi_], SSB[1 - ssi_]
                st["ssi"] = 1 - ssi_
                po = self.ps.get()
                for h in range(4):
                    r0 = (h % 2) * 64
                    oreg = po[r0:r0 + 64, (h // 2) * 64:(h // 2) * 64 + 64]
                    P.mm(oreg, Sb[:, h, :], CDT[:, h, cs], start=True, stop=False)
                    P.mm(oreg, XDT[:, c, h, :], SCT[:, c, h, :], start=False, stop=True)
                pk_ = self.ps.get()
                for h in range(4):
                    P.mm(pk_[:, h * 64:(h + 1) * 64], BDEC[:, c, h, :], XDT[:, c, h, :])
                for h in range(4):
                    P.stt(Sn[:, h, :], S[:, h, :], EGRW[1][:, h, il:il + 1], pk_[:, h * 64:(h + 1) * 64], ALU.mult, ALU.add)
                P.cp(Sbn[:, :, :], Sn[:, :, :], e="pool")
                posrc = sub(po, po.t[:, 0:128].rearrange("p (g i) -> p g i", g=2))
                if first:
                    P.cp(oyv, posrc, e="act")
                else:
                    P.tt(oyv, posrc, oyv, ALU.add)

            corder = (0, 1) if d == 0 else (1, 0)
            cur = 0
            for lev in range(6):
                mp, at = MP[cur], AT[cur]
                mpn, atn = MP[1 - cur], AT[1 - cur]
                lastlev = (lev == 5)
                pM = [self.ps.get(), self.ps.get()]
                for h in range(4):
                    reg0 = (h % 2) * 256
                    if self.hl:
                        P.mm(pM[h // 2][:, reg0:reg0 + 256], at[:, h, :],
                             sub(mp, mp.t[:, h, 0:2, :].rearrange("p a i -> p (a i)")), start=True, stop=False)
                        P.mm(pM[h // 2][:, reg0 + 128:reg0 + 256], at[:, h, :], mp[:, h, 2, :], start=False, stop=True)
                    else:
                        P.mm(pM[h // 2][:, reg0:reg0 + 256], at[:, h, :],
                             sub(mp, mp.t[:, h, :, :].rearrange("p a i -> p (a i)")))
                if not lastlev:
                    pA = self.ps.get()
                    for h in range(4):
                        P.mm(pA[:, h * 128:(h + 1) * 128], mp[:, h, 0, :], at[:, h, :])
                    P.cp(atn[:, :, :], h4(pA), e="act")
                for hp in range(2):
                    src = pM[hp].t[:, :].rearrange("p (h a i) -> p h a i", h=2, a=2)
                    if not lastlev:
                        P.cp(mpn[:, hp * 2:hp * 2 + 2, 0, :], sub(pM[hp], src[:, :, 0, :]), e="act")
                    if self.hl:
                        PF = T["PF"]
                        P.tt(PF[:, hp * 2:hp * 2 + 2, :], sub(pM[hp], src[:, :, 1, :]), PF[:, hp * 2:hp * 2 + 2, :], ALU.add)
                    else:
                        P.tt(mpn[:, hp * 2:hp * 2 + 2, 1, :], sub(pM[hp], src[:, :, 1, :]), f32v(mp[:, hp * 2:hp * 2 + 2, 1, :]), ALU.add)
                if self.hl and not lastlev:
                    hi = mpn[:, :, 1, :]
                    lo = mpn[:, :, 2, :]
                    P.cp(hi, T["PF"][:, :, :], e="pool")
                    P.tt(lo, T["PF"][:, :, :], hi, ALU.subtract)
                cur = 1 - cur
                if lev == 0:
                    P.tt(KBG[:, :, :], KT4, sub(BEG, BEG.t[:, :].unsqueeze(2).to_broadcast([128, 4, 64])), ALU.mult)
                    P.tt(VB[:, :, :], VT4, sub(ABT, beta.ap.unsqueeze(2).to_broadcast([128, 4, 64])), ALU.mult)
                    for h in range(4):
                        r0 = (h % 2) * 64
                        P.tt(QDT[:, h, :], FM["q"][h // 2][r0:r0 + 64, :], EGRW[0][r0:r0 + 64, h, :], ALU.mult,
                             e=("pool" if h % 2 else "dve"))
                if lev == 1:
                    for c in range(2):
                        cs = slice(c * 64, c * 64 + 64)
                        P.tt(KDEC[:, c, :, :], sub(TM, KT4.ap[cs, :, :]),
                             sub(EGL, EGL.t[cs, 0:4].unsqueeze(2).to_broadcast([64, 4, 64])), ALU.mult)
                        P.tt(XDT[:, c, :, :], sub(TM, XT4.ap[cs, :, :]),
                             sub(ABT, dtc.ap[cs, :].unsqueeze(2).to_broadcast([64, 4, 64])), ALU.mult)
                if lev == 2:
                    for c in range(2):
                        cs = slice(c * 64, c * 64 + 64)
                        for h in range(4):
                            gr = h // 2
                            P.ts(BDEC[:, c, h, :], sub(TM, BT.ap[cs, gr * 128:(gr + 1) * 128]), EGL[cs, 4 + h:5 + h], None,
                                 op0=ALU.mult, e=("pool" if h % 2 else "dve"))
                if lev == 3:
                    for h in range(4):
                        P.tt(CDT[:, h, :], FM["c"][h // 2][:, :], EGRW[1][:, h, :], ALU.mult, e=("pool" if h % 2 else "dve"))
                if lev == 4:
                    ssd_step(corder[0])
                if lev == 5:
                    ssd_step(corder[1])
                self._ny += 1
                if self._ny >= scut:
                    return
                yield
            mp = MP[cur]
            pt = self.ps.get()
            if self.hl:
                for h in range(4):
                    P.tr(pt[:, h * 128:(h + 1) * 128], T["PF"][:, h, :], ID)
                P.cp(R[:, :, :], h4(pt), e="act")
            elif self.dI == BF16:
                ptb = pt.t[:, :].bitcast(BF16)
                for h in range(4):
                    P.tr(sub(pt, ptb[:, h * 128:(h + 1) * 128]), mp[:, h, 1, :], IDB[:, :])
                P.cp(R[:, :, :], sub(pt, ptb[:, 0:512].rearrange("p (h i) -> p h i", h=4)), e="act")
            else:
                for h in range(4):
                    P.tr(pt[:, h * 128:(h + 1) * 128], f32v(mp[:, h, 1, :]), ID)
                P.cp(R[:, :, :], h4(pt), e="act")
            self._ny += 1
            if self._ny >= scut:
                return
            yield
            pt = self.ps.get()
            for h in range(4):
                if False:
                    pass
                else:
                    P.mm(pt[0:64, h * 128:(h + 1) * 128], KBG[:, h, :], R[:, h, :])
            P.cp(WT[:, :, :], sub(pt, pt.t[0:64, :].rearrange("p (h i) -> p h i", h=4)), e="act")
            pt = self.ps.get()
            for c in range(2):
                cs = slice(c * 64, c * 64 + 64)
                for h in range(4):
                    oreg = pt[0:64, (c * 4 + h) * 64:(c * 4 + h + 1) * 64]
                    if False:
                        pass
                    else:
                        P.mm(oreg, R[cs, h, cs], VB[cs, h, :])
            P.cp(U[:, :, :, :], sub(pt, pt.t[0:64, :].rearrange("p (c h v) -> p c h v", c=2, h=4)))
            self._ny += 1
            if self._ny >= scut:
                return
            yield
            for c in ((0, 1) if d == 0 else (1, 0)):
                cs = slice(c * 64, c * 64 + 64)
                il = c * 64 + (63 if d == 0 else 0)
                ctok = slice(b * 128 + c * 64, b * 128 + c * 64 + 64)
                ck = b * 2 + c
                first = ck not in touched
                touched.add(ck)
                ogv = V(OGt.t[:, :, ctok], obufs[0][ck])
                oyv = V(OYt.t[:, :, ctok], obufs[1][ck])
                S, Sn, Sb, Sbn = SG[sgi], SG[1 - sgi], SGB[sgi], SGB[1 - sgi]
                sgi = 1 - sgi
                pw = self.ps.get()
                for h in range(4):
                    P.mm(pw[0:64, h * 64:(h + 1) * 64], WT[:, h, cs], Sb[:, h, :])
                P.tt(VN[:, :, :], U[:, c, :, :], sub(pw, pw.t[0:64, 0:256].rearrange("p (h v) -> p h v", h=4)), ALU.subtract)
                self._ny += 1
                if self._ny >= scut:
                    return
                yield
                pk_ = self.ps.get()
                for h in range(4):
                    P.mm(pk_[0:64, h * 64:(h + 1) * 64], KDEC[:, c, h, :], VN[:, h, :])
                po = self.ps.get()
                for h in range(4):
                    r0 = (h % 2) * 64
                    oreg = po[r0:r0 + 64, (h // 2) * 64:(h // 2) * 64 + 64]
                    P.mm(oreg, Sb[:, h, :], QDT[:, h, cs], start=True, stop=False)
                    P.mm(oreg, VN[:, h, :], ATT[:, c, h, :], start=False, stop=True)
                for h in range(4):
                    P.stt(Sn[:, h, :], S[:, h, :], EGRW[0][0:64, h, il:il + 1], pk_[0:64, h * 64:(h + 1) * 64], ALU.mult, ALU.add)
                P.cp(Sbn[:, :, :], Sn[:, :, :], e="pool")
                posrc = sub(po, po.t[:, 0:128].rearrange("p (g i) -> p g i", g=2))
                if first:
                    P.cp(ogv, posrc, e="act")
                else:
                    P.tt(ogv, posrc, ogv, ALU.add)
                self._ny += 1
                if self._ny >= scut:
                    return
                yield
        if not is_dec:
            P.dma(dv(self.osg[sq, l, d].rearrange("h k v -> k h v")), SG[sgi][:, :, :], q="pool")
            pt = self.ps.get()
            for h in range(4):
                P.tr(pt[0:64, h * 128:(h + 1) * 128], SS[st["ssi"]][:, h, :], ID)
            P.cp(STMP[:, :, :], sub(pt, pt.t[0:64, :].rearrange("p (h n) -> p h n", h=4)))
            P.dma(dv(self.oss[sq, l, d].rearrange("h p n -> p h n")), STMP[:, :, :], q="pool")

    def phase_scan(self, l, sq, L, is_dec, TW, NT, NB, ABT, OG, OY):
        P, A = self.P, self.A
        m_sc = A.mark()
        obufs = [[Buf() for _ in range(2 * NB)] for _ in range(2)]
        touched = set()
        if is_dec:
            tsets = [self.scan_tiles(), self.scan_tiles()]
            groups = [[0, 1]]
        else:
            ts1 = self.scan_tiles()
            tsets = [ts1, ts1]
            groups = [[0], [1]]
        for grp in groups:
            gens = [self.scan_dir(d, tsets[d], l, sq, L, is_dec, NB, ABT, OG, OY, obufs, touched) for d in grp]
            while gens:
                for g in list(gens):
                    try:
                        next(g)
                    except StopIteration:
                        gens.remove(g)
        self.barrier()
        A.release(m_sc)
        T_ = lambda shape, name: A.tile(shape, F32, name)
        GT = [A.tile([128, 2, TW], BF16, "gt") for _ in range(3)]
        TA = T_([128, 2, TW], "ta")
        TB = T_([128, 2, TW], "tb")
        RS = T_([128, TW], "rs")
        OB = [A.tile([128, 2, TW], BF16, "ob") for _ in range(2)]
        import os
        for t in range(NT):
            if 'epi' in os.environ.get('K_SKIP', ''):
                break
            ts_ = slice(t * TW, (t + 1) * TW)
            g = GT[0]
            P.dma(g[:, :, :], dv(self.pa[768:1024, ts_].rearrange("(g p) t -> p g t", p=128)))
            P.tt(TA[:, :, :], OG[:, :, ts_], OG[:, :, ts_], ALU.mult)
            ob = OB[0]
            for gi in range(2):
                pt = self.ps.get()
                P.mm(pt[:, 0:TW], self.cst(C_ONESBD), TA[:, gi, :])
                self.rstd_from(RS[:, :], pt[:, 0:TW], 64.0)
                P.stt(TB[:, gi, :], OG[:, gi, ts_], self.pkc(PK_GDNN), RS[:, :], ALU.mult, ALU.mult)
            P.tt(ob[:, :, :], TB[:, :, :], g[:, :, :], ALU.mult)
            P.dma(dv(self.catd[0:256, ts_].rearrange("(g p) t -> p g t", p=128)), ob[:, :, :], q="pool")
            z = GT[1]
            P.dma(z[:, :, :], dv(self.pa[1024:1280, ts_].rearrange("(g p) t -> p g t", p=128)))
            xs = GT[2]
            P.dma(xs[:, :, :], dv(self.pa[1280:1536, ts_].rearrange("(g p) t -> p g t", p=128)))
            for gi in range(2):
                P.stt(TA[:, gi, :], xs[:, gi, :], self.pkc(PK_SSDD + gi), OY[:, gi, ts_], ALU.mult, ALU.add)
            P.tt(TA[:, :, :], TA[:, :, :], z[:, :, :], ALU.mult)
            P.tt(TB[:, :, :], TA[:, :, :], TA[:, :, :], ALU.mult)
            pt = self.ps.get()
            for gi in range(2):
                P.mm(pt[:, 0:TW], self.cst(C_ONES), TB[:, gi, :], start=(gi == 0), stop=(gi == 1))
            self.rstd_from(RS[:, :], pt[:, 0:TW], 256.0)
            ob = OB[1]
            for gi in range(2):
                P.stt(ob[:, gi, :], TA[:, gi, :], self.pkc(PK_SSDN + gi), RS[:, :], ALU.mult, ALU.mult)
            P.dma(dv(self.catd[768:1024, ts_].rearrange("(g p) t -> p g t", p=128)), ob[:, :, :], q="pool")


def _consts():
    C = np.zeros((128, NCONST * 128), np.float32)
    idx = np.arange(128)
    sc = (idx[:, None] // 64) == (idx[None, :] // 64)
    t = idx[:, None]
    i = idx[None, :]

    def put(blk, m):
        C[:, blk * 128:(blk + 1) * 128] = m.astype(np.float32)

    put(C_ID, np.eye(128))
    for d in range(2):
        before = (t < i) if d == 0 else (t > i)
        after = (t > i) if d == 0 else (t < i)
        tri = sc & (before | (t == i))
        put(C_TRIF + d, tri)
        put(C_TRISF + d, sc & after)
        put(C_NEGTF + d, np.where(tri, 0.0, -30000.0))
        put(C_POSSF + d, np.where(sc & after, 0.0, 30000.0))
    put(C_ONESBD, sc)
    put(C_ONES, np.ones((128, 128)))
    return C


def _rope_table(L):
    tt = np.arange(L)
    r = (tt // GRID_W).astype(np.float32)
    col = (tt % GRID_W).astype(np.float32)
    nf = 8
    inv = (np.float32(10000.0) ** (-np.arange(nf, dtype=np.float32) / np.float32(nf))).astype(np.float32)
    ang = np.concatenate([r[:, None] * inv, col[:, None] * inv], axis=-1).astype(np.float32)
    cos, sin = np.cos(ang).astype(np.float32), np.sin(ang).astype(np.float32)
    R = np.zeros((48, 2, L), np.float32)
    R[0:16, 0] = cos.T
    R[32:48, 0] = cos.T
    R[0:16, 1] = -sin.T
    R[32:48, 1] = sin.T
    return R


def _fm(v, n):
    return np.ascontiguousarray(np.asarray(v, np.float32).reshape(n, 128).T)


def _prep_shared(inp, depth, L):
    f = lambda k: np.asarray(inp[k], np.float32)
    w_in = f("w_in")
    win = np.zeros((depth, D, NG_IN * 128), np.float32)
    win[:, :, 0:768] = w_in[:, :, 0:768]
    win[:, :, 768:1024] = w_in[:, :, 768:1024]
    win[:, :, G_CQ * 128:G_CQ * 128 + 256] = w_in[:, :, 1040:1296]
    win[:, :, G_CKV * 128:G_CKV * 128 + 128] = w_in[:, :, 1296:1424]
    win[:, :, G_KR * 128 + 0:G_KR * 128 + 16] = w_in[:, :, 1424:1440]
    win[:, :, G_KR * 128 + 32:G_KR * 128 + 48] = w_in[:, :, 1440:1456]
    win[:, :, G_KRB * 128 + 0:G_KRB * 128 + 16] = w_in[:, :, 1440:1456]
    win[:, :, G_KRB * 128 + 32:G_KRB * 128 + 48] = w_in[:, :, 1424:1440]
    win[:, :, G_AB * 128 + 0:G_AB * 128 + 8] = w_in[:, :, 1024:1032]
    win[:, :, G_AB * 128 + 8:G_AB * 128 + 16] = w_in[:, :, 2480:2488]
    win[:, :, G_AB * 128 + 32:G_AB * 128 + 40] = w_in[:, :, 1032:1040]
    win[:, :, G_Z * 128:G_Z * 128 + 256] = w_in[:, :, 1456:1712]
    win[:, :, G_XS * 128:G_XS * 128 + 768] = w_in[:, :, 1712:2480]
    pk = np.zeros((depth, 128, NPK), np.float32)
    p = np.arange(128)
    for l in range(depth):
        pk[l, :, PK_NMP:PK_NMP + 8] = _fm(f("norm_mix_pre")[l], 8)
        pk[l, :, PK_NMO:PK_NMO + 8] = _fm(f("norm_mix_post")[l], 8)
        pk[l, :, PK_NFP:PK_NFP + 8] = _fm(f("norm_ffn_pre")[l], 8)
        pk[l, :, PK_NFO:PK_NFO + 8] = _fm(f("norm_ffn_post")[l], 8)
        pk[l, :, PK_BADA:PK_BADA + 48] = _fm(f("b_ada")[l], 48)
        for i in range(6):
            pk[l, :, PK_GCONV + i * 5:PK_GCONV + i * 5 + 5] = f("gdn_conv")[l][:, i * 128:(i + 1) * 128].T
            pk[l, :, PK_SCONV + i * 5:PK_SCONV + i * 5 + 5] = f("ssd_conv")[l][:, i * 128:(i + 1) * 128].T
            pk[l, :, PK_SCB + i] = f("ssd_conv_b")[l][i * 128:(i + 1) * 128]
        for cg in range(44):
            pk[l, :, PK_FCONV + cg * 3:PK_FCONV + cg * 3 + 3] = f("ffn_conv")[l][:, cg * 128:(cg + 1) * 128].T
        pk[l, :, PK_QN:PK_QN + 2] = _fm(f("mla_q_norm")[l], 2)
        pk[l, :, PK_KVN] = f("mla_kv_norm")[l]
        pk[l, :, PK_GDNN] = f("gdn_norm")[l][p % 64]
        pk[l, :, PK_SSDN:PK_SSDN + 2] = _fm(f("ssd_norm")[l], 2)
        for gi in range(2):
            pk[l, :, PK_SSDD + gi] = f("ssd_d")[l][2 * gi + p // 64]
        pk[l, 0:8, PK_ALOG] = f("gdn_a_log")[l].reshape(8)
        pk[l, 8:16, PK_ALOG] = f("ssd_a_log")[l].reshape(8)
        pk[l, 0:8, PK_DTB] = f("gdn_dt_bias")[l].reshape(8)
        pk[l, 8:16, PK_DTB] = f("ssd_dt_bias")[l].reshape(8)
    w_uq = f("mla_w_uq")
    wuq = np.zeros((depth, 256, 8 * 128), np.float32)
    wuqb = np.zeros((depth, 256, 8 * 64), np.float32)
    w_ukv = f("mla_w_ukv")
    wuk = np.zeros((depth, 128, 8 * 128), np.float32)
    wuv = np.zeros((depth, 128, 8 * 64), np.float32)
    for h in range(8):
        x1 = w_uq[:, :, h * 96 + 64:h * 96 + 80]
        x2 = w_uq[:, :, h * 96 + 80:h * 96 + 96]
        wuq[:, :, h * 128 + 0:h * 128 + 16] = x1
        wuq[:, :, h * 128 + 32:h * 128 + 48] = x2
        wuq[:, :, h * 128 + 64:h * 128 + 128] = w_uq[:, :, h * 96:h * 96 + 64]
        wuqb[:, :, h * 64 + 0:h * 64 + 16] = x2
        wuqb[:, :, h * 64 + 32:h * 64 + 48] = x1
        wuk[:, :, h * 128 + 64:h * 128 + 128] = w_ukv[:, :, h * 128:h * 128 + 64]
        wuv[:, :, h * 64:(h + 1) * 64] = w_ukv[:, :, h * 128 + 64:h * 128 + 128]
    def grp(w):
        dd, _, gc = w.shape
        g = gc // 128
        return np.ascontiguousarray(w.reshape(dd, 8, 128, g, 128).transpose(0, 3, 2, 1, 4)).reshape(dd, g, 128, 1024)
    return {"wada": grp(f("w_ada")), "win": grp(win), "pk": pk, "wuq": wuq, "wuqb": wuqb, "wuk": wuk, "wuv": wuv,
            "wout": f("w_out"), "wup": grp(f("ffn_w_up")), "wdn": f("ffn_w_down"), "consts": _consts(),
            "rope": _rope_table(L)}


def _in_maps(inp, ncores, depth, npr, L):
    sh = _prep_shared(inp, depth, L)
    f = lambda k: np.asarray(inp[k], np.float32)
    ndec = f("x_sample").shape[0]
    ckr = f("cache_mla_krope")
    ckr48 = np.zeros(ckr.shape[:-1] + (48,), np.float32)
    ckr48[..., 0:16] = ckr[..., 0:16]
    ckr48[..., 32:48] = ckr[..., 16:32]
    maps = []
    for c in range(ncores):
        b = c % ndec
        m = dict(sh)
        m["xp"] = np.ascontiguousarray(f("x_prompt")[c * npr:(c + 1) * npr])
        m["xs"] = np.ascontiguousarray(f("x_sample")[b])
        m["cckv"] = np.ascontiguousarray(f("cache_mla_ckv")[b])
        m["ckr"] = np.ascontiguousarray(ckr48[b])
        m["stg"] = np.ascontiguousarray(f("state_gdn")[b])
        m["sts"] = np.ascontiguousarray(f("state_ssd")[b])
        ct = np.zeros((128, 8, 2), np.float32)
        ct[:, :, 0] = f("c")[b].reshape(8, 128).T
        ct[:, :, 1] = f("c_ctx").reshape(8, 128).T
        m["condT"] = ct
        maps.append(m)
    return maps


def run(inp, ncores=NCORES, depth=DEPTH, seq=SEQ, dec_seq=DEC_SEQ, past=PAST, npr=NPR, debug=None, stop=99, only=None):
    bld = Builder(depth=depth, seq=seq, dec_seq=dec_seq, past=past, npr=npr, debug=debug, stop=stop, only=only)
    maps = _in_maps(inp, ncores, depth, npr, dec_seq)
    res = run_bass_kernel_spmd(bld.nc, maps, core_ids=list(range(ncores)))
    return res.results, bld


def kernel(**inputs):
    res, _ = run(inputs)
    ndec = DEC_BATCH
    y_prompt = np.concatenate([res[c]["yp"] for c in range(NCORES)], axis=0).astype(np.float32)
    y_sample = np.stack([res[b]["ys"] for b in range(ndec)], axis=0).astype(np.float32)
    ckv = np.concatenate([res[c]["ockv"] for c in range(NCORES)], axis=0).astype(np.float32)
    kr = np.concatenate([res[c]["okr"] for c in range(NCORES)], axis=0).astype(np.float32)
    sg = np.concatenate([res[c]["osg"] for c in range(NCORES)], axis=0).astype(np.float32)
    ss = np.concatenate([res[c]["oss"] for c in range(NCORES)], axis=0).astype(np.float32)
    return (y_prompt, y_sample, ckv, kr, sg, ss)
```

```python
import threading
import numpy as np
import concourse.bass as bass
import concourse.mybir as mybir
from concourse.bass_utils import run_bass_kernel_spmd

F32 = mybir.dt.float32
BF16 = mybir.dt.bfloat16
F32R = mybir.dt.float32r
AF = mybir.ActivationFunctionType
ALU = mybir.AluOpType

D = 1024
DEPTH = 2
BATCH = 16
SEQ = 256
DEC_BATCH = 4
DEC_SEQ = 4096
PAST = 256
GRID_W = 64
EPS = 1e-6
DFF = 2816
NCORES = 8
NPR = BATCH // NCORES
NG_IN = 23
G_Q, G_K, G_V, G_GATE, G_CQ, G_CKV, G_KR, G_KRB, G_AB, G_Z, G_XS, G_BM, G_CM = 0, 2, 4, 6, 8, 10, 11, 12, 13, 14, 16, 18, 20
C_ID, C_TRIF, C_TRIB, C_TRISF, C_TRISB, C_NEGTF, C_NEGTB, C_POSSF, C_POSSB, C_ONESBD, C_ONES = range(11)
NCONST = 11
PK_NMP, PK_NMO, PK_NFP, PK_NFO, PK_BADA = 0, 8, 16, 24, 32
PK_GCONV = 80
PK_SCONV = 110
PK_SCB = 140
PK_FCONV = 146
PK_QN = 278
PK_KVN = 280
PK_GDNN = 281
PK_SSDN = 282
PK_SSDD = 284
PK_ALOG = 286
PK_DTB = 287
NPK = 288


class Buf:
    __slots__ = ("w", "r", "excl")

    def __init__(self, excl=False):
        self.w = None
        self.r = {}
        self.excl = excl


class V:
    __slots__ = ("ap", "buf")

    def __init__(self, ap, buf):
        self.ap = ap
        self.buf = buf

    def bitcast(self, dt):
        return V(self.ap.bitcast(dt), self.buf)

    def bc(self, shape):
        return V(self.ap.to_broadcast(shape), self.buf)


class Tl:
    def __init__(self, t, buf=None):
        self.t = t
        self.buf = buf if buf is not None else Buf()

    def __getitem__(self, idx):
        return V(self.t[idx], self.buf)


class Prog:
    CE = ("pe", "dve", "act", "pool")

    def __init__(self, nc):
        self.nc = nc
        self.eng = {"pe": nc.tensor, "dve": nc.vector, "act": nc.scalar, "pool": nc.gpsimd, "sp": nc.sync}
        self.sem = {}
        self.cnt = {}
        self.sid = 0
        for e in self.CE:
            self.sem[e] = self._newsem(e)
        self.dsem = {"sp": [self._newsem("dsp%d" % i) for i in range(20)],
                     "pool": [self._newsem("dpl%d" % i) for i in range(8)]}
        self.drr = {"sp": 0, "pool": 0}
        self.waited = {e: {} for e in self.eng}
        self.ninst = 0
        self.nwait = 0
        self.on_barrier = None
        self.on_op = None
        self.tok = None

    def _newsem(self, name):
        h = self.nc.alloc_semaphore(name)
        s = (self.sid, h)
        self.cnt[self.sid] = 0
        self.sid += 1
        return s

    def _wait(self, e, deps):
        best = {}
        for (s, v) in deps:
            if best.get(s, (None, 0))[1] < v:
                best[s] = (s, v)
        for s, v in best.values():
            if self.waited[e].get(s[0], 0) < v:
                self.eng[e].wait_ge(s[1], v)
                self.waited[e][s[0]] = v
                self.nwait += 1

    def _deps(self, e, reads, writes, pe_acc=False):
        deps = []
        for b in reads:
            if b.w is not None:
                deps.append(b.w)
            if b.excl and e in self.sem:
                me = self.sem[e][0]
                for sid, tok in b.r.items():
                    if sid != me:
                        deps.append(tok)
        for b in writes:
            if b.w is not None:
                if not (pe_acc and b.w[0] is self.sem["pe"]):
                    deps.append(b.w)
            for s, v in b.r.values():
                deps.append((s, v))
        return deps

    def _commit(self, tok, reads, writes):
        if self.tok is not None:
            self.tok[tok[0][0]] = tok
        for b in reads:
            b.r[tok[0][0]] = tok
        for b in writes:
            b.w = tok
            b.r = {}

    def op(self, e, fn, reads, writes, pe_acc=False):
        reads = [v.buf for v in reads if v is not None]
        writes = [v.buf for v in writes if v is not None]
        self._wait(e, self._deps(e, reads, writes, pe_acc))
        inst = fn(self.eng[e])
        s = self.sem[e]
        self.cnt[s[0]] += 1
        inst.then_inc(s[1], 1)
        self.ninst += 1
        self._commit((s, self.cnt[s[0]]), reads, writes)
        if self.on_op is not None:
            self.on_op()

    def dma(self, out, in_, q="sp", slow=False):
        reads = [in_.buf]
        writes = [out.buf]
        sl = self.dsem[q]
        s = sl[self.drr[q] % len(sl)]
        self.drr[q] += 1
        deps = self._deps(q, reads, writes)
        if self.cnt[s[0]] > 0:
            deps.append((s, self.cnt[s[0]]))
        self._wait(q, deps)
        kw = {"allow_slow_non_contiguous": True} if slow else {}
        inst = self.eng[q].dma_start(out=out.ap, in_=in_.ap, **kw)
        self.cnt[s[0]] += 16
        inst.then_inc(s[1], 16)
        self.ninst += 1
        self._commit((s, self.cnt[s[0]]), reads, writes)
        if self.on_op is not None:
            self.on_op()

    def barrier(self, local=False):
        if local and self.tok is not None:
            deps = list(self.tok.values())
        else:
            allsems = [self.sem[e] for e in self.CE] + self.dsem["sp"] + self.dsem["pool"]
            deps = [(s, self.cnt[s[0]]) for s in allsems if self.cnt[s[0]] > 0]
        for e in self.eng:
            self._wait(e, deps)
        if self.on_barrier is not None:
            self.on_barrier(local)

    def mm(self, out, lhsT, rhs, start=True, stop=True):
        self.op("pe", lambda E: E.matmul(out.ap, lhsT=lhsT.ap, rhs=rhs.ap, start=start, stop=stop),
                [lhsT, rhs], [out], pe_acc=not start)

    def tr(self, out, in_, ident):
        self.op("pe", lambda E: E.transpose(out.ap, in_.ap, ident.ap), [in_, ident], [out])

    def act(self, out, in_, func, bias=None, scale=None, accum=None, after=()):
        kw = {}
        rd = [in_] + list(after)
        if bias is not None:
            if isinstance(bias, V):
                kw["bias"] = bias.ap
                rd.append(bias)
            else:
                kw["bias"] = float(bias)
        if scale is not None:
            if isinstance(scale, V):
                kw["scale"] = scale.ap
                rd.append(scale)
            else:
                kw["scale"] = float(scale)
        wr = [out]
        if accum is not None:
            kw["accum_out"] = accum.ap
            wr.append(accum)
        self.op("act", lambda E: E.activation(out=out.ap, in_=in_.ap, func=func, **kw), rd, wr)

    def tt(self, out, in0, in1, op, e="dve"):
        self.op(e, lambda E: E.tensor_tensor(out=out.ap, in0=in0.ap, in1=in1.ap, op=op), [in0, in1], [out])

    def ts(self, out, in0, s1, s2=None, op0=ALU.mult, op1=None, e="dve"):
        rd = [in0]
        a1 = s1
        if isinstance(s1, V):
            a1 = s1.ap
            rd.append(s1)
        a2 = s2
        if isinstance(s2, V):
            a2 = s2.ap
            rd.append(s2)
        kw = {}
        if op1 is not None:
            kw["op1"] = op1
        self.op(e, lambda E: E.tensor_scalar(out=out.ap, in0=in0.ap, scalar1=a1, scalar2=a2, op0=op0, **kw),
                rd, [out])

    def stt(self, out, in0, sc, in1, op0, op1):
        rd = [in0, in1]
        a = sc
        if isinstance(sc, V):
            a = sc.ap
            rd.append(sc)
        self.op("dve", lambda E: E.scalar_tensor_tensor(out=out.ap, in0=in0.ap, scalar=a, in1=in1.ap,
                                                       op0=op0, op1=op1), rd, [out])

    def cp(self, out, in_, e="dve"):
        if e == "act":
            self.op("act", lambda E: E.copy(out=out.ap, in_=in_.ap), [in_], [out])
        else:
            self.op(e, lambda E: E.tensor_copy(out=out.ap, in_=in_.ap), [in_], [out])

    def memset(self, v, val, e="pool"):
        self.op(e, lambda E: E.memset(v.ap, val), [], [v])

    def recip(self, out, in_):
        self.op("dve", lambda E: E.reciprocal(out=out.ap, in_=in_.ap), [in_], [out])


class Arena:
    def __init__(self, nc, lo, hi):
        self.nc = nc
        self.lo = lo
        self.hi = hi
        self.p = lo
        self.n = 0
        self.peak = lo
        self.reg = []

    def mark(self):
        return self.p

    def release(self, m):
        self.p = m

    def tile_at(self, off, shape, dt, name="t"):
        self.n += 1
        t = self.nc.alloc_sbuf_tensor_at("%s_%d" % (name, self.n), list(shape), dt, offset=off)
        tl = Tl(t)
        nb = self.nbytes(shape, dt)
        keep = []
        for (o, n, old) in self.reg:
            if o < off + nb and off < o + n:
                toks = list(old.buf.r.values())
                if old.buf.w is not None:
                    toks.append(old.buf.w)
                for tok in toks:
                    sid = tok[0][0]
                    if tl.buf.r.get(sid, (None, 0))[1] < tok[1]:
                        tl.buf.r[sid] = tok
                if not (off <= o and o + n <= off + nb):
                    keep.append((o, n, old))
            else:
                keep.append((o, n, old))
        keep.append((off, nb, tl))
        self.reg = keep
        return tl

    @staticmethod
    def nbytes(shape, dt):
        n = 1
        for s in shape[1:]:
            n *= s
        return (n * (2 if dt == BF16 else 4) + 31) // 32 * 32

    def tile(self, shape, dt, name="t"):
        nb = self.nbytes(shape, dt)
        off = self.p
        assert off + nb <= self.hi, "SBUF arena overflow %s %d+%d>%d" % (name, off, nb, self.hi)
        self.p += nb
        self.peak = max(self.peak, self.p)
        return self.tile_at(off, shape, dt, name)


class PsumPool:
    def __init__(self, nc=None, banks=None):
        if banks is None:
            banks = [Tl(nc.alloc_psum_tensor("psb%d" % i, [128, 512], F32), Buf(excl=True)) for i in range(8)]
        self.banks = banks
        self.res = set()
        self.i = 0

    def get(self):
        while True:
            k = self.i % len(self.banks)
            self.i += 1
            if k not in self.res:
                return self.banks[k]

    def reserve(self):
        b = self.get()
        self.res.add(self.banks.index(b))
        return b

    def free(self, b):
        self.res.discard(self.banks.index(b))


def dv(ap):
    return V(ap, Buf())


def sub(tl, ap):
    return V(ap, tl.buf)


class Ctx:
    pass


CTX_NAMES = ("A", "ps", "GWT", "GWTMP", "gw_key", "pa", "qt", "ckvd", "krd", "catd", "actT",
             "_oi", "_wi", "_obi", "_ny", "dI", "hl")


class Coop:
    def __init__(self):
        self.evs = []
        self.alive = []
        self.cur = 0
        self.exc = None
        self.on_resume = None

    def run(self, fns):
        n = len(fns)
        self.evs = [threading.Event() for _ in range(n)]
        self.alive = [True] * n
        done = threading.Event()

        def wrap(i, fn):
            self.evs[i].wait()
            try:
                fn()
            except BaseException as e:
                self.exc = e
            self.alive[i] = False
            nxt = self._next(i)
            if nxt is None:
                done.set()
            else:
                self.cur = nxt
                self.evs[nxt].set()

        ths = [threading.Thread(target=wrap, args=(i, f)) for i, f in enumerate(fns)]
        for t in ths:
            t.start()
        self.cur = 0
        self.evs[0].set()
        done.wait()
        for t in ths:
            t.join()
        self.evs = []
        if self.exc is not None:
            raise self.exc

    def _next(self, i):
        n = len(self.alive)
        for k in range(1, n + 1):
            j = (i + k) % n
            if self.alive[j] and j != i:
                return j
        return None

    def switch(self):
        if not self.evs:
            return
        i = self.cur
        nxt = self._next(i)
        if nxt is None:
            return
        self.evs[i].clear()
        self.cur = nxt
        self.evs[nxt].set()
        self.evs[i].wait()
        if self.on_resume is not None:
            self.on_resume()


class Builder:
    def __getattr__(self, name):
        if name in CTX_NAMES:
            return getattr(self.__dict__["_tls"].ctx, name)
        raise AttributeError(name)

    def __setattr__(self, name, val):
        if name in CTX_NAMES:
            setattr(self.__dict__["_tls"].ctx, name, val)
        else:
            self.__dict__[name] = val

    def use_ctx(self, ctx):
        self._tls.ctx = ctx
        self.P.tok = ctx.tokens

    def barrier(self):
        ctx = self._tls.ctx
        self.P.tok = ctx.tokens
        if self.coop.evs:
            self.P.barrier(local=True)
        else:
            self.P.barrier()

    def cswitch(self):
        self.coop.switch()

    def _tick(self):
        self._tick_n += 1
        if self._tick_n % 24 == 0:
            self.coop.switch()

    def __init__(self, depth=DEPTH, seq=SEQ, dec_seq=DEC_SEQ, past=PAST, npr=NPR, debug=None, stop=99, only=None):
        self.depth, self.seq, self.dec_seq, self.past, self.npr = depth, seq, dec_seq, past, npr
        self.stop, self.only = stop, only
        nc = bass.Bass("TRN2", target_bir_lowering=False)
        self.nc = nc
        self.P = Prog(nc)
        dt = nc.dram_tensor
        L, S = dec_seq, seq
        I = lambda name, shape, d=F32: dt(name, list(shape), d, kind="ExternalInput").ap()
        O = lambda name, shape, d=F32: dt(name, list(shape), d, kind="ExternalOutput").ap()
        X = lambda name, shape, d=F32: dt(name, list(shape), d, kind="Internal").ap()
        self.xp = I("xp", [npr, S, D])
        self.xs = I("xs", [L, D])
        self.cckv = I("cckv", [depth, past, 128])
        self.ckr = I("ckr", [depth, past, 48])
        self.stg = I("stg", [depth, 2, 4, 64, 64])
        self.sts = I("sts", [depth, 2, 4, 64, 128])
        self.condT = I("condT", [128, 8, 2])
        self.wada = I("wada", [depth, 48, 128, 8 * 128])
        self.win = I("win", [depth, NG_IN, 128, 8 * 128])
        self.pk = I("pk", [depth, 128, NPK])
        self.wuq = I("wuq", [depth, 256, 8 * 128])
        self.wuqb = I("wuqb", [depth, 256, 8 * 64])
        self.wuk = I("wuk", [depth, 128, 8 * 128])
        self.wuv = I("wuv", [depth, 128, 8 * 64])
        self.wout = I("wout", [depth, D, D])
        self.wup = I("wup", [depth, 44, 128, 8 * 128])
        self.wdn = I("wdn", [depth, DFF, D])
        self.consts = I("consts", [128, NCONST * 128])
        self.rope = I("rope", [48, 2, L])
        self.yp = O("yp", [npr, S, D])
        self.ys = O("ys", [L, D])
        self.ockv = O("ockv", [npr, depth, S, 128])
        self.okr = O("okr", [npr, depth, S, 32])
        self.osg = O("osg", [npr, depth, 2, 4, 64, 64])
        self.oss = O("oss", [npr, depth, 2, 4, 64, 128])
        self._tls = threading.local()
        self.coop = Coop()
        self.xres = X("xres", [L, D])

        def scratch(tag, Ls):
            return {"pa": X("pa" + tag, [2048, Ls], BF16),
                    "qt": X("qt" + tag, [1024, Ls], BF16),
                    "ckvd": X("ckvd" + tag, [128, Ls + past], BF16),
                    "krd": X("krd" + tag, [48, Ls + past], BF16),
                    "catd": X("catd" + tag, [1024, Ls], BF16),
                    "actT": X("actT" + tag, [DFF, Ls], BF16)}
        self.scr_main = scratch("", L)
        self.scr_p = [scratch("_p%d" % i, S) for i in range(npr)]
        self.dbg = {}
        if debug:
            self.dbg = {k: O("dbg_" + k, shp) for k, shp in debug.items()}
        lo = (nc.sbuf_base + 31) // 32 * 32
        self.arenas = []
        self.main_ctx = self.make_ctx(Arena(nc, lo, nc.sbuf_top // 32 * 32), PsumPool(nc), self.scr_main)
        self.use_ctx(self.main_ctx)
        self.P.on_barrier = lambda local: ([self._tls.ctx.A.reg.clear()] if local else [a.reg.clear() for a in self.arenas])
        self._tick_n = 0
        self.P.on_op = self._tick
        self.coop.on_resume = lambda: setattr(self.P, "tok", self._tls.ctx.tokens)
        self.build()

    def make_ctx(self, arena, ps, scr):
        c = Ctx()
        c.A, c.ps = arena, ps
        self.arenas.append(arena)
        for k, v in scr.items():
            setattr(c, k, v)
        c.tokens = {}
        c.gw_key = None
        c.GWT = c.GWTMP = None
        c._oi = c._wi = c._obi = c._ny = 0
        c.dI, c.hl = None, False
        return c

    def cst(self, blk, r0=0, r1=128, c0=0, c1=128):
        return self.CON[r0:r1, blk * 128 + c0:blk * 128 + c1]

    def pkc(self, col, r0=0, r1=128):
        return self.PK[r0:r1, col:col + 1]

    def rstd_from(self, out, in_, n, rows=128):
        P = self.P
        P.act(out, in_, AF.Ln, bias=self.EPSB[0:rows, :], scale=1.0 / n)
        P.act(out, out, AF.Exp, scale=-0.5)

    def build(self):
        P, A = self.P, self.A
        self.CON = A.tile([128, NCONST * 128], F32, "con")
        P.dma(self.CON[:, :], dv(self.consts[:, :]))
        self.IDB = A.tile([128, 128], BF16, "idb")
        P.cp(self.IDB[:, :], self.cst(C_ID))
        self.ONESB = A.tile([128, 64], BF16, "onesb")
        P.memset(self.ONESB[:, :], 1.0)
        self.EPSB = A.tile([128, 1], F32, "epsb")
        P.memset(self.EPSB[:, :], EPS)
        self.ONE1 = A.tile([128, 1], F32, "one1")
        P.memset(self.ONE1[:, :], 1.0)
        self.ZERO1 = A.tile([128, 1], F32, "zero1")
        P.memset(self.ZERO1[:, :], 0.0)
        self.CONDS = A.tile([128, 8, 2], F32, "conds")
        ct = A.tile([128, 8, 2], F32, "condraw")
        P.dma(ct[:, :, :], dv(self.condT[:, :, :]))
        P.act(self.CONDS[:, :, :], ct[:, :, :], AF.Silu)
        self.PK = A.tile([128, NPK], F32, "pk")
        self.MOD = A.tile([128, 48, 2], F32, "mod")
        self.MODV = A.tile([128, 6, 8, 2], F32, "modv")
        self.GWT = A.tile([128, D], F32, "gwt")
        self.GWTMP = A.tile([128, 128], F32, "gwtmp")
        self.gw_key = None
        base_mark = A.mark()
        npr = self.npr
        span = (A.hi - base_mark) // npr // 32 * 32
        banks = self.main_ctx.ps.banks
        nb = 8 // npr
        pctx = []
        for i in range(npr):
            c = self.make_ctx(Arena(self.nc, base_mark + i * span, base_mark + (i + 1) * span),
                              PsumPool(banks=banks[i * nb:(i + 1) * nb]), self.scr_p[i])
            pctx.append(c)
        for c in pctx:
            self.use_ctx(c)
            self.GWT = c.A.tile([128, D], F32, "gwt")
            self.GWTMP = c.A.tile([128, 128], F32, "gwtmp")
            c.base = c.A.mark()
        self.use_ctx(self.main_ctx)
        for l in range(self.depth):
            A.release(base_mark)
            P.dma(self.PK[:, :], dv(self.pk[l]))
            self.adaln(l)
            for c in [self.main_ctx] + pctx:
                c.gw_key = None
            last = (l == self.depth - 1)
            if self.only != "dec":
                def mk(i):
                    def fn():
                        self.use_ctx(pctx[i])
                        pctx[i].A.release(pctx[i].base)
                        self.layer_seq(l, i, self.seq, False, last)
                    return fn
                self.coop.run([mk(i) for i in range(npr)])
                self.use_ctx(self.main_ctx)
                self.barrier()
            if self.only == "pr":
                continue
            m = A.mark()
            self.layer_seq(l, -1, self.dec_seq, True, last)
            self.barrier()
            A.release(m)
        self.barrier()

    def adaln(self, l):
        P, A = self.P, self.A
        m = A.mark()
        pt = self.ps.get()
        wts = [A.tile([128, 8, 128], F32, "wada") for _ in range(3)]
        for c in range(48):
            wt = wts[c % 3]
            P.dma(sub(wt, wt.t[:, :, :].rearrange("p k c -> p (k c)")), dv(self.wada[l, c]))
            for k in range(8):
                P.mm(pt[:, c * 2:c * 2 + 2], wt[:, k, :], self.CONDS[:, k, :], start=(k == 0), stop=(k == 7))
        P.tt(self.MOD[:, :, :], sub(pt, pt.t[:, 0:96].rearrange("p (c j) -> p c j", j=2)),
             sub(self.PK, self.PK.t[:, PK_BADA:PK_BADA + 48].unsqueeze(2).to_broadcast([128, 48, 2])), ALU.add)
        for half, (nw, nwo) in enumerate(((PK_NMP, PK_NMO), (PK_NFP, PK_NFO))):
            sh = self.MOD[:, (half * 3 + 0) * 8:(half * 3 + 1) * 8, :]
            sc = self.MOD[:, (half * 3 + 1) * 8:(half * 3 + 2) * 8, :]
            g = self.MOD[:, (half * 3 + 2) * 8:(half * 3 + 3) * 8, :]
            nwb = sub(self.PK, self.PK.t[:, nw:nw + 8].unsqueeze(2).to_broadcast([128, 8, 2]))
            nwob = sub(self.PK, self.PK.t[:, nwo:nwo + 8].unsqueeze(2).to_broadcast([128, 8, 2]))
            P.stt(self.MODV[:, half * 3 + 0, :, :], sc, 1.0, nwb, ALU.add, ALU.mult)
            P.cp(self.MODV[:, half * 3 + 1, :, :], sh)
            P.tt(self.MODV[:, half * 3 + 2, :, :], g, nwob, ALU.mult)
        self.gw_key = None
        self.barrier()
        A.release(m)

    def layer_seq(self, l, sq, L, is_dec, last):
        P, A = self.P, self.A
        who = 0 if is_dec else 1
        TW = min(512, L)
        NT = L // TW
        NB = L // 128
        NK = L + (self.past if is_dec else 0)
        if is_dec:
            x_in = self.xs if l == 0 else self.xres
            x_mid = self.xres
            x_out = self.ys if last else self.xres
        else:
            x_in = self.xp[sq] if l == 0 else self.yp[sq]
            x_mid = self.yp[sq]
            x_out = self.yp[sq]
        hreg = A.mark()
        hT = [A.tile([128, 8, TW], BF16, "hT") for _ in range(NT)]
        OG = A.tile_at(hreg, [128, 2, L], F32, "og")
        OY = A.tile_at(hreg + A.nbytes([128, 2, L], F32), [128, 2, L], F32, "oy")
        ABT = A.tile([128, NB, 80], F32, "abt")

        if self.stop < 0:
            return
        m0 = A.mark()
        xt = [A.tile([128, D], F32, "xt") for _ in range(2)]
        for b in range(NB):
            x = xt[b % 2]
            P.dma(x[:, :], dv(x_in[b * 128:(b + 1) * 128, :]))
            self.norm_to_hT(x, hT, b, TW, 0, who)
        self.barrier()
        A.release(m0)
        if self.stop < 1:
            return
        m0 = A.mark()
        self.phase_a(l, sq, L, is_dec, hT, TW, NT, NB, NK, ABT)
        self.barrier()
        A.release(m0)
        if self.stop < 2:
            return
        m0 = A.mark()
        self.phase_scan(l, sq, L, is_dec, TW, NT, NB, ABT, OG, OY)
        self.barrier()
        A.release(m0)
        if self.stop < 3:
            return
        m0 = A.mark()
        self.phase_attn(l, L, is_dec, TW, NT, NK)
        self.barrier()
        A.release(m0)
        if self.stop < 4:
            return
        m0 = A.mark()
        wo = A.tile([128, 8, D], BF16, "wout")
        self.load_w_bf16(wo, self.wout[l], 8, D)
        xt = [A.tile([128, D], F32, "xt") for _ in range(2)]
        ct = [A.tile([128, 8, 128], BF16, "ct") for _ in range(2)]
        for b in range(NB):
            x = xt[b % 2]
            c = ct[b % 2]
            P.dma(x[:, :], dv(x_in[b * 128:(b + 1) * 128, :]))
            P.dma(c[:, :, :], dv(self.catd[:, b * 128:(b + 1) * 128].rearrange("(k p) t -> p k t", p=128)))
            import os
            cut2 = int(os.environ.get("K_CUT2", "99"))
            self.gw_rows(2, who)
            pss = [self.ps.get(), self.ps.get()]
            for hh in range(2):
                for k in range(8):
                    P.mm(pss[hh][:, :], c[:, k, :], wo[:, k, hh * 512:(hh + 1) * 512], start=(k == 0), stop=(k == 7))
            if cut2 < 1:
                continue
            self.residual(pss, x, 2, who, x_mid, b)
            if cut2 < 5:
                continue
            self.norm_to_hT(x, hT, b, TW, 3, who)
        self.barrier()
        A.release(m0)
        if self.stop < 5:
            return
        m0 = A.mark()
        self.phase_ffn_up(l, L, hT, TW, NT)
        self.barrier()
        A.release(m0)
        if self.stop < 6:
            return
        m0 = A.mark()
        wd = A.tile([128, 22, D], BF16, "wdn")
        self.load_w_bf16(wd, self.wdn[l], 22, D)
        xt = [A.tile([128, D], F32, "xt") for _ in range(2)]
        at = [A.tile([128, 22, 128], BF16, "at") for _ in range(2)]
        for b in range(NB):
            x = xt[b % 2]
            a = at[b % 2]
            P.dma(x[:, :], dv(x_mid[b * 128:(b + 1) * 128, :]))
            P.dma(a[:, :, :], dv(self.actT[:, b * 128:(b + 1) * 128].rearrange("(j p) t -> p j t", p=128)))
            self.gw_rows(5, who)
            pss = [self.ps.get(), self.ps.get()]
            for hh in range(2):
                for j in range(22):
                    P.mm(pss[hh][:, :], a[:, j, :], wd[:, j, hh * 512:(hh + 1) * 512], start=(j == 0), stop=(j == 21))
            self.residual(pss, x, 5, who, x_out, b)
        self.barrier()
        A.release(m0)
        A.release(hreg)

    def load_w_bf16(self, dst, src, nk, ncols, engs=("dve", "act")):
        P, A = self.P, self.A
        cw = min(ncols, 512)
        st = [A.tile([128, cw], F32, "wst") for _ in range(3)]
        i = 0
        for k in range(nk):
            for c0 in range(0, ncols, cw):
                s = st[i % 3]
                P.dma(s[:, :], dv(src[k * 128:(k + 1) * 128, c0:c0 + cw]))
                P.cp(dst[:, k, c0:c0 + cw], s[:, :], e=engs[i % len(engs)])
                i += 1

    def load_wg(self, dst, src, stage, e):
        P = self.P
        P.dma(sub(stage, stage.t[:, :, :].rearrange("p k c -> p (k c)")), dv(src))
        P.cp(dst[:, :, :], stage[:, :, :], e=e)

    def norm_to_hT(self, x, hT, b, TW, mv, who):
        P, A = self.P, self.A
        m = A.mark()
        junk = A.tile([128, D], BF16, "junk")
        ssq = A.tile([128, 1], F32, "ssq")
        rstd = A.tile([128, 1], F32, "rstd")
        xb = A.tile([128, D], BF16, "xb")
        import os
        cut = int(os.environ.get("K_CUT", "99"))
        P.act(junk[:, :], x[:, :], AF.Square, scale=float(D) ** -0.5, accum=ssq[:, :])
        if cut < 1:
            A.release(m); return
        self.rstd_from(rstd[:, :], ssq[:, :], 1.0)
        if cut < 2:
            A.release(m); return
        P.ts(xb[:, :], x[:, :], rstd[:, :], None, op0=ALU.mult)
        if cut < 3:
            A.release(m); return
        pt = self.ps.get()
        ptb = pt.t[:, :].bitcast(BF16)
        for k in range(8):
            P.tr(sub(pt, ptb[:, k * 128:(k + 1) * 128]), xb[:, k * 128:(k + 1) * 128], self.IDB[:, :])
        tt_, off = b * 128 // TW, (b * 128) % TW
        if cut < 4:
            A.release(m); return
        for k in range(8):
            src = sub(pt, ptb[:, k * 128:(k + 1) * 128])
            dst = hT[tt_][:, k, off:off + 128]
            if (k % 2 == 0 or cut == 4) and cut != 5:
                P.ts(dst, src, self.MODV[:, mv, k, who:who + 1], self.MODV[:, mv + 1, k, who:who + 1],
                     op0=ALU.mult, op1=ALU.add)
            else:
                P.act(dst, src, AF.Identity, bias=self.MODV[:, mv + 1, k, who:who + 1],
                      scale=self.MODV[:, mv, k, who:who + 1])
        A.release(m)

    def residual(self, pss, x, mv, who, x_dst, b):
        P, A = self.P, self.A
        m = A.mark()
        GW = self.gw_rows(mv, who)
        junk = A.tile([128, 512], F32, "junk")
        ss = A.tile([128, 2], F32, "ss")
        rstd = A.tile([128, 1], F32, "rstd")
        t = A.tile([128, D], F32, "t")
        import os
        cut2 = int(os.environ.get("K_CUT2", "99"))
        for hh in range(2):
            P.act(junk[:, :], pss[hh][:, :], AF.Square, scale=float(D) ** -0.5, accum=ss[:, hh:hh + 1])
        if cut2 < 2:
            A.release(m); return
        P.tt(rstd[:, :], ss[:, 0:1], ss[:, 1:2], ALU.add)
        self.rstd_from(rstd[:, :], rstd[:, :], 1.0)
        for hh in range(2):
            P.tt(t[:, hh * 512:(hh + 1) * 512], pss[hh][:, :], GW[:, hh * 512:(hh + 1) * 512], ALU.mult)
        if cut2 < 3:
            A.release(m); return
        P.stt(x[:, :], t[:, :], rstd[:, :], x[:, :], ALU.mult, ALU.add)
        if cut2 < 4:
            A.release(m); return
        P.dma(dv(x_dst[b * 128:(b + 1) * 128, :]), x[:, :], q="pool")
        A.release(m)

    def gw_rows(self, mv, who):
        if self.gw_key == (mv, who):
            return self.GWT
        P = self.P
        for k in range(8):
            P.ts(self.GWTMP[:, :], self.cst(C_ONES), self.MODV[:, mv, k, who:who + 1], None, op0=ALU.mult)
            pt = self.ps.get()
            P.mm(pt[:, 0:128], self.GWTMP[:, :], self.cst(C_ID))
            P.cp(self.GWT[:, k * 128:(k + 1) * 128], pt[:, 0:128])
        self.gw_key = (mv, who)
        return self.GWT
    def phase_a(self, l, sq, L, is_dec, hT, TW, NT, NB, NK, ABT):
        P, A = self.P, self.A
        stg = [A.tile([128, 8, 128], F32, "wstg") for _ in range(2)]
        wgs = [A.tile([128, 8, 128], BF16, "wg") for _ in range(3)]
        RAW = [A.tile([128, L + 4], BF16, "raw") for _ in range(2)]
        for r in RAW:
            P.memset(r[:, 0:2], 0.0)
            P.memset(r[:, L + 2:L + 4], 0.0)
        DG = A.tile([128, 5, 128], BF16, "dg")
        OUTS = [A.tile([128, TW], F32, "outs") for _ in range(3)]
        TMP = [A.tile([128, TW], F32, "tmpa") for _ in range(3)]
        CQRAW = A.tile([128, 2, L], F32, "cqraw")
        self._oi = 0
        self._wi = 0

        OUTB = [A.tile([128, TW], BF16, "outb") for _ in range(3)]
        self._obi = 0

        def nxt_out():
            self._oi += 1
            return OUTS[self._oi % 3]

        def nxt_outb():
            self._obi += 1
            return OUTB[self._obi % 3]

        def load_group(g, ncol=128):
            self._wi += 1
            w = wgs[self._wi % 3]
            self.load_wg(w, self.win[l, g], stg[self._wi % 2], "act" if self._wi % 2 else "dve")
            return w

        def proj(w, t, ncol=128):
            pt = self.ps.get()
            for k in range(8):
                P.mm(pt[0:ncol, 0:TW], w[:, k, 0:ncol], hT[t][:, k, :], start=(k == 0), stop=(k == 7))
            return pt

        def store_pa(row0, t, src):
            P.dma(dv(self.pa[row0:row0 + 128, t * TW:(t + 1) * TW]), src, q="pool")

        convs = []
        for i in range(6):
            kind = "qk" if i < 4 else "plain"
            convs.append((G_Q + i, PK_GCONV + i * 5, None, kind, i * 128, 0.125 if i < 2 else 1.0))
        for i in range(6):
            convs.append((G_XS + i, PK_SCONV + i * 5, PK_SCB + i, "plain", 1280 + i * 128, 1.0))
        for ci, (g, ccol, bcol, kind, row0, qs) in enumerate(convs):
            w = load_group(g)
            raw = RAW[ci % 2]
            for t in range(NT):
                pt = proj(w, t)
                if t % 2 == 0:
                    P.cp(raw[:, 2 + t * TW:2 + (t + 1) * TW], pt[:, 0:TW], e="act")
                else:
                    P.cp(raw[:, 2 + t * TW:2 + (t + 1) * TW], pt[:, 0:TW], e="dve")
            for j in range(5):
                P.ts(DG[:, j, :], self.IDB[:, :], self.pkc(ccol + j), None, op0=ALU.mult)
            for t in range(NT):
                pt = self.ps.get()
                for j in range(5):
                    P.mm(pt[:, 0:TW], DG[:, j, :], raw[:, t * TW + j:t * TW + j + TW], start=(j == 0), stop=(j == 4))
                ob_ = nxt_outb()
                if kind == "qk":
                    o = nxt_out()
                    P.act(o[:, :], pt[:, 0:TW], AF.Silu)
                    sqt = TMP[0]
                    P.tt(sqt[:, :], o[:, :], o[:, :], ALU.mult)
                    p2 = self.ps.get()
                    P.mm(p2[:, 0:TW], self.cst(C_ONESBD), sqt[:, :])
                    rs = TMP[1]
                    P.act(rs[:, :], p2[:, 0:TW], AF.Ln, bias=self.EPSB[:, :])
                    P.act(rs[:, :], rs[:, :], AF.Exp, scale=-0.5)
                    P.stt(ob_[:, :], o[:, :], qs, rs[:, :], ALU.mult, ALU.mult)
                elif bcol is None:
                    P.act(ob_[:, :], pt[:, 0:TW], AF.Silu)
                else:
                    P.act(ob_[:, :], pt[:, 0:TW], AF.Silu, bias=self.pkc(bcol))
                store_pa(row0, t, ob_[:, :])
        import os
        cut3 = int(os.environ.get("K_CUT3", "99"))
        if cut3 < 1:
            return
        for (g, row0) in ((G_GATE, 768), (G_GATE + 1, 896), (G_Z, 1024), (G_Z + 1, 1152)):
            w = load_group(g)
            for t in range(NT):
                pt = proj(w, t)
                ob_ = nxt_outb()
                P.act(ob_[:, :], pt[:, 0:TW], AF.Silu)
                store_pa(row0, t, ob_[:, :])
        if cut3 < 2:
            return
        w = load_group(G_AB)
        NEA = A.tile([16, 1], F32, "nea")
        P.act(NEA[:, :], self.pkc(PK_ALOG, 0, 16), AF.Exp)
        P.ts(NEA[:, :], NEA[:, :], -1.0, None, op0=ALU.mult)
        ABF = A.tile([128, TW], F32, "abf")
        P.memset(ABF[:, :], 0.0)
        for t in range(NT):
            pt = proj(w, t)
            e1 = TMP[0]
            P.act(e1[0:16, :], pt[0:16, 0:TW], AF.Exp, bias=self.pkc(PK_DTB, 0, 16))
            P.act(ABF[64:80, :], e1[0:16, :], AF.Ln, bias=self.ONE1[0:16, :])
            P.act(e1[0:16, :], e1[0:16, :], AF.Ln, bias=self.ONE1[0:16, :])
            P.ts(ABF[0:16, :], e1[0:16, :], NEA[:, :], None, op0=ALU.mult)
            P.act(ABF[32:40, :], pt[32:40, 0:TW], AF.Sigmoid)
            for s in range(TW // 128):
                b = t * (TW // 128) + s
                p2 = self.ps.get()
                P.tr(p2[:, 0:80], ABF[0:80, s * 128:(s + 1) * 128], self.cst(C_ID, 0, 80, 0, 80))
                P.cp(ABT[:, b, :], p2[:, 0:80])
        if cut3 < 3:
            return
        w = load_group(G_CKV)
        for t in range(NT):
            pt = proj(w, t)
            sqt = TMP[0]
            P.act(sqt[:, :], pt[:, 0:TW], AF.Square)
            p2 = self.ps.get()
            P.mm(p2[:, 0:TW], self.cst(C_ONES), sqt[:, :])
            rs = TMP[1]
            self.rstd_from(rs[:, :], p2[:, 0:TW], 128.0)
            o = nxt_out()
            P.stt(o[:, :], pt[:, 0:TW], self.pkc(PK_KVN), rs[:, :], ALU.mult, ALU.mult)
            ob = TMP[2]
            obb = sub(ob, ob.t[:, 0:TW // 2].bitcast(BF16))
            P.cp(obb, o[:, :], e="act")
            P.dma(dv(self.ckvd[:, t * TW:(t + 1) * TW]), obb, q="pool")
            if not is_dec:
                for s in range(TW // 128):
                    b = t * (TW // 128) + s
                    p3 = self.ps.get()
                    P.tr(p3[:, 0:128], o[:, s * 128:(s + 1) * 128], self.cst(C_ID))
                    o3 = nxt_out()
                    P.cp(o3[:, 0:128], p3[:, 0:128])
                    P.dma(dv(self.ockv[sq, l, b * 128:(b + 1) * 128, :]), o3[:, 0:128], q="pool")
        if is_dec:
            for s in range(self.past // 128):
                c = TMP[0]
                P.dma(c[:, 0:128], dv(self.cckv[l, s * 128:(s + 1) * 128, :]))
                p3 = self.ps.get()
                P.tr(p3[:, 0:128], c[:, 0:128], self.cst(C_ID))
                ob = TMP[2]
                obb = sub(ob, ob.t[:, 0:64].bitcast(BF16))
                P.cp(obb, p3[:, 0:128])
                P.dma(dv(self.ckvd[:, L + s * 128:L + (s + 1) * 128]), obb, q="pool")
        if cut3 < 4:
            return
        wa = load_group(G_KR, 48)
        wb = load_group(G_KRB, 48)
        ROP = [A.tile([48, 2, TW], F32, "rop") for _ in range(2)]
        for t in range(NT):
            pa_ = proj(wa, t, 48)
            ob = TMP[2]
            obb = sub(ob, ob.t[0:48, 0:TW // 2].bitcast(BF16))
            if is_dec:
                pb_ = proj(wb, t, 48)
                rp = ROP[t % 2]
                P.dma(rp[:, :, :], dv(self.rope[:, :, t * TW:(t + 1) * TW]))
                t1 = TMP[0]
                t2 = TMP[1]
                P.tt(t1[0:48, :], pa_[0:48, 0:TW], rp[:, 0, :], ALU.mult)
                P.tt(t2[0:48, :], pb_[0:48, 0:TW], rp[:, 1, :], ALU.mult)
                P.tt(obb, t1[0:48, :], t2[0:48, :], ALU.add)
            else:
                o = nxt_out()
                P.cp(o[0:48, :], pa_[0:48, 0:TW])
                P.cp(obb, o[0:48, :], e="act")
                for s in range(TW // 128):
                    b = t * (TW // 128) + s
                    p3 = self.ps.get()
                    P.tr(p3[:, 0:48], o[0:48, s * 128:(s + 1) * 128], self.cst(C_ID, 0, 48, 0, 48))
                    o3 = nxt_out()
                    P.cp(o3[:, 0:48], p3[:, 0:48])
                    P.dma(dv(self.okr[sq, l, b * 128:(b + 1) * 128, 0:16]), o3[:, 0:16], q="pool")
                    P.dma(dv(self.okr[sq, l, b * 128:(b + 1) * 128, 16:32]), o3[:, 32:48], q="pool")
            P.dma(dv(self.krd[:, t * TW:(t + 1) * TW]), obb, q="pool")
        skip = os.environ.get('K_SKIP', '').split(',')
        if is_dec and 'ctxkr' not in skip:
            for s in range(self.past // 128):
                c = TMP[0]
                P.dma(c[:, 0:48], dv(self.ckr[l, s * 128:(s + 1) * 128, :]))
                p3 = self.ps.get()
                P.tr(p3[0:64, 0:128], c[:, 0:64], self.cst(C_ID))
                ob = TMP[2]
                obb = sub(ob, ob.t[0:48, 0:64].bitcast(BF16))
                P.cp(obb, p3[0:48, 0:128])
                P.dma(dv(self.krd[:, L + s * 128:L + (s + 1) * 128]), obb, q="pool")
        if cut3 < 5:
            return
        for i in range(2):
            w = load_group(G_CQ + i)
            for t in range(NT):
                pt = proj(w, t)
                P.cp(CQRAW[:, i, t * TW:(t + 1) * TW], pt[:, 0:TW], e=("act" if t % 2 else "dve"))
        WQ = A.tile([128, 2, 1024], BF16, "wq")
        WQB = A.tile([128, 2, 512], BF16, "wqb")
        self.load_w_bf16(WQ, self.wuq[l], 2, 1024)
        self.load_w_bf16(WQB, self.wuqb[l], 2, 512)
        CQN = A.tile([128, 2, TW], BF16, "cqn")
        QO = [A.tile([128, TW], BF16, "qo") for _ in range(2)]
        for t in range(NT):
            p2 = self.ps.get()
            for i in range(2):
                sqt = TMP[i]
                P.act(sqt[:, :], CQRAW[:, i, t * TW:(t + 1) * TW], AF.Square)
                P.mm(p2[:, 0:TW], self.cst(C_ONES), sqt[:, :], start=(i == 0), stop=(i == 1))
            rs = TMP[2]
            self.rstd_from(rs[:, :], p2[:, 0:TW], 256.0)
            for i in range(2):
                P.stt(CQN[:, i, :], CQRAW[:, i, t * TW:(t + 1) * TW], self.pkc(PK_QN + i), rs[:, :], ALU.mult, ALU.mult)
            if is_dec:
                rp = ROP[t % 2]
                P.dma(rp[:, :, :], dv(self.rope[:, :, t * TW:(t + 1) * TW]))
            for h in range(8):
                pa_ = self.ps.get()
                for i in range(2):
                    P.mm(pa_[:, 0:TW], WQ[:, i, h * 128:(h + 1) * 128], CQN[:, i, :], start=(i == 0), stop=(i == 1))
                qo = QO[h % 2]
                P.cp(qo[:, :], pa_[:, 0:TW], e="dve")
                if is_dec:
                    skip = os.environ.get('K_SKIP', '').split(',')
                    if 'qpb' in skip:
                        pb_ = pa_
                    else:
                        pb_ = self.ps.get()
                        for i in range(2):
                            P.mm(pb_[0:48, 0:TW], WQB[:, i, h * 64:h * 64 + 48], CQN[:, i, :], start=(i == 0), stop=(i == 1))
                    t1 = TMP[0]
                    t2 = TMP[1]
                    if 'qtt' not in skip:
                        P.tt(t1[0:48, :], pa_[0:48, 0:TW], rp[:, 0, :], ALU.mult)
                        P.tt(t2[0:48, :], pb_[0:48, 0:TW], rp[:, 1, :], ALU.mult)
                        P.tt(qo[0:48, :], t1[0:48, :], t2[0:48, :], ALU.add)
                P.dma(dv(self.qt[h * 128:(h + 1) * 128, t * TW:(t + 1) * TW]), qo[:, :], q="pool")

    def phase_ffn_up(self, l, L, hT, TW, NT):
        P, A = self.P, self.A
        stg = [A.tile([128, 8, 128], F32, "wstg") for _ in range(2)]
        WG = [A.tile([128, 8, 128], BF16, "wg") for _ in range(2)]
        WU = [A.tile([128, 8, 128], BF16, "wu") for _ in range(2)]
        GB = [A.tile([128, L + 2], BF16, "gb") for _ in range(2)]
        UB = [A.tile([128, L + 2], BF16, "ub") for _ in range(2)]
        for r in GB + UB:
            P.memset(r[:, 0:1], 0.0)
            P.memset(r[:, L + 1:L + 2], 0.0)
        DGG = A.tile([128, 3, 128], BF16, "dgg")
        DGU = A.tile([128, 3, 128], BF16, "dgu")
        SGT = [A.tile([128, TW], F32, "sgt") for _ in range(2)]
        AO = [A.tile([128, TW], BF16, "ao") for _ in range(3)]
        n = 0
        for cg in range(22):
            wg, wu, gb, ub = WG[cg % 2], WU[cg % 2], GB[cg % 2], UB[cg % 2]
            self.load_wg(wg, self.wup[l, cg], stg[0], "dve")
            self.load_wg(wu, self.wup[l, 22 + cg], stg[1], "act")
            for t in range(NT):
                pg = self.ps.get()
                pu = self.ps.get()
                for k in range(8):
                    P.mm(pg[:, 0:TW], wg[:, k, :], hT[t][:, k, :], start=(k == 0), stop=(k == 7))
                for k in range(8):
                    P.mm(pu[:, 0:TW], wu[:, k, :], hT[t][:, k, :], start=(k == 0), stop=(k == 7))
                P.cp(gb[:, 1 + t * TW:1 + (t + 1) * TW], pg[:, 0:TW], e="act")
                P.cp(ub[:, 1 + t * TW:1 + (t + 1) * TW], pu[:, 0:TW], e="dve")
            for j in range(3):
                P.ts(DGG[:, j, :], self.IDB[:, :], self.pkc(PK_FCONV + cg * 3 + j), None, op0=ALU.mult)
                P.ts(DGU[:, j, :], self.IDB[:, :], self.pkc(PK_FCONV + (22 + cg) * 3 + j), None, op0=ALU.mult)
            for t in range(NT):
                pg = self.ps.get()
                pu = self.ps.get()
                for j in range(3):
                    P.mm(pg[:, 0:TW], DGG[:, j, :], gb[:, t * TW + j:t * TW + j + TW], start=(j == 0), stop=(j == 2))
                for j in range(3):
                    P.mm(pu[:, 0:TW], DGU[:, j, :], ub[:, t * TW + j:t * TW + j + TW], start=(j == 0), stop=(j == 2))
                sg = SGT[n % 2]
                ao = AO[n % 3]
                n += 1
                P.act(sg[:, :], pg[:, 0:TW], AF.Silu)
                P.tt(ao[:, :], pu[:, 0:TW], sg[:, :], ALU.mult)
                P.dma(dv(self.actT[cg * 128:(cg + 1) * 128, t * TW:(t + 1) * TW]), ao[:, :], q="pool")

    def phase_attn(self, l, L, is_dec, TW, NT, NK):
        P, A = self.P, self.A
        NKT = NK // 128
        scale = 96.0 ** -0.5
        CKVB = A.tile([128, NK], BF16, "ckvb")
        KRB = A.tile([48, NK], BF16, "krb")
        P.dma(CKVB[:, :], dv(self.ckvd[:, 0:NK]))
        P.dma(KRB[:, :], dv(self.krd[:, 0:NK]))
        WUK = A.tile([128, 1, 1024], BF16, "wuk")
        WUV = A.tile([128, 1, 512], BF16, "wuv")
        self.load_w_bf16(WUK, self.wuk[l], 1, 1024)
        self.load_w_bf16(WUV, self.wuv[l], 1, 512)
        KT = [A.tile([128, NK], BF16, "kt") for _ in range(2)]
        VA = [A.tile([128, NKT, 128], BF16, "va") for _ in range(2)]
        for va in VA:
            P.cp(va[:, :, 64:128], sub(self.ONESB, self.ONESB.t[:, :].unsqueeze(1).to_broadcast([128, NKT, 64])), e="pool")
        QT = [A.tile([128, L], BF16, "qth") for _ in range(2)]
        PT = [A.tile([128, TW], BF16, "pt") for _ in range(8)]
        REC = [A.tile([64, TW], F32, "rec") for _ in range(2)]
        OT = [A.tile([64, TW], BF16, "ot") for _ in range(2)]
        npt = 0
        pend = []
        LAG = 4

        def drain(n):
            while len(pend) > n:
                pend.pop(0)()

        for h in range(8):
            kt, va, qt = KT[h % 2], VA[h % 2], QT[h % 2]
            while pend and pend[0].head <= h - 2:
                pend.pop(0)()
            P.dma(qt[:, :], dv(self.qt[h * 128:(h + 1) * 128, 0:L]))
            for c0 in range(0, NK, 512):
                cw = min(512, NK - c0)
                pt = self.ps.get()
                P.mm(pt[:, 0:cw], WUK[:, 0, h * 128:(h + 1) * 128], CKVB[:, c0:c0 + cw], start=True, stop=False)
                P.mm(pt[:, 0:cw], self.IDB[0:48, :], KRB[:, c0:c0 + cw], start=False, stop=True)
                P.cp(kt[:, c0:c0 + cw], pt[:, 0:cw], e="dve")
            for k0 in range(0, NKT, 8):
                kn = min(8, NKT - k0)
                pt = self.ps.get()
                for kk in range(kn):
                    P.mm(pt[:, kk * 64:(kk + 1) * 64], CKVB[:, (k0 + kk) * 128:(k0 + kk + 1) * 128],
                         WUV[:, 0, h * 64:(h + 1) * 64])
                P.cp(va[:, k0:k0 + kn, 0:64], sub(pt, pt.t[:, 0:kn * 64].rearrange("p (k v) -> p k v", v=64)), e="dve")
            for t in range(NT):
                po = self.ps.reserve()
                for kti in range(NKT):
                    ps_ = self.ps.get()
                    P.mm(ps_[:, 0:TW], kt[:, kti * 128:(kti + 1) * 128], qt[:, t * TW:(t + 1) * TW])
                    pt_ = PT[npt % len(PT)]
                    npt += 1
                    P.act(pt_[:, :], ps_[:, 0:TW], AF.Exp, scale=scale)

                    def pv(po=po, kti=kti, pt_=pt_, va=va, t=t, h=h):
                        P.mm(po[:, 0:TW], va[:, kti, :], pt_[:, :], start=(kti == 0), stop=(kti == NKT - 1))
                        if kti == NKT - 1:
                            rec = REC[(h * NT + t) % 2]
                            P.recip(rec[:, :], po[64:128, 0:TW])
                            ot = OT[(h * NT + t) % 2]
                            P.tt(ot[:, :], po[0:64, 0:TW], rec[:, :], ALU.mult)
                            self.ps.free(po)
                            P.dma(dv(self.catd[256 + h * 64:256 + (h + 1) * 64, t * TW:(t + 1) * TW]), ot[:, :], q="pool")

                    pv.head = h
                    pend.append(pv)
                    drain(LAG)
        drain(0)

    def scan_tiles(self):
        import os
        A = self.A
        f = lambda shape, name: A.tile(shape, F32, name)
        h = lambda shape, name: A.tile(shape, BF16, name)
        mode = os.environ.get("K_DI", "r32")
        self.dI = {"f32": F32, "r32": F32R}.get(mode, BF16)
        self.hl = (mode == "hl")
        i_ = lambda shape, name: A.tile(shape, self.dI, name)
        T = {}
        T["FMS"] = [{k: [h([128, 128], "fm" + k) for _ in range(2)] for k in ("q", "k", "v", "x", "b", "c")} for _ in range(2)]
        T["TM"] = h([128, 4, 256], "tm")
        T["GC"], T["EG"], T["EGL"] = f([128, 8], "gc"), f([128, 8], "eg"), f([128, 8], "egl")
        T["NB"], T["BEG"] = f([128, 4], "nbeta"), f([128, 4], "beg")
        T["LAB"] = [f([128, 512], "lab") for _ in range(2)]
        T["EGRW"] = [f([128, 4, 128], "egrw") for _ in range(2)]
        T["DECT"] = [f([128, 4, 128], "dect") for _ in range(2)]
        T["DECS"] = f([128, 4, 128], "decs")
        T["MP"] = [i_([128, 4, 3 if self.hl else 2, 128], "mp") for _ in range(2)]
        T["AT"] = [i_([128, 4, 128], "at") for _ in range(2)]
        T["R"] = f([128, 4, 128], "r") if (self.hl or self.dI == F32R) else i_([128, 4, 128], "r")
        if self.hl:
            T["PF"] = f([128, 4, 128], "pf")
            T["PHL"] = h([128, 4, 2, 128], "phl")
        kv_ = f if (self.hl or self.dI == F32R) else i_
        T["KBG"], T["VB"] = kv_([128, 4, 64], "kbg"), kv_([128, 4, 64], "vb")
        T["WT"] = h([64, 4, 128], "wt")
        T["U"] = f([64, 2, 4, 64], "u")
        T["ATT"] = h([64, 2, 4, 64], "att")
        T["QDT"] = h([64, 4, 128], "qdt")
        T["KDEC"] = h([64, 2, 4, 64], "kdec")
        T["SCT"] = h([64, 2, 4, 64], "sct")
        T["XDT"] = h([64, 2, 4, 64], "xdt")
        T["BDEC"] = h([64, 2, 4, 128], "bdec")
        T["CDT"] = h([128, 4, 128], "cdt")
        T["VN"] = h([64, 4, 64], "vn")
        T["SG"] = [f([64, 4, 64], "sg") for _ in range(2)]
        T["SGB"] = [h([64, 4, 64], "sgb") for _ in range(2)]
        T["SS"] = [f([128, 4, 64], "ss") for _ in range(2)]
        T["SSB"] = [h([128, 4, 64], "ssb") for _ in range(2)]
        T["STMP"] = f([64, 4, 128], "stmp")
        return T

    def scan_dir(self, d, T, l, sq, L, is_dec, NB, ABT, OGt, OYt, obufs, touched):
        P = self.P
        ID = self.cst(C_ID)
        IDB = self.IDB
        rows = {"q": 0, "k": 256, "v": 512, "x": 1280, "b": 1536, "c": 1792}
        TRI = self.cst(C_TRIF + d)
        TRIS = self.cst(C_TRISF + d)
        b4 = lambda blk: sub(self.CON, self.CON.t[:, blk * 128:(blk + 1) * 128].unsqueeze(1).to_broadcast([128, 4, 128]))
        TRI4, TRIS4 = b4(C_TRIF + d), b4(C_TRISF + d)
        IDB4 = sub(IDB, IDB.t[:, :].unsqueeze(1).to_broadcast([128, 4, 128]))
        ID4 = b4(C_ID)
        FMS, TM = T["FMS"], T["TM"]
        GC, EG, EGL, NB_, BEG, LAB = T["GC"], T["EG"], T["EGL"], T["NB"], T["BEG"], T["LAB"]
        EGRW, DECT, DECS, MP, AT, R = T["EGRW"], T["DECT"], T["DECS"], T["MP"], T["AT"], T["R"]
        KBG, VB, WT, U, ATT, QDT, KDEC = T["KBG"], T["VB"], T["WT"], T["U"], T["ATT"], T["QDT"], T["KDEC"]
        SCT, XDT, BDEC, CDT, VN = T["SCT"], T["XDT"], T["BDEC"], T["CDT"], T["VN"]
        SG, SGB, SS, SSB, STMP = T["SG"], T["SGB"], T["SS"], T["SSB"], T["STMP"]
        h4 = lambda pt, n=128: sub(pt, pt.t[:, 0:4 * n].rearrange("p (h i) -> p h i", h=4))
        f32v = (lambda v: V(v.ap.bitcast(F32), v.buf)) if self.dI == F32R else (lambda v: v)
        sgi, ssi = 0, 0
        st = {"ssi": 0}
        import os
        scut = int(os.environ.get("K_SCUT", "100000"))
        MASKE = os.environ.get('K_MASKE', 'dve')
        self._ny = 0
        if is_dec:
            P.dma(SG[0][:, :, :], dv(self.stg[l, d].rearrange("h k v -> k h v")))
            P.dma(STMP[:, :, :], dv(self.sts[l, d].rearrange("h p n -> p h n")))
            pt = self.ps.get()
            for h in range(4):
                P.tr(pt[:, h * 64:(h + 1) * 64], STMP[:, h, :], self.cst(C_ID, 0, 64, 0, 64))
            P.cp(SS[0][:, :, :], h4(pt, 64))
        else:
            P.memset(SG[0][:, :, :], 0.0)
            P.memset(SS[0][:, :, :], 0.0)
        P.cp(SGB[0][:, :, :], SG[0][:, :, :], e="pool")
        P.cp(SSB[0][:, :, :], SS[0][:, :, :], e="pool")
        self._ny += 1
        if self._ny >= scut:
            return
        yield
        blocks = list(range(NB)) if d == 0 else list(range(NB - 1, -1, -1))

        def load_fm(bi):
            fm = FMS[bi % 2]
            tk = slice(blocks[bi] * 128, (blocks[bi] + 1) * 128)
            for k in fm:
                for g in range(2):
                    P.dma(fm[k][g][:, :], dv(self.pa[rows[k] + g * 128:rows[k] + (g + 1) * 128, tk]))

        load_fm(0)
        for bi, b in enumerate(blocks):
            tok = slice(b * 128, (b + 1) * 128)
            FM = FMS[bi % 2]
            if bi + 1 < len(blocks):
                load_fm(bi + 1)
            skip = os.environ.get('K_SKIP', '')
            for half, keys in enumerate((("k", "v"), ("x", "b"))):
                if 'tm' in skip:
                    break
                pt = self.ps.get()
                ptb = pt.t[:, :].bitcast(BF16)
                for i, k in enumerate(keys):
                    for g in range(2):
                        c0 = i * 256 + g * 128
                        P.tr(sub(pt, ptb[:, c0:c0 + 128]), FM[k][g][:, :], IDB[:, :])
                P.ts(sub(TM, TM.t[:, half * 2:half * 2 + 2, :].rearrange("p a c -> p (a c)")), sub(pt, ptb[:, 0:512]),
                     1.0, None, op0=ALU.mult)
            KT4 = sub(TM, TM.t[:, 0, :].rearrange("p (h v) -> p h v", h=4))
            VT4 = sub(TM, TM.t[:, 1, :].rearrange("p (h v) -> p h v", h=4))
            XT4 = sub(TM, TM.t[:, 2, :].rearrange("p (h v) -> p h v", h=4))
            BT = sub(TM, TM.t[:, 3, :])
            lasel = sub(ABT, ABT.t[:, b, 0:16].rearrange("p (t d h) -> p t d h", t=2, d=2)[:, :, d, :])
            pt = self.ps.get()
            P.mm(sub(pt, pt.t[:, 0:8].rearrange("p (t h) -> p t h", t=2)), TRI, lasel)
            P.mm(sub(pt, pt.t[:, 8:16].rearrange("p (t h) -> p t h", t=2)), TRIS, lasel)
            P.cp(GC[:, :], pt[:, 0:8])
            P.act(EG[:, :], pt[:, 0:8], AF.Exp)
            P.act(EGL[:, :], pt[:, 8:16], AF.Exp)
            beta = ABT[:, b, 32 + d * 4:36 + d * 4]
            dtc = ABT[:, b, 72 + d * 4:76 + d * 4]
            P.ts(NB_[:, :], beta, -1.0, None, op0=ALU.mult)
            P.tt(BEG[:, :], beta, EG[:, 0:4], ALU.mult)
            for ty in range(2):
                for h in range(4):
                    P.ts(sub(LAB[ty], LAB[ty].t[:, h * 128:(h + 1) * 128]), TRI, ABT[:, b, ty * 8 + d * 4 + h:ty * 8 + d * 4 + h + 1],
                         None, op0=ALU.mult)
            self._ny += 1
            if self._ny >= scut:
                return
            yield
            pg = [self.ps.get(), self.ps.get()]
            for ty in range(2):
                P.mm(pg[ty][:, :], self.cst(C_ONES), LAB[ty][:, :])
            for h in range(4):
                P.ts(DECS[:, h, :], pg[0][:, h * 128:(h + 1) * 128], GC[:, h:h + 1], self.ZERO1[:, :], op0=ALU.subtract, op1=ALU.max)
            P.act(DECS[:, :, :], DECS[:, :, :], AF.Exp, scale=-1.0)
            P.tt(DECS[:, :, :], DECS[:, :, :], TRIS4, ALU.mult, e=MASKE)
            for ty in range(2):
                for h in range(4):
                    P.ts(DECT[ty][:, h, :], pg[ty][:, h * 128:(h + 1) * 128], GC[:, ty * 4 + h:ty * 4 + h + 1], self.ZERO1[:, :],
                         op0=ALU.subtract, op1=ALU.min)
                P.act(EGRW[ty][:, :, :], h4(pg[ty]), AF.Exp, after=[DECT[ty][:, :, :]] + ([DECS[:, :, :]] if ty == 0 else []))
                P.act(DECT[ty][:, :, :], DECT[ty][:, :, :], AF.Exp)
                P.tt(DECT[ty][:, :, :], DECT[ty][:, :, :], TRI4, ALU.mult, e=MASKE)
            self._ny += 1
            if self._ny >= scut:
                return
            yield
            mp, at = MP[0], AT[0]
            pkk = self.ps.get()
            for h in range(4):
                kf = FM["k"][h // 2][(h % 2) * 64:(h % 2) * 64 + 64, :]
                P.mm(pkk[:, h * 128:(h + 1) * 128], kf, kf)
            for h in range(4):
                P.stt(mp[:, h, 0, :], pkk[:, h * 128:(h + 1) * 128], NB_[:, h:h + 1], DECS[:, h, :], ALU.mult, ALU.mult)
            if self.dI == F32R:
                P.cp(mp[:, :, 1, :], ID4)
            else:
                P.cp(mp[:, :, 1, :], IDB4, e="pool")
            if self.hl:
                P.memset(mp[:, :, 2, :], 0.0)
                P.cp(T["PF"][:, :, :], ID4, e="pool")
            pt = self.ps.get()
            if self.dI == BF16:
                ptb = pt.t[:, :].bitcast(BF16)
                for h in range(4):
                    P.tr(sub(pt, ptb[:, h * 128:(h + 1) * 128]), mp[:, h, 0, :], IDB[:, :])
                P.cp(at[:, :, :], sub(pt, ptb[:, 0:512].rearrange("p (h i) -> p h i", h=4)), e="act")
            else:
                for h in range(4):
                    P.tr(pt[:, h * 128:(h + 1) * 128], f32v(mp[:, h, 0, :]), ID)
                P.cp(at[:, :, :], h4(pt), e="act")
            pqk = self.ps.get()
            for h in range(4):
                r0 = (h % 2) * 64
                P.mm(pqk[:, h * 128:(h + 1) * 128], FM["k"][h // 2][r0:r0 + 64, :], FM["q"][h // 2][r0:r0 + 64, :])
            pbc = self.ps.get()
            for gr in range(2):
                P.mm(pbc[:, gr * 128:(gr + 1) * 128], FM["b"][gr][:, :], FM["c"][gr][:, :])
            for c in range(2):
                cs = slice(c * 64, c * 64 + 64)
                P.tt(ATT[:, c, :, :], sub(pqk, pqk.t[cs, :].rearrange("p (h i) -> p h i", h=4)[:, :, cs]),
                     DECT[0][cs, :, cs], ALU.mult)
                for gr in range(2):
                    P.tt(SCT[:, c, gr * 2:gr * 2 + 2, :],
                         sub(pbc, pbc.t[cs, gr * 128 + c * 64:gr * 128 + c * 64 + 64].unsqueeze(1).to_broadcast([64, 2, 64])),
                         DECT[1][cs, gr * 2:gr * 2 + 2, cs], ALU.mult)
            self._ny += 1
            if self._ny >= scut:
                return
            yield
            def ssd_step(c):
                cs = slice(c * 64, c * 64 + 64)
                il = c * 64 + (63 if d == 0 else 0)
                ctok = slice(b * 128 + c * 64, b * 128 + c * 64 + 64)
                ck = b * 2 + c
                first = ("y", ck) not in touched
                touched.add(("y", ck))
                oyv = V(OYt.t[:, :, ctok], obufs[1][ck])
                ssi_ = st["ssi"]
                S, Sn, Sb, Sbn = SS[ssi_], SS[1 - ssi_], SSB[ssi_], SSB[1 - ssi_]
                st["ssi"] = 1 - ssi_
                po = self.ps.get()
                for h in range(4):
                    r0 = (h % 2) * 64
                    oreg = po[r0:r0 + 64, (h // 2) * 64:(h // 2) * 64 + 64]
                    P.mm(oreg, Sb[:, h, :], CDT[:, h, cs], start=True, stop=False)
                    P.mm(oreg, XDT[:, c, h, :], SCT[:, c, h, :], start=False, stop=True)
                pk_ = self.ps.get()
                for h in range(4):
                    P.mm(pk_[:, h * 64:(h + 1) * 64], BDEC[:, c, h, :], XDT[:, c, h, :])
                for h in range(4):
                    P.stt(Sn[:, h, :], S[:, h, :], EGRW[1][:, h, il:il + 1], pk_[:, h * 64:(h + 1) * 64], ALU.mult, ALU.add)
                P.cp(Sbn[:, :, :], Sn[:, :, :], e="pool")
                posrc = sub(po, po.t[:, 0:128].rearrange("p (g i) -> p g i", g=2))
                if first:
                    P.cp(oyv, posrc, e="act")
                else:
                    P.tt(oyv, posrc, oyv, ALU.add)

            corder = (0, 1) if d == 0 else (1, 0)
            cur = 0
            for lev in range(6):
                mp, at = MP[cur], AT[cur]
                mpn, atn = MP[1 - cur], AT[1 - cur]
                lastlev = (lev == 5)
                pM = [self.ps.get(), self.ps.get()]
                for h in range(4):
                    reg0 = (h % 2) * 256
                    if self.hl:
                        P.mm(pM[h // 2][:, reg0:reg0 + 256], at[:, h, :],
                             sub(mp, mp.t[:, h, 0:2, :].rearrange("p a i -> p (a i)")), start=True, stop=False)
                        P.mm(pM[h // 2][:, reg0 + 128:reg0 + 256], at[:, h, :], mp[:, h, 2, :], start=False, stop=True)
                    else:
                        P.mm(pM[h // 2][:, reg0:reg0 + 256], at[:, h, :],
                             sub(mp, mp.t[:, h, :, :].rearrange("p a i -> p (a i)")))
                if not lastlev:
                    pA = self.ps.get()
                    for h in range(4):
                        P.mm(pA[:, h * 128:(h + 1) * 128], mp[:, h, 0, :], at[:, h, :])
                    P.cp(atn[:, :, :], h4(pA), e="act")
                for hp in range(2):
                    src = pM[hp].t[:, :].rearrange("p (h a i) -> p h a i", h=2, a=2)
                    if not lastlev:
                        P.cp(mpn[:, hp * 2:hp * 2 + 2, 0, :], sub(pM[hp], src[:, :, 0, :]), e="act")
                    if self.hl:
                        PF = T["PF"]
                        P.tt(PF[:, hp * 2:hp * 2 + 2, :], sub(pM[hp], src[:, :, 1, :]), PF[:, hp * 2:hp * 2 + 2, :], ALU.add)
                    else:
                        P.tt(mpn[:, hp * 2:hp * 2 + 2, 1, :], sub(pM[hp], src[:, :, 1, :]), f32v(mp[:, hp * 2:hp * 2 + 2, 1, :]), ALU.add)
                if self.hl and not lastlev:
                    hi = mpn[:, :, 1, :]
                    lo = mpn[:, :, 2, :]
                    P.cp(hi, T["PF"][:, :, :], e="pool")
                    P.tt(lo, T["PF"][:, :, :], hi, ALU.subtract)
                cur = 1 - cur
                if lev == 0:
                    P.tt(KBG[:, :, :], KT4, sub(BEG, BEG.t[:, :].unsqueeze(2).to_broadcast([128, 4, 64])), ALU.mult)
                    P.tt(VB[:, :, :], VT4, sub(ABT, beta.ap.unsqueeze(2).to_broadcast([128, 4, 64])), ALU.mult)
                    for h in range(4):
                        r0 = (h % 2) * 64
                        P.tt(QDT[:, h, :], FM["q"][h // 2][r0:r0 + 64, :], EGRW[0][r0:r0 + 64, h, :], ALU.mult,
                             e=("pool" if h % 2 else "dve"))
                if lev == 1:
                    for c in range(2):
                        cs = slice(c * 64, c * 64 + 64)
                        P.tt(KDEC[:, c, :, :], sub(TM, KT4.ap[cs, :, :]),
                             sub(EGL, EGL.t[cs, 0:4].unsqueeze(2).to_broadcast([64, 4, 64])), ALU.mult)
                        P.tt(XDT[:, c, :, :], sub(TM, XT4.ap[cs, :, :]),
                             sub(ABT, dtc.ap[cs, :].unsqueeze(2).to_broadcast([64, 4, 64])), ALU.mult)
                if lev == 2:
                    for c in range(2):
                        cs = slice(c * 64, c * 64 + 64)
                        for h in range(4):
                            gr = h // 2
                            P.ts(BDEC[:, c, h, :], sub(TM, BT.ap[cs, gr * 128:(gr + 1) * 128]), EGL[cs, 4 + h:5 + h], None,
                                 op0=ALU.mult, e=("pool" if h % 2 else "dve"))
                if lev == 3:
                    for h in range(4):
                        P.tt(CDT[:, h, :], FM["c"][h // 2][:, :], EGRW[1][:, h, :], ALU.mult, e=("pool" if h % 2 else "dve"))
                if lev == 4:
                    ssd_step(corder[0])
                if lev == 5:
                    ssd_step(corder[1])
                self._ny += 1
                if self._ny >= scut:
                    return
                yield
            mp = MP[cur]
            pt = self.ps.get()
            if self.hl:
                for h in range(4):
                    P.tr(pt[:, h * 128:(h + 1) * 128], T["PF"][:, h, :], ID)
                P.cp(R[:, :, :], h4(pt), e="act")
            elif self.dI == BF16:
                ptb = pt.t[:, :].bitcast(BF16)
                for h in range(4):
                    P.tr(sub(pt, ptb[:, h * 128:(h + 1) * 128]), mp[:, h, 1, :], IDB[:, :])
                P.cp(R[:, :, :], sub(pt, ptb[:, 0:512].rearrange("p (h i) -> p h i", h=4)), e="act")
            else:
                for h in range(4):
                    P.tr(pt[:, h * 128:(h + 1) * 128], f32v(mp[:, h, 1, :]), ID)
                P.cp(R[:, :, :], h4(pt), e="act")
            self._ny += 1
            if self._ny >= scut:
                return
            yield
            pt = self.ps.get()
            for h in range(4):
                if False:
                    pass
                else:
                    P.mm(pt[0:64, h * 128:(h + 1) * 128], KBG[:, h, :], R[:, h, :])
            P.cp(WT[:, :, :], sub(pt, pt.t[0:64, :].rearrange("p (h i) -> p h i", h=4)), e="act")
            pt = self.ps.get()
            for c in range(2):
                cs = slice(c * 64, c * 64 + 64)
                for h in range(4):
                    oreg = pt[0:64, (c * 4 + h) * 64:(c * 4 + h + 1) * 64]
                    if False:
                        pass
                    else:
                        P.mm(oreg, R[cs, h, cs], VB[cs, h, :])
            P.cp(U[:, :, :, :], sub(pt, pt.t[0:64, :].rearrange("p (c h v) -> p c h v", c=2, h=4)))
            self._ny += 1
            if self._ny >= scut:
                return
            yield
            for c in ((0, 1) if d == 0 else (1, 0)):
                cs = slice(c * 64, c * 64 + 64)
                il = c * 64 + (63 if d == 0 else 0)
                ctok = slice(b * 128 + c * 64, b * 128 + c * 64 + 64)
                ck = b * 2 + c
                first = ck not in touched
                touched.add(ck)
                ogv = V(OGt.t[:, :, ctok], obufs[0][ck])
                oyv = V(OYt.t[:, :, ctok], obufs[1][ck])
                S, Sn, Sb, Sbn = SG[sgi], SG[1 - sgi], SGB[sgi], SGB[1 - sgi]
                sgi = 1 - sgi
                pw = self.ps.get()
                for h in range(4):
                    P.mm(pw[0:64, h * 64:(h + 1) * 64], WT[:, h, cs], Sb[:, h, :])
                P.tt(VN[:, :, :], U[:, c, :, :], sub(pw, pw.t[0:64, 0:256].rearrange("p (h v) -> p h v", h=4)), ALU.subtract)
                self._ny += 1
                if self._ny >= scut:
                    return
                yield
                pk_ = self.ps.get()
                for h in range(4):
                    P.mm(pk_[0:64, h * 64:(h + 1) * 64], KDEC[:, c, h, :], VN[:, h, :])
                po = self.ps.get()
                for h in range(4):
                    r0 = (h % 2) * 64
                    oreg = po[r0:r0 + 64, (h // 2) * 64:(h // 2) * 64 + 64]
                    P.mm(oreg, Sb[:, h, :], QDT[:, h, cs], start=True, stop=False)
                    P.mm(oreg, VN[:, h, :], ATT[:, c, h, :], start=False, stop=True)
                for h in range(4):
                    P.stt(Sn[:, h, :], S[:, h, :], EGRW[0][0:64, h, il:il + 1], pk_[0:64, h * 64:(h + 1) * 64], ALU.mult, ALU.add)
                P.cp(Sbn[:, :, :], Sn[:, :, :], e="pool")
                posrc = sub(po, po.t[:, 0:128].rearrange("p (g i) -> p g i", g=2))
                if first:
                    P.cp(ogv, posrc, e="act")
                else:
                    P.tt(ogv, posrc, ogv, ALU.add)
                self._ny += 1
                if self._ny >= scut:
                    return
                yield
        if not is_dec:
            P.dma(dv(self.osg[sq, l, d].rearrange("h k v -> k h v")), SG[sgi][:, :, :], q="pool")
            pt = self.ps.get()
            for h in range(4):
                P.tr(pt[0:64, h * 128:(h + 1) * 128], SS[st["ssi"]][:, h, :], ID)
            P.cp(STMP[:, :, :], sub(pt, pt.t[0:64, :].rearrange("p (h n) -> p h n", h=4)))
            P.dma(dv(self.oss[sq, l, d].rearrange("h p n -> p h n")), STMP[:, :, :], q="pool")

    def phase_scan(self, l, sq, L, is_dec, TW, NT, NB, ABT, OG, OY):
        P, A = self.P, self.A
        m_sc = A.mark()
        obufs = [[Buf() for _ in range(2 * NB)] for _ in range(2)]
        touched = set()
        if is_dec:
            tsets = [self.scan_tiles(), self.scan_tiles()]
            groups = [[0, 1]]
        else:
            ts1 = self.scan_tiles()
            tsets = [ts1, ts1]
            groups = [[0], [1]]
        for grp in groups:
            gens = [self.scan_dir(d, tsets[d], l, sq, L, is_dec, NB, ABT, OG, OY, obufs, touched) for d in grp]
            while gens:
                for g in list(gens):
                    try:
                        next(g)
                    except StopIteration:
                        gens.remove(g)
        self.barrier()
        A.release(m_sc)
        T_ = lambda shape, name: A.tile(shape, F32, name)
        GT = [A.tile([128, 2, TW], BF16, "gt") for _ in range(3)]
        TA = T_([128, 2, TW], "ta")
        TB = T_([128, 2, TW], "tb")
        RS = T_([128, TW], "rs")
        OB = [A.tile([128, 2, TW], BF16, "ob") for _ in range(2)]
        import os
        for t in range(NT):
            if 'epi' in os.environ.get('K_SKIP', ''):
                break
            ts_ = slice(t * TW, (t + 1) * TW)
            g = GT[0]
            P.dma(g[:, :, :], dv(self.pa[768:1024, ts_].rearrange("(g p) t -> p g t", p=128)))
            P.tt(TA[:, :, :], OG[:, :, ts_], OG[:, :, ts_], ALU.mult)
            ob = OB[0]
            for gi in range(2):
                pt = self.ps.get()
                P.mm(pt[:, 0:TW], self.cst(C_ONESBD), TA[:, gi, :])
                self.rstd_from(RS[:, :], pt[:, 0:TW], 64.0)
                P.stt(TB[:, gi, :], OG[:, gi, ts_], self.pkc(PK_GDNN), RS[:, :], ALU.mult, ALU.mult)
            P.tt(ob[:, :, :], TB[:, :, :], g[:, :, :], ALU.mult)
            P.dma(dv(self.catd[0:256, ts_].rearrange("(g p) t -> p g t", p=128)), ob[:, :, :], q="pool")
            z = GT[1]
            P.dma(z[:, :, :], dv(self.pa[1024:1280, ts_].rearrange("(g p) t -> p g t", p=128)))
            xs = GT[2]
            P.dma(xs[:, :, :], dv(self.pa[1280:1536, ts_].rearrange("(g p) t -> p g t", p=128)))
            for gi in range(2):
                P.stt(TA[:, gi, :], xs[:, gi, :], self.pkc(PK_SSDD + gi), OY[:, gi, ts_], ALU.mult, ALU.add)
            P.tt(TA[:, :, :], TA[:, :, :], z[:, :, :], ALU.mult)
            P.tt(TB[:, :, :], TA[:, :, :], TA[:, :, :], ALU.mult)
            pt = self.ps.get()
            for gi in range(2):
                P.mm(pt[:, 0:TW], self.cst(C_ONES), TB[:, gi, :], start=(gi == 0), stop=(gi == 1))
            self.rstd_from(RS[:, :], pt[:, 0:TW], 256.0)
            ob = OB[1]
            for gi in range(2):
                P.stt(ob[:, gi, :], TA[:, gi, :], self.pkc(PK_SSDN + gi), RS[:, :], ALU.mult, ALU.mult)
            P.dma(dv(self.catd[768:1024, ts_].rearrange("(g p) t -> p g t", p=128)), ob[:, :, :], q="pool")


def _consts():
    C = np.zeros((128, NCONST * 128), np.float32)
    idx = np.arange(128)
    sc = (idx[:, None] // 64) == (idx[None, :] // 64)
    t = idx[:, None]
    i = idx[None, :]

    def put(blk, m):
        C[:, blk * 128:(blk + 1) * 128] = m.astype(np.float32)

    put(C_ID, np.eye(128))
    for d in range(2):
        before = (t < i) if d == 0 else (t > i)
        after = (t > i) if d == 0 else (t < i)
        tri = sc & (before | (t == i))
        put(C_TRIF + d, tri)
        put(C_TRISF + d, sc & after)
        put(C_NEGTF + d, np.where(tri, 0.0, -30000.0))
        put(C_POSSF + d, np.where(sc & after, 0.0, 30000.0))
    put(C_ONESBD, sc)
    put(C_ONES, np.ones((128, 128)))
    return C


def _rope_table(L):
    tt = np.arange(L)
    r = (tt // GRID_W).astype(np.float32)
    col = (tt % GRID_W).astype(np.float32)
    nf = 8
    inv = (np.float32(10000.0) ** (-np.arange(nf, dtype=np.float32) / np.float32(nf))).astype(np.float32)
    ang = np.concatenate([r[:, None] * inv, col[:, None] * inv], axis=-1).astype(np.float32)
    cos, sin = np.cos(ang).astype(np.float32), np.sin(ang).astype(np.float32)
    R = np.zeros((48, 2, L), np.float32)
    R[0:16, 0] = cos.T
    R[32:48, 0] = cos.T
    R[0:16, 1] = -sin.T
    R[32:48, 1] = sin.T
    return R


def _fm(v, n):
    return np.ascontiguousarray(np.asarray(v, np.float32).reshape(n, 128).T)


def _prep_shared(inp, depth, L):
    f = lambda k: np.asarray(inp[k], np.float32)
    w_in = f("w_in")
    win = np.zeros((depth, D, NG_IN * 128), np.float32)
    win[:, :, 0:768] = w_in[:, :, 0:768]
    win[:, :, 768:1024] = w_in[:, :, 768:1024]
    win[:, :, G_CQ * 128:G_CQ * 128 + 256] = w_in[:, :, 1040:1296]
    win[:, :, G_CKV * 128:G_CKV * 128 + 128] = w_in[:, :, 1296:1424]
    win[:, :, G_KR * 128 + 0:G_KR * 128 + 16] = w_in[:, :, 1424:1440]
    win[:, :, G_KR * 128 + 32:G_KR * 128 + 48] = w_in[:, :, 1440:1456]
    win[:, :, G_KRB * 128 + 0:G_KRB * 128 + 16] = w_in[:, :, 1440:1456]
    win[:, :, G_KRB * 128 + 32:G_KRB * 128 + 48] = w_in[:, :, 1424:1440]
    win[:, :, G_AB * 128 + 0:G_AB * 128 + 8] = w_in[:, :, 1024:1032]
    win[:, :, G_AB * 128 + 8:G_AB * 128 + 16] = w_in[:, :, 2480:2488]
    win[:, :, G_AB * 128 + 32:G_AB * 128 + 40] = w_in[:, :, 1032:1040]
    win[:, :, G_Z * 128:G_Z * 128 + 256] = w_in[:, :, 1456:1712]
    win[:, :, G_XS * 128:G_XS * 128 + 768] = w_in[:, :, 1712:2480]
    pk = np.zeros((depth, 128, NPK), np.float32)
    p = np.arange(128)
    for l in range(depth):
        pk[l, :, PK_NMP:PK_NMP + 8] = _fm(f("norm_mix_pre")[l], 8)
        pk[l, :, PK_NMO:PK_NMO + 8] = _fm(f("norm_mix_post")[l], 8)
        pk[l, :, PK_NFP:PK_NFP + 8] = _fm(f("norm_ffn_pre")[l], 8)
        pk[l, :, PK_NFO:PK_NFO + 8] = _fm(f("norm_ffn_post")[l], 8)
        pk[l, :, PK_BADA:PK_BADA + 48] = _fm(f("b_ada")[l], 48)
        for i in range(6):
            pk[l, :, PK_GCONV + i * 5:PK_GCONV + i * 5 + 5] = f("gdn_conv")[l][:, i * 128:(i + 1) * 128].T
            pk[l, :, PK_SCONV + i * 5:PK_SCONV + i * 5 + 5] = f("ssd_conv")[l][:, i * 128:(i + 1) * 128].T
            pk[l, :, PK_SCB + i] = f("ssd_conv_b")[l][i * 128:(i + 1) * 128]
        for cg in range(44):
            pk[l, :, PK_FCONV + cg * 3:PK_FCONV + cg * 3 + 3] = f("ffn_conv")[l][:, cg * 128:(cg + 1) * 128].T
        pk[l, :, PK_QN:PK_QN + 2] = _fm(f("mla_q_norm")[l], 2)
        pk[l, :, PK_KVN] = f("mla_kv_norm")[l]
        pk[l, :, PK_GDNN] = f("gdn_norm")[l][p % 64]
        pk[l, :, PK_SSDN:PK_SSDN + 2] = _fm(f("ssd_norm")[l], 2)
        for gi in range(2):
            pk[l, :, PK_SSDD + gi] = f("ssd_d")[l][2 * gi + p // 64]
        pk[l, 0:8, PK_ALOG] = f("gdn_a_log")[l].reshape(8)
        pk[l, 8:16, PK_ALOG] = f("ssd_a_log")[l].reshape(8)
        pk[l, 0:8, PK_DTB] = f("gdn_dt_bias")[l].reshape(8)
        pk[l, 8:16, PK_DTB] = f("ssd_dt_bias")[l].reshape(8)
    w_uq = f("mla_w_uq")
    wuq = np.zeros((depth, 256, 8 * 128), np.float32)
    wuqb = np.zeros((depth, 256, 8 * 64), np.float32)
    w_ukv = f("mla_w_ukv")
    wuk = np.zeros((depth, 128, 8 * 128), np.float32)
    wuv = np.zeros((depth, 128, 8 * 64), np.float32)
    for h in range(8):
        x1 = w_uq[:, :, h * 96 + 64:h * 96 + 80]
        x2 = w_uq[:, :, h * 96 + 80:h * 96 + 96]
        wuq[:, :, h * 128 + 0:h * 128 + 16] = x1
        wuq[:, :, h * 128 + 32:h * 128 + 48] = x2
        wuq[:, :, h * 128 + 64:h * 128 + 128] = w_uq[:, :, h * 96:h * 96 + 64]
        wuqb[:, :, h * 64 + 0:h * 64 + 16] = x2
        wuqb[:, :, h * 64 + 32:h * 64 + 48] = x1
        wuk[:, :, h * 128 + 64:h * 128 + 128] = w_ukv[:, :, h * 128:h * 128 + 64]
        wuv[:, :, h * 64:(h + 1) * 64] = w_ukv[:, :, h * 128 + 64:h * 128 + 128]
    def grp(w):
        dd, _, gc = w.shape
        g = gc // 128
        return np.ascontiguousarray(w.reshape(dd, 8, 128, g, 128).transpose(0, 3, 2, 1, 4)).reshape(dd, g, 128, 1024)
    return {"wada": grp(f("w_ada")), "win": grp(win), "pk": pk, "wuq": wuq, "wuqb": wuqb, "wuk": wuk, "wuv": wuv,
            "wout": f("w_out"), "wup": grp(f("ffn_w_up")), "wdn": f("ffn_w_down"), "consts": _consts(),
            "rope": _rope_table(L)}


def _in_maps(inp, ncores, depth, npr, L):
    sh = _prep_shared(inp, depth, L)
    f = lambda k: np.asarray(inp[k], np.float32)
    ndec = f("x_sample").shape[0]
    ckr = f("cache_mla_krope")
    ckr48 = np.zeros(ckr.shape[:-1] + (48,), np.float32)
    ckr48[..., 0:16] = ckr[..., 0:16]
    ckr48[..., 32:48] = ckr[..., 16:32]
    maps = []
    for c in range(ncores):
        b = c % ndec
        m = dict(sh)
        m["xp"] = np.ascontiguousarray(f("x_prompt")[c * npr:(c + 1) * npr])
        m["xs"] = np.ascontiguousarray(f("x_sample")[b])
        m["cckv"] = np.ascontiguousarray(f("cache_mla_ckv")[b])
        m["ckr"] = np.ascontiguousarray(ckr48[b])
        m["stg"] = np.ascontiguousarray(f("state_gdn")[b])
        m["sts"] = np.ascontiguousarray(f("state_ssd")[b])
        ct = np.zeros((128, 8, 2), np.float32)
        ct[:, :, 0] = f("c")[b].reshape(8, 128).T
        ct[:, :, 1] = f("c_ctx").reshape(8, 128).T
        m["condT"] = ct
        maps.append(m)
    return maps


def run(inp, ncores=NCORES, depth=DEPTH, seq=SEQ, dec_seq=DEC_SEQ, past=PAST, npr=NPR, debug=None, stop=99, only=None):
    bld = Builder(depth=depth, seq=seq, dec_seq=dec_seq, past=past, npr=npr, debug=debug, stop=stop, only=only)
    maps = _in_maps(inp, ncores, depth, npr, dec_seq)
    res = run_bass_kernel_spmd(bld.nc, maps, core_ids=list(range(ncores)))
    return res.results, bld


def kernel(**inputs):
    res, _ = run(inputs)
    ndec = DEC_BATCH
    y_prompt = np.concatenate([res[c]["yp"] for c in range(NCORES)], axis=0).astype(np.float32)
    y_sample = np.stack([res[b]["ys"] for b in range(ndec)], axis=0).astype(np.float32)
    ckv = np.concatenate([res[c]["ockv"] for c in range(NCORES)], axis=0).astype(np.float32)
    kr = np.concatenate([res[c]["okr"] for c in range(NCORES)], axis=0).astype(np.float32)
    sg = np.concatenate([res[c]["osg"] for c in range(NCORES)], axis=0).astype(np.float32)
    ss = np.concatenate([res[c]["oss"] for c in range(NCORES)], axis=0).astype(np.float32)
    return (y_prompt, y_sample, ckv, kr, sg, ss)
```

```python
import threading
import numpy as np
import concourse.bass as bass
import concourse.mybir as mybir
from concourse.bass_utils import run_bass_kernel_spmd

F32 = mybir.dt.float32
BF16 = mybir.dt.bfloat16
F32R = mybir.dt.float32r
AF = mybir.ActivationFunctionType
ALU = mybir.AluOpType

D = 1024
DEPTH = 2
BATCH = 16
SEQ = 256
DEC_BATCH = 4
DEC_SEQ = 4096
PAST = 256
GRID_W = 64
EPS = 1e-6
DFF = 2816
NCORES = 8
NPR = BATCH // NCORES
NG_IN = 23
G_Q, G_K, G_V, G_GATE, G_CQ, G_CKV, G_KR, G_KRB, G_AB, G_Z, G_XS, G_BM, G_CM = 0, 2, 4, 6, 8, 10, 11, 12, 13, 14, 16, 18, 20
C_ID, C_TRIF, C_TRIB, C_TRISF, C_TRISB, C_NEGTF, C_NEGTB, C_POSSF, C_POSSB, C_ONESBD, C_ONES = range(11)
NCONST = 11
PK_NMP, PK_NMO, PK_NFP, PK_NFO, PK_BADA = 0, 8, 16, 24, 32
PK_GCONV = 80
PK_SCONV = 110
PK_SCB = 140
PK_FCONV = 146
PK_QN = 278
PK_KVN = 280
PK_GDNN = 281
PK_SSDN = 282
PK_SSDD = 284
PK_ALOG = 286
PK_DTB = 287
NPK = 288


class Buf:
    __slots__ = ("w", "r", "excl")

    def __init__(self, excl=False):
        self.w = None
        self.r = {}
        self.excl = excl


class V:
    __slots__ = ("ap", "buf")

    def __init__(self, ap, buf):
        self.ap = ap
        self.buf = buf

    def bitcast(self, dt):
        return V(self.ap.bitcast(dt), self.buf)

    def bc(self, shape):
        return V(self.ap.to_broadcast(shape), self.buf)


class Tl:
    def __init__(self, t, buf=None):
        self.t = t
        self.buf = buf if buf is not None else Buf()

    def __getitem__(self, idx):
        return V(self.t[idx], self.buf)


class Prog:
    CE = ("pe", "dve", "act", "pool")

    def __init__(self, nc):
        self.nc = nc
        self.eng = {"pe": nc.tensor, "dve": nc.vector, "act": nc.scalar, "pool": nc.gpsimd, "sp": nc.sync}
        self.sem = {}
        self.cnt = {}
        self.sid = 0
        for e in self.CE:
            self.sem[e] = self._newsem(e)
        self.dsem = {"sp": [self._newsem("dsp%d" % i) for i in range(20)],
                     "pool": [self._newsem("dpl%d" % i) for i in range(8)]}
        self.drr = {"sp": 0, "pool": 0}
        self.waited = {e: {} for e in self.eng}
        self.ninst = 0
        self.nwait = 0
        self.on_barrier = None
        self.on_op = None
        self.tok = None

    def _newsem(self, name):
        h = self.nc.alloc_semaphore(name)
        s = (self.sid, h)
        self.cnt[self.sid] = 0
        self.sid += 1
        return s

    def _wait(self, e, deps):
        best = {}
        for (s, v) in deps:
            if best.get(s, (None, 0))[1] < v:
                best[s] = (s, v)
        for s, v in best.values():
            if self.waited[e].get(s[0], 0) < v:
                self.eng[e].wait_ge(s[1], v)
                self.waited[e][s[0]] = v
                self.nwait += 1

    def _deps(self, e, reads, writes, pe_acc=False):
        deps = []
        for b in reads:
            if b.w is not None:
                deps.append(b.w)
            if b.excl and e in self.sem:
                me = self.sem[e][0]
                for sid, tok in b.r.items():
                    if sid != me:
                        deps.append(tok)
        for b in writes:
            if b.w is not None:
                if not (pe_acc and b.w[0] is self.sem["pe"]):
                    deps.append(b.w)
            for s, v in b.r.values():
                deps.append((s, v))
        return deps

    def _commit(self, tok, reads, writes):
        if self.tok is not None:
            self.tok[tok[0][0]] = tok
        for b in reads:
            b.r[tok[0][0]] = tok
        for b in writes:
            b.w = tok
            b.r = {}

    def op(self, e, fn, reads, writes, pe_acc=False):
        reads = [v.buf for v in reads if v is not None]
        writes = [v.buf for v in writes if v is not None]
        self._wait(e, self._deps(e, reads, writes, pe_acc))
        inst = fn(self.eng[e])
        s = self.sem[e]
        self.cnt[s[0]] += 1
        inst.then_inc(s[1], 1)
        self.ninst += 1
        self._commit((s, self.cnt[s[0]]), reads, writes)
        if self.on_op is not None:
            self.on_op()

    def dma(self, out, in_, q="sp", slow=False):
        reads = [in_.buf]
        writes = [out.buf]
        sl = self.dsem[q]
        s = sl[self.drr[q] % len(sl)]
        self.drr[q] += 1
        deps = self._deps(q, reads, writes)
        if self.cnt[s[0]] > 0:
            deps.append((s, self.cnt[s[0]]))
        self._wait(q, deps)
        kw = {"allow_slow_non_contiguous": True} if slow else {}
        inst = self.eng[q].dma_start(out=out.ap, in_=in_.ap, **kw)
        self.cnt[s[0]] += 16
        inst.then_inc(s[1], 16)
        self.ninst += 1
        self._commit((s, self.cnt[s[0]]), reads, writes)
        if self.on_op is not None:
            self.on_op()

    def barrier(self, local=False):
        if local and self.tok is not None:
            deps = list(self.tok.values())
        else:
            allsems = [self.sem[e] for e in self.CE] + self.dsem["sp"] + self.dsem["pool"]
            deps = [(s, self.cnt[s[0]]) for s in allsems if self.cnt[s[0]] > 0]
        for e in self.eng:
            self._wait(e, deps)
        if self.on_barrier is not None:
            self.on_barrier(local)

    def mm(self, out, lhsT, rhs, start=True, stop=True):
        self.op("pe", lambda E: E.matmul(out.ap, lhsT=lhsT.ap, rhs=rhs.ap, start=start, stop=stop),
                [lhsT, rhs], [out], pe_acc=not start)

    def tr(self, out, in_, ident):
        self.op("pe", lambda E: E.transpose(out.ap, in_.ap, ident.ap), [in_, ident], [out])

    def act(self, out, in_, func, bias=None, scale=None, accum=None, after=()):
        kw = {}
        rd = [in_] + list(after)
        if bias is not None:
            if isinstance(bias, V):
                kw["bias"] = bias.ap
                rd.append(bias)
            else:
                kw["bias"] = float(bias)
        if scale is not None:
            if isinstance(scale, V):
                kw["scale"] = scale.ap
                rd.append(scale)
            else:
                kw["scale"] = float(scale)
        wr = [out]
        if accum is not None:
            kw["accum_out"] = accum.ap
            wr.append(accum)
        self.op("act", lambda E: E.activation(out=out.ap, in_=in_.ap, func=func, **kw), rd, wr)

    def tt(self, out, in0, in1, op, e="dve"):
        self.op(e, lambda E: E.tensor_tensor(out=out.ap, in0=in0.ap, in1=in1.ap, op=op), [in0, in1], [out])

    def ts(self, out, in0, s1, s2=None, op0=ALU.mult, op1=None, e="dve"):
        rd = [in0]
        a1 = s1
        if isinstance(s1, V):
            a1 = s1.ap
            rd.append(s1)
        a2 = s2
        if isinstance(s2, V):
            a2 = s2.ap
            rd.append(s2)
        kw = {}
        if op1 is not None:
            kw["op1"] = op1
        self.op(e, lambda E: E.tensor_scalar(out=out.ap, in0=in0.ap, scalar1=a1, scalar2=a2, op0=op0, **kw),
                rd, [out])

    def stt(self, out, in0, sc, in1, op0, op1):
        rd = [in0, in1]
        a = sc
        if isinstance(sc, V):
            a = sc.ap
            rd.append(sc)
        self.op("dve", lambda E: E.scalar_tensor_tensor(out=out.ap, in0=in0.ap, scalar=a, in1=in1.ap,
                                                       op0=op0, op1=op1), rd, [out])

    def cp(self, out, in_, e="dve"):
        if e == "act":
            self.op("act", lambda E: E.copy(out=out.ap, in_=in_.ap), [in_], [out])
        else:
            self.op(e, lambda E: E.tensor_copy(out=out.ap, in_=in_.ap), [in_], [out])

    def memset(self, v, val, e="pool"):
        self.op(e, lambda E: E.memset(v.ap, val), [], [v])

    def recip(self, out, in_):
        self.op("dve", lambda E: E.reciprocal(out=out.ap, in_=in_.ap), [in_], [out])


class Arena:
    def __init__(self, nc, lo, hi):
        self.nc = nc
        self.lo = lo
        self.hi = hi
        self.p = lo
        self.n = 0
        self.peak = lo
        self.reg = []

    def mark(self):
        return self.p

    def release(self, m):
        self.p = m

    def tile_at(self, off, shape, dt, name="t"):
        self.n += 1
        t = self.nc.alloc_sbuf_tensor_at("%s_%d" % (name, self.n), list(shape), dt, offset=off)
        tl = Tl(t)
        nb = self.nbytes(shape, dt)
        keep = []
        for (o, n, old) in self.reg:
            if o < off + nb and off < o + n:
                toks = list(old.buf.r.values())
                if old.buf.w is not None:
                    toks.append(old.buf.w)
                for tok in toks:
                    sid = tok[0][0]
                    if tl.buf.r.get(sid, (None, 0))[1] < tok[1]:
                        tl.buf.r[sid] = tok
                if not (off <= o and o + n <= off + nb):
                    keep.append((o, n, old))
            else:
                keep.append((o, n, old))
        keep.append((off, nb, tl))
        self.reg = keep
        return tl

    @staticmethod
    def nbytes(shape, dt):
        n = 1
        for s in shape[1:]:
            n *= s
        return (n * (2 if dt == BF16 else 4) + 31) // 32 * 32

    def tile(self, shape, dt, name="t"):
        nb = self.nbytes(shape, dt)
        off = self.p
        assert off + nb <= self.hi, "SBUF arena overflow %s %d+%d>%d" % (name, off, nb, self.hi)
        self.p += nb
        self.peak = max(self.peak, self.p)
        return self.tile_at(off, shape, dt, name)


class PsumPool:
    def __init__(self, nc=None, banks=None):
        if banks is None:
            banks = [Tl(nc.alloc_psum_tensor("psb%d" % i, [128, 512], F32), Buf(excl=True)) for i in range(8)]
        self.banks = banks
        self.res = set()
        self.i = 0

    def get(self):
        while True:
            k = self.i % len(self.banks)
            self.i += 1
            if k not in self.res:
                return self.banks[k]

    def reserve(self):
        b = self.get()
        self.res.add(self.banks.index(b))
        return b

    def free(self, b):
        self.res.discard(self.banks.index(b))


def dv(ap):
    return V(ap, Buf())


def sub(tl, ap):
    return V(ap, tl.buf)


class Ctx:
    pass


CTX_NAMES = ("A", "ps", "GWT", "GWTMP", "gw_key", "pa", "qt", "ckvd", "krd", "catd", "actT",
             "_oi", "_wi", "_obi", "_ny", "dI", "hl")


class Coop:
    def __init__(self):
        self.evs = []
        self.alive = []
        self.cur = 0
        self.exc = None
        self.on_resume = None

    def run(self, fns):
        n = len(fns)
        self.evs = [threading.Event() for _ in range(n)]
        self.alive = [True] * n
        done = threading.Event()

        def wrap(i, fn):
            self.evs[i].wait()
            try:
                fn()
            except BaseException as e:
                self.exc = e
            self.alive[i] = False
            nxt = self._next(i)
            if nxt is None:
                done.set()
            else:
                self.cur = nxt
                self.evs[nxt].set()

        ths = [threading.Thread(target=wrap, args=(i, f)) for i, f in enumerate(fns)]
        for t in ths:
            t.start()
        self.cur = 0
        self.evs[0].set()
        done.wait()
        for t in ths:
            t.join()
        self.evs = []
        if self.exc is not None:
            raise self.exc

    def _next(self, i):
        n = len(self.alive)
        for k in range(1, n + 1):
            j = (i + k) % n
            if self.alive[j] and j != i:
                return j
        return None

    def switch(self):
        if not self.evs:
            return
        i = self.cur
        nxt = self._next(i)
        if nxt is None:
            return
        self.evs[i].clear()
        self.cur = nxt
        self.evs[nxt].set()
        self.evs[i].wait()
        if self.on_resume is not None:
            self.on_resume()


class Builder:
    def __getattr__(self, name):
        if name in CTX_NAMES:
            return getattr(self.__dict__["_tls"].ctx, name)
        raise AttributeError(name)

    def __setattr__(self, name, val):
        if name in CTX_NAMES:
            setattr(self.__dict__["_tls"].ctx, name, val)
        else:
            self.__dict__[name] = val

    def use_ctx(self, ctx):
        self._tls.ctx = ctx
        self.P.tok = ctx.tokens

    def barrier(self):
        ctx = self._tls.ctx
        self.P.tok = ctx.tokens
        if self.coop.evs:
            self.P.barrier(local=True)
        else:
            self.P.barrier()

    def cswitch(self):
        self.coop.switch()

    def _tick(self):
        self._tick_n += 1
        if self._tick_n % 24 == 0:
            self.coop.switch()

    def __init__(self, depth=DEPTH, seq=SEQ, dec_seq=DEC_SEQ, past=PAST, npr=NPR, debug=None, stop=99, only=None):
        self.depth, self.seq, self.dec_seq, self.past, self.npr = depth, seq, dec_seq, past, npr
        self.stop, self.only = stop, only
        nc = bass.Bass("TRN2", target_bir_lowering=False)
        self.nc = nc
        self.P = Prog(nc)
        dt = nc.dram_tensor
        L, S = dec_seq, seq
        I = lambda name, shape, d=F32: dt(name, list(shape), d, kind="ExternalInput").ap()
        O = lambda name, shape, d=F32: dt(name, list(shape), d, kind="ExternalOutput").ap()
        X = lambda name, shape, d=F32: dt(name, list(shape), d, kind="Internal").ap()
        self.xp = I("xp", [npr, S, D])
        self.xs = I("xs", [L, D])
        self.cckv = I("cckv", [depth, past, 128])
        self.ckr = I("ckr", [depth, past, 48])
        self.stg = I("stg", [depth, 2, 4, 64, 64])
        self.sts = I("sts", [depth, 2, 4, 64, 128])
        self.condT = I("condT", [128, 8, 2])
        self.wada = I("wada", [depth, 48, 128, 8 * 128])
        self.win = I("win", [depth, NG_IN, 128, 8 * 128])
        self.pk = I("pk", [depth, 128, NPK])
        self.wuq = I("wuq", [depth, 256, 8 * 128])
        self.wuqb = I("wuqb", [depth, 256, 8 * 64])
        self.wuk = I("wuk", [depth, 128, 8 * 128])
        self.wuv = I("wuv", [depth, 128, 8 * 64])
        self.wout = I("wout", [depth, D, D])
        self.wup = I("wup", [depth, 44, 128, 8 * 128])
        self.wdn = I("wdn", [depth, DFF, D])
        self.consts = I("consts", [128, NCONST * 128])
        self.rope = I("rope", [48, 2, L])
        self.yp = O("yp", [npr, S, D])
        self.ys = O("ys", [L, D])
        self.ockv = O("ockv", [npr, depth, S, 128])
        self.okr = O("okr", [npr, depth, S, 32])
        self.osg = O("osg", [npr, depth, 2, 4, 64, 64])
        self.oss = O("oss", [npr, depth, 2, 4, 64, 128])
        self._tls = threading.local()
        self.coop = Coop()
        self.xres = X("xres", [L, D])

        def scratch(tag, Ls):
            return {"pa": X("pa" + tag, [2048, Ls], BF16),
                    "qt": X("qt" + tag, [1024, Ls], BF16),
                    "ckvd": X("ckvd" + tag, [128, Ls + past], BF16),
                    "krd": X("krd" + tag, [48, Ls + past], BF16),
                    "catd": X("catd" + tag, [1024, Ls], BF16),
                    "actT": X("actT" + tag, [DFF, Ls], BF16)}
        self.scr_main = scratch("", L)
        self.scr_p = [scratch("_p%d" % i, S) for i in range(npr)]
        self.dbg = {}
        if debug:
            self.dbg = {k: O("dbg_" + k, shp) for k, shp in debug.items()}
        lo = (nc.sbuf_base + 31) // 32 * 32
        self.arenas = []
        self.main_ctx = self.make_ctx(Arena(nc, lo, nc.sbuf_top // 32 * 32), PsumPool(nc), self.scr_main)
        self.use_ctx(self.main_ctx)
        self.P.on_barrier = lambda local: ([self._tls.ctx.A.reg.clear()] if local else [a.reg.clear() for a in self.arenas])
        self._tick_n = 0
        self.P.on_op = self._tick
        self.coop.on_resume = lambda: setattr(self.P, "tok", self._tls.ctx.tokens)
        self.build()

    def make_ctx(self, arena, ps, scr):
        c = Ctx()
        c.A, c.ps = arena, ps
        self.arenas.append(arena)
        for k, v in scr.items():
            setattr(c, k, v)
        c.tokens = {}
        c.gw_key = None
        c.GWT = c.GWTMP = None
        c._oi = c._wi = c._obi = c._ny = 0
        c.dI, c.hl = None, False
        return c

    def cst(self, blk, r0=0, r1=128, c0=0, c1=128):
        return self.CON[r0:r1, blk * 128 + c0:blk * 128 + c1]

    def pkc(self, col, r0=0, r1=128):
        return self.PK[r0:r1, col:col + 1]

    def rstd_from(self, out, in_, n, rows=128):
        P = self.P
        P.act(out, in_, AF.Ln, bias=self.EPSB[0:rows, :], scale=1.0 / n)
        P.act(out, out, AF.Exp, scale=-0.5)

    def build(self):
        P, A = self.P, self.A
        self.CON = A.tile([128, NCONST * 128], F32, "con")
        P.dma(self.CON[:, :], dv(self.consts[:, :]))
        self.IDB = A.tile([128, 128], BF16, "idb")
        P.cp(self.IDB[:, :], self.cst(C_ID))
        self.ONESB = A.tile([128, 64], BF16, "onesb")
        P.memset(self.ONESB[:, :], 1.0)
        self.EPSB = A.tile([128, 1], F32, "epsb")
        P.memset(self.EPSB[:, :], EPS)
        self.ONE1 = A.tile([128, 1], F32, "one1")
        P.memset(self.ONE1[:, :], 1.0)
        self.ZERO1 = A.tile([128, 1], F32, "zero1")
        P.memset(self.ZERO1[:, :], 0.0)
        self.CONDS = A.tile([128, 8, 2], F32, "conds")
        ct = A.tile([128, 8, 2], F32, "condraw")
        P.dma(ct[:, :, :], dv(self.condT[:, :, :]))
        P.act(self.CONDS[:, :, :], ct[:, :, :], AF.Silu)
        self.PK = A.tile([128, NPK], F32, "pk")
        self.MOD = A.tile([128, 48, 2], F32, "mod")
        self.MODV = A.tile([128, 6, 8, 2], F32, "modv")
        self.GWT = A.tile([128, D], F32, "gwt")
        self.GWTMP = A.tile([128, 128], F32, "gwtmp")
        self.gw_key = None
        base_mark = A.mark()
        npr = self.npr
        span = (A.hi - base_mark) // npr // 32 * 32
        banks = self.main_ctx.ps.banks
        nb = 8 // npr
        pctx = []
        for i in range(npr):
            c = self.make_ctx(Arena(self.nc, base_mark + i * span, base_mark + (i + 1) * span),
                              PsumPool(banks=banks[i * nb:(i + 1) * nb]), self.scr_p[i])
            pctx.append(c)
        for c in pctx:
            self.use_ctx(c)
            self.GWT = c.A.tile([128, D], F32, "gwt")
            self.GWTMP = c.A.tile([128, 128], F32, "gwtmp")
            c.base = c.A.mark()
        self.use_ctx(self.main_ctx)
        for l in range(self.depth):
            A.release(base_mark)
            P.dma(self.PK[:, :], dv(self.pk[l]))
            self.adaln(l)
            for c in [self.main_ctx] + pctx:
                c.gw_key = None
            last = (l == self.depth - 1)
            if self.only != "dec":
                def mk(i):
                    def fn():
                        self.use_ctx(pctx[i])
                        pctx[i].A.release(pctx[i].base)
                        self.layer_seq(l, i, self.seq, False, last)
                    return fn
                self.coop.run([mk(i) for i in range(npr)])
                self.use_ctx(self.main_ctx)
                self.barrier()
            if self.only == "pr":
                continue
            m = A.mark()
            self.layer_seq(l, -1, self.dec_seq, True, last)
            self.barrier()
            A.release(m)
        self.barrier()

    def adaln(self, l):
        P, A = self.P, self.A
        m = A.mark()
        pt = self.ps.get()
        wts = [A.tile([128, 8, 128], F32, "wada") for _ in range(3)]
        for c in range(48):
            wt = wts[c % 3]
            P.dma(sub(wt, wt.t[:, :, :].rearrange("p k c -> p (k c)")), dv(self.wada[l, c]))
            for k in range(8):
                P.mm(pt[:, c * 2:c * 2 + 2], wt[:, k, :], self.CONDS[:, k, :], start=(k == 0), stop=(k == 7))
        P.tt(self.MOD[:, :, :], sub(pt, pt.t[:, 0:96].rearrange("p (c j) -> p c j", j=2)),
             sub(self.PK, self.PK.t[:, PK_BADA:PK_BADA + 48].unsqueeze(2).to_broadcast([128, 48, 2])), ALU.add)
        for half, (nw, nwo) in enumerate(((PK_NMP, PK_NMO), (PK_NFP, PK_NFO))):
            sh = self.MOD[:, (half * 3 + 0) * 8:(half * 3 + 1) * 8, :]
            sc = self.MOD[:, (half * 3 + 1) * 8:(half * 3 + 2) * 8, :]
            g = self.MOD[:, (half * 3 + 2) * 8:(half * 3 + 3) * 8, :]
            nwb = sub(self.PK, self.PK.t[:, nw:nw + 8].unsqueeze(2).to_broadcast([128, 8, 2]))
            nwob = sub(self.PK, self.PK.t[:, nwo:nwo + 8].unsqueeze(2).to_broadcast([128, 8, 2]))
            P.stt(self.MODV[:, half * 3 + 0, :, :], sc, 1.0, nwb, ALU.add, ALU.mult)
            P.cp(self.MODV[:, half * 3 + 1, :, :], sh)
            P.tt(self.MODV[:, half * 3 + 2, :, :], g, nwob, ALU.mult)
        self.gw_key = None
        self.barrier()
        A.release(m)

    def layer_seq(self, l, sq, L, is_dec, last):
        P, A = self.P, self.A
        who = 0 if is_dec else 1
        TW = min(512, L)
        NT = L // TW
        NB = L // 128
        NK = L + (self.past if is_dec else 0)
        if is_dec:
            x_in = self.xs if l == 0 else self.xres
            x_mid = self.xres
            x_out = self.ys if last else self.xres
        else:
            x_in = self.xp[sq] if l == 0 else self.yp[sq]
            x_mid = self.yp[sq]
            x_out = self.yp[sq]
        hreg = A.mark()
        hT = [A.tile([128, 8, TW], BF16, "hT") for _ in range(NT)]
        OG = A.tile_at(hreg, [128, 2, L], F32, "og")
        OY = A.tile_at(hreg + A.nbytes([128, 2, L], F32), [128, 2, L], F32, "oy")
        ABT = A.tile([128, NB, 80], F32, "abt")

        if self.stop < 0:
            return
        m0 = A.mark()
        xt = [A.tile([128, D], F32, "xt") for _ in range(2)]
        for b in range(NB):
            x = xt[b % 2]
            P.dma(x[:, :], dv(x_in[b * 128:(b + 1) * 128, :]))
            self.norm_to_hT(x, hT, b, TW, 0, who)
        self.barrier()
        A.release(m0)
        if self.stop < 1:
            return
        m0 = A.mark()
        self.phase_a(l, sq, L, is_dec, hT, TW, NT, NB, NK, ABT)
        self.barrier()
        A.release(m0)
        if self.stop < 2:
            return
        m0 = A.mark()
        self.phase_scan(l, sq, L, is_dec, TW, NT, NB, ABT, OG, OY)
        self.barrier()
        A.release(m0)
        if self.stop < 3:
            return
        m0 = A.mark()
        self.phase_attn(l, L, is_dec, TW, NT, NK)
        self.barrier()
        A.release(m0)
        if self.stop < 4:
            return
        m0 = A.mark()
        wo = A.tile([128, 8, D], BF16, "wout")
        self.load_w_bf16(wo, self.wout[l], 8, D)
        xt = [A.tile([128, D], F32, "xt") for _ in range(2)]
        ct = [A.tile([128, 8, 128], BF16, "ct") for _ in range(2)]
        for b in range(NB):
            x = xt[b % 2]
            c = ct[b % 2]
            P.dma(x[:, :], dv(x_in[b * 128:(b + 1) * 128, :]))
            P.dma(c[:, :, :], dv(self.catd[:, b * 128:(b + 1) * 128].rearrange("(k p) t -> p k t", p=128)))
            import os
            cut2 = int(os.environ.get("K_CUT2", "99"))
            self.gw_rows(2, who)
            pss = [self.ps.get(), self.ps.get()]
            for hh in range(2):
                for k in range(8):
                    P.mm(pss[hh][:, :], c[:, k, :], wo[:, k, hh * 512:(hh + 1) * 512], start=(k == 0), stop=(k == 7))
            if cut2 < 1:
                continue
            self.residual(pss, x, 2, who, x_mid, b)
            if cut2 < 5:
                continue
            self.norm_to_hT(x, hT, b, TW, 3, who)
        self.barrier()
        A.release(m0)
        if self.stop < 5:
            return
        m0 = A.mark()
        wd = A.tile([128, 22, D], BF16, "wdn")
        wst = [A.tile([128, 512], F32, "wst") for _ in range(4)]
        pieces = [(k, c0) for k in range(22) for c0 in (0, 512)]
        pstate = {"i": 0}

        def prefetch_wd(n):
            for _ in range(n):
                i = pstate["i"]
                if i >= len(pieces):
                    return
                k, c0 = pieces[i]
                st_ = wst[i % 4]
                P.dma(st_[:, :], dv(self.wdn[l][k * 128:(k + 1) * 128, c0:c0 + 512]))
                P.cp(wd[:, k, c0:c0 + 512], st_[:, :], e=("dve" if i % 2 else "act"))
                pstate["i"] = i + 1
        m1 = A.mark()
        self.phase_ffn_up(l, L, hT, TW, NT, hook=prefetch_wd)
        prefetch_wd(len(pieces))
        self.barrier()
        A.release(m1)
        if self.stop < 6:
            return
        xt = [A.tile([128, D], F32, "xt") for _ in range(2)]
        at = [A.tile([128, 22, 128], BF16, "at") for _ in range(2)]
        for b in range(NB):
            x = xt[b % 2]
            a = at[b % 2]
            P.dma(x[:, :], dv(x_mid[b * 128:(b + 1) * 128, :]))
            P.dma(a[:, :, :], dv(self.actT[:, b * 128:(b + 1) * 128].rearrange("(j p) t -> p j t", p=128)))
            self.gw_rows(5, who)
            pss = [self.ps.get(), self.ps.get()]
            for hh in range(2):
                for j in range(22):
                    P.mm(pss[hh][:, :], a[:, j, :], wd[:, j, hh * 512:(hh + 1) * 512], start=(j == 0), stop=(j == 21))
            self.residual(pss, x, 5, who, x_out, b)
        self.barrier()
        A.release(m0)
        A.release(hreg)

    def load_w_bf16(self, dst, src, nk, ncols, engs=("dve", "act")):
        P, A = self.P, self.A
        cw = min(ncols, 512)
        st = [A.tile([128, cw], F32, "wst") for _ in range(3)]
        i = 0
        for k in range(nk):
            for c0 in range(0, ncols, cw):
                s = st[i % 3]
                P.dma(s[:, :], dv(src[k * 128:(k + 1) * 128, c0:c0 + cw]))
                P.cp(dst[:, k, c0:c0 + cw], s[:, :], e=engs[i % len(engs)])
                i += 1

    def load_wg(self, dst, src, stage, e):
        P = self.P
        P.dma(sub(stage, stage.t[:, :, :].rearrange("p k c -> p (k c)")), dv(src))
        P.cp(dst[:, :, :], stage[:, :, :], e=e)

    def norm_to_hT(self, x, hT, b, TW, mv, who):
        P, A = self.P, self.A
        m = A.mark()
        junk = A.tile([128, D], BF16, "junk")
        ssq = A.tile([128, 1], F32, "ssq")
        rstd = A.tile([128, 1], F32, "rstd")
        xb = A.tile([128, D], BF16, "xb")
        import os
        cut = int(os.environ.get("K_CUT", "99"))
        P.act(junk[:, :], x[:, :], AF.Square, scale=float(D) ** -0.5, accum=ssq[:, :])
        if cut < 1:
            A.release(m); return
        self.rstd_from(rstd[:, :], ssq[:, :], 1.0)
        if cut < 2:
            A.release(m); return
        P.ts(xb[:, :], x[:, :], rstd[:, :], None, op0=ALU.mult)
        if cut < 3:
            A.release(m); return
        pt = self.ps.get()
        ptb = pt.t[:, :].bitcast(BF16)
        for k in range(8):
            P.tr(sub(pt, ptb[:, k * 128:(k + 1) * 128]), xb[:, k * 128:(k + 1) * 128], self.IDB[:, :])
        tt_, off = b * 128 // TW, (b * 128) % TW
        if cut < 4:
            A.release(m); return
        for k in range(8):
            src = sub(pt, ptb[:, k * 128:(k + 1) * 128])
            dst = hT[tt_][:, k, off:off + 128]
            if (k % 2 == 0 or cut == 4) and cut != 5:
                P.ts(dst, src, self.MODV[:, mv, k, who:who + 1], self.MODV[:, mv + 1, k, who:who + 1],
                     op0=ALU.mult, op1=ALU.add)
            else:
                P.act(dst, src, AF.Identity, bias=self.MODV[:, mv + 1, k, who:who + 1],
                      scale=self.MODV[:, mv, k, who:who + 1])
        A.release(m)

    def residual(self, pss, x, mv, who, x_dst, b):
        P, A = self.P, self.A
        m = A.mark()
        GW = self.gw_rows(mv, who)
        junk = A.tile([128, 512], F32, "junk")
        ss = A.tile([128, 2], F32, "ss")
        rstd = A.tile([128, 1], F32, "rstd")
        t = A.tile([128, D], F32, "t")
        import os
        cut2 = int(os.environ.get("K_CUT2", "99"))
        for hh in range(2):
            P.act(junk[:, :], pss[hh][:, :], AF.Square, scale=float(D) ** -0.5, accum=ss[:, hh:hh + 1])
        if cut2 < 2:
            A.release(m); return
        P.tt(rstd[:, :], ss[:, 0:1], ss[:, 1:2], ALU.add)
        self.rstd_from(rstd[:, :], rstd[:, :], 1.0)
        for hh in range(2):
            P.tt(t[:, hh * 512:(hh + 1) * 512], pss[hh][:, :], GW[:, hh * 512:(hh + 1) * 512], ALU.mult)
        if cut2 < 3:
            A.release(m); return
        P.stt(x[:, :], t[:, :], rstd[:, :], x[:, :], ALU.mult, ALU.add)
        if cut2 < 4:
            A.release(m); return
        P.dma(dv(x_dst[b * 128:(b + 1) * 128, :]), x[:, :], q="pool")
        A.release(m)

    def gw_rows(self, mv, who):
        if self.gw_key == (mv, who):
            return self.GWT
        P = self.P
        for k in range(8):
            P.ts(self.GWTMP[:, :], self.cst(C_ONES), self.MODV[:, mv, k, who:who + 1], None, op0=ALU.mult)
            pt = self.ps.get()
            P.mm(pt[:, 0:128], self.GWTMP[:, :], self.cst(C_ID))
            P.cp(self.GWT[:, k * 128:(k + 1) * 128], pt[:, 0:128])
        self.gw_key = (mv, who)
        return self.GWT
    def phase_a(self, l, sq, L, is_dec, hT, TW, NT, NB, NK, ABT):
        P, A = self.P, self.A
        stg = [A.tile([128, 8, 128], F32, "wstg") for _ in range(2)]
        wgs = [A.tile([128, 8, 128], BF16, "wg") for _ in range(3)]
        RAW = [A.tile([128, L + 4], BF16, "raw") for _ in range(2)]
        for r in RAW:
            P.memset(r[:, 0:2], 0.0)
            P.memset(r[:, L + 2:L + 4], 0.0)
        DG = A.tile([128, 5, 128], BF16, "dg")
        OUTS = [A.tile([128, TW], F32, "outs") for _ in range(3)]
        TMP = [A.tile([128, TW], F32, "tmpa") for _ in range(3)]
        CQRAW = A.tile([128, 2, L], F32, "cqraw")
        self._oi = 0
        self._wi = 0

        OUTB = [A.tile([128, TW], BF16, "outb") for _ in range(3)]
        self._obi = 0

        def nxt_out():
            self._oi += 1
            return OUTS[self._oi % 3]

        def nxt_outb():
            self._obi += 1
            return OUTB[self._obi % 3]

        def load_group(g, ncol=128):
            self._wi += 1
            w = wgs[self._wi % 3]
            self.load_wg(w, self.win[l, g], stg[self._wi % 2], "act" if self._wi % 2 else "dve")
            return w

        def proj(w, t, ncol=128):
            pt = self.ps.get()
            for k in range(8):
                P.mm(pt[0:ncol, 0:TW], w[:, k, 0:ncol], hT[t][:, k, :], start=(k == 0), stop=(k == 7))
            return pt

        def store_pa(row0, t, src):
            P.dma(dv(self.pa[row0:row0 + 128, t * TW:(t + 1) * TW]), src, q="pool")

        convs = []
        for i in range(6):
            kind = "qk" if i < 4 else "plain"
            convs.append((G_Q + i, PK_GCONV + i * 5, None, kind, i * 128, 0.125 if i < 2 else 1.0))
        for i in range(6):
            convs.append((G_XS + i, PK_SCONV + i * 5, PK_SCB + i, "plain", 1280 + i * 128, 1.0))
        for ci, (g, ccol, bcol, kind, row0, qs) in enumerate(convs):
            w = load_group(g)
            raw = RAW[ci % 2]
            for t in range(NT):
                pt = proj(w, t)
                if t % 2 == 0:
                    P.cp(raw[:, 2 + t * TW:2 + (t + 1) * TW], pt[:, 0:TW], e="act")
                else:
                    P.cp(raw[:, 2 + t * TW:2 + (t + 1) * TW], pt[:, 0:TW], e="dve")
            for j in range(5):
                P.ts(DG[:, j, :], self.IDB[:, :], self.pkc(ccol + j), None, op0=ALU.mult)
            for t in range(NT):
                pt = self.ps.get()
                for j in range(5):
                    P.mm(pt[:, 0:TW], DG[:, j, :], raw[:, t * TW + j:t * TW + j + TW], start=(j == 0), stop=(j == 4))
                ob_ = nxt_outb()
                if kind == "qk":
                    o = nxt_out()
                    P.act(o[:, :], pt[:, 0:TW], AF.Silu)
                    sqt = TMP[0]
                    P.tt(sqt[:, :], o[:, :], o[:, :], ALU.mult)
                    p2 = self.ps.get()
                    P.mm(p2[:, 0:TW], self.cst(C_ONESBD), sqt[:, :])
                    rs = TMP[1]
                    P.act(rs[:, :], p2[:, 0:TW], AF.Ln, bias=self.EPSB[:, :])
                    P.act(rs[:, :], rs[:, :], AF.Exp, scale=-0.5)
                    P.stt(ob_[:, :], o[:, :], qs, rs[:, :], ALU.mult, ALU.mult)
                elif bcol is None:
                    P.act(ob_[:, :], pt[:, 0:TW], AF.Silu)
                else:
                    P.act(ob_[:, :], pt[:, 0:TW], AF.Silu, bias=self.pkc(bcol))
                store_pa(row0, t, ob_[:, :])
        import os
        cut3 = int(os.environ.get("K_CUT3", "99"))
        if cut3 < 1:
            return
        for (g, row0) in ((G_GATE, 768), (G_GATE + 1, 896), (G_Z, 1024), (G_Z + 1, 1152)):
            w = load_group(g)
            for t in range(NT):
                pt = proj(w, t)
                ob_ = nxt_outb()
                P.act(ob_[:, :], pt[:, 0:TW], AF.Silu)
                store_pa(row0, t, ob_[:, :])
        if cut3 < 2:
            return
        w = load_group(G_AB)
        NEA = A.tile([16, 1], F32, "nea")
        P.act(NEA[:, :], self.pkc(PK_ALOG, 0, 16), AF.Exp)
        P.ts(NEA[:, :], NEA[:, :], -1.0, None, op0=ALU.mult)
        ABF = A.tile([128, TW], F32, "abf")
        P.memset(ABF[:, :], 0.0)
        for t in range(NT):
            pt = proj(w, t)
            e1 = TMP[0]
            P.act(e1[0:16, :], pt[0:16, 0:TW], AF.Exp, bias=self.pkc(PK_DTB, 0, 16))
            P.act(ABF[64:80, :], e1[0:16, :], AF.Ln, bias=self.ONE1[0:16, :])
            P.act(e1[0:16, :], e1[0:16, :], AF.Ln, bias=self.ONE1[0:16, :])
            P.ts(ABF[0:16, :], e1[0:16, :], NEA[:, :], None, op0=ALU.mult)
            P.act(ABF[32:40, :], pt[32:40, 0:TW], AF.Sigmoid)
            for s in range(TW // 128):
                b = t * (TW // 128) + s
                p2 = self.ps.get()
                P.tr(p2[:, 0:80], ABF[0:80, s * 128:(s + 1) * 128], self.cst(C_ID, 0, 80, 0, 80))
                P.cp(ABT[:, b, :], p2[:, 0:80])
        if cut3 < 3:
            return
        w = load_group(G_CKV)
        for t in range(NT):
            pt = proj(w, t)
            sqt = TMP[0]
            P.act(sqt[:, :], pt[:, 0:TW], AF.Square)
            p2 = self.ps.get()
            P.mm(p2[:, 0:TW], self.cst(C_ONES), sqt[:, :])
            rs = TMP[1]
            self.rstd_from(rs[:, :], p2[:, 0:TW], 128.0)
            o = nxt_out()
            P.stt(o[:, :], pt[:, 0:TW], self.pkc(PK_KVN), rs[:, :], ALU.mult, ALU.mult)
            ob = TMP[2]
            obb = sub(ob, ob.t[:, 0:TW // 2].bitcast(BF16))
            P.cp(obb, o[:, :], e="act")
            P.dma(dv(self.ckvd[:, t * TW:(t + 1) * TW]), obb, q="pool")
            if not is_dec:
                for s in range(TW // 128):
                    b = t * (TW // 128) + s
                    p3 = self.ps.get()
                    P.tr(p3[:, 0:128], o[:, s * 128:(s + 1) * 128], self.cst(C_ID))
                    o3 = nxt_out()
                    P.cp(o3[:, 0:128], p3[:, 0:128])
                    P.dma(dv(self.ockv[sq, l, b * 128:(b + 1) * 128, :]), o3[:, 0:128], q="pool")
        if is_dec:
            for s in range(self.past // 128):
                c = TMP[0]
                P.dma(c[:, 0:128], dv(self.cckv[l, s * 128:(s + 1) * 128, :]))
                p3 = self.ps.get()
                P.tr(p3[:, 0:128], c[:, 0:128], self.cst(C_ID))
                ob = TMP[2]
                obb = sub(ob, ob.t[:, 0:64].bitcast(BF16))
                P.cp(obb, p3[:, 0:128])
                P.dma(dv(self.ckvd[:, L + s * 128:L + (s + 1) * 128]), obb, q="pool")
        if cut3 < 4:
            return
        wa = load_group(G_KR, 48)
        wb = load_group(G_KRB, 48)
        ROP = [A.tile([48, 2, TW], F32, "rop") for _ in range(2)]
        for t in range(NT):
            pa_ = proj(wa, t, 48)
            ob = TMP[2]
            obb = sub(ob, ob.t[0:48, 0:TW // 2].bitcast(BF16))
            if is_dec:
                pb_ = proj(wb, t, 48)
                rp = ROP[t % 2]
                P.dma(rp[:, :, :], dv(self.rope[:, :, t * TW:(t + 1) * TW]))
                t1 = TMP[0]
                t2 = TMP[1]
                P.tt(t1[0:48, :], pa_[0:48, 0:TW], rp[:, 0, :], ALU.mult)
                P.tt(t2[0:48, :], pb_[0:48, 0:TW], rp[:, 1, :], ALU.mult)
                P.tt(obb, t1[0:48, :], t2[0:48, :], ALU.add)
            else:
                o = nxt_out()
                P.cp(o[0:48, :], pa_[0:48, 0:TW])
                P.cp(obb, o[0:48, :], e="act")
                for s in range(TW // 128):
                    b = t * (TW // 128) + s
                    p3 = self.ps.get()
                    P.tr(p3[:, 0:48], o[0:48, s * 128:(s + 1) * 128], self.cst(C_ID, 0, 48, 0, 48))
                    o3 = nxt_out()
                    P.cp(o3[:, 0:48], p3[:, 0:48])
                    P.dma(dv(self.okr[sq, l, b * 128:(b + 1) * 128, 0:16]), o3[:, 0:16], q="pool")
                    P.dma(dv(self.okr[sq, l, b * 128:(b + 1) * 128, 16:32]), o3[:, 32:48], q="pool")
            P.dma(dv(self.krd[:, t * TW:(t + 1) * TW]), obb, q="pool")
        skip = os.environ.get('K_SKIP', '').split(',')
        if is_dec and 'ctxkr' not in skip:
            for s in range(self.past // 128):
                c = TMP[0]
                P.dma(c[:, 0:48], dv(self.ckr[l, s * 128:(s + 1) * 128, :]))
                p3 = self.ps.get()
                P.tr(p3[0:64, 0:128], c[:, 0:64], self.cst(C_ID))
                ob = TMP[2]
                obb = sub(ob, ob.t[0:48, 0:64].bitcast(BF16))
                P.cp(obb, p3[0:48, 0:128])
                P.dma(dv(self.krd[:, L + s * 128:L + (s + 1) * 128]), obb, q="pool")
        if cut3 < 5:
            return
        for i in range(2):
            w = load_group(G_CQ + i)
            for t in range(NT):
                pt = proj(w, t)
                P.cp(CQRAW[:, i, t * TW:(t + 1) * TW], pt[:, 0:TW], e=("act" if t % 2 else "dve"))
        WQ = A.tile([128, 2, 1024], BF16, "wq")
        WQB = A.tile([128, 2, 512], BF16, "wqb")
        self.load_w_bf16(WQ, self.wuq[l], 2, 1024)
        self.load_w_bf16(WQB, self.wuqb[l], 2, 512)
        CQN = A.tile([128, 2, TW], BF16, "cqn")
        QO = [A.tile([128, TW], BF16, "qo") for _ in range(2)]
        for t in range(NT):
            p2 = self.ps.get()
            for i in range(2):
                sqt = TMP[i]
                P.act(sqt[:, :], CQRAW[:, i, t * TW:(t + 1) * TW], AF.Square)
                P.mm(p2[:, 0:TW], self.cst(C_ONES), sqt[:, :], start=(i == 0), stop=(i == 1))
            rs = TMP[2]
            self.rstd_from(rs[:, :], p2[:, 0:TW], 256.0)
            for i in range(2):
                P.stt(CQN[:, i, :], CQRAW[:, i, t * TW:(t + 1) * TW], self.pkc(PK_QN + i), rs[:, :], ALU.mult, ALU.mult)
            if is_dec:
                rp = ROP[t % 2]
                P.dma(rp[:, :, :], dv(self.rope[:, :, t * TW:(t + 1) * TW]))
            for h in range(8):
                pa_ = self.ps.get()
                for i in range(2):
                    P.mm(pa_[:, 0:TW], WQ[:, i, h * 128:(h + 1) * 128], CQN[:, i, :], start=(i == 0), stop=(i == 1))
                qo = QO[h % 2]
                P.cp(qo[:, :], pa_[:, 0:TW], e="dve")
                if is_dec:
                    skip = os.environ.get('K_SKIP', '').split(',')
                    if 'qpb' in skip:
                        pb_ = pa_
                    else:
                        pb_ = self.ps.get()
                        for i in range(2):
                            P.mm(pb_[0:48, 0:TW], WQB[:, i, h * 64:h * 64 + 48], CQN[:, i, :], start=(i == 0), stop=(i == 1))
                    t1 = TMP[0]
                    t2 = TMP[1]
                    if 'qtt' not in skip:
                        P.tt(t1[0:48, :], pa_[0:48, 0:TW], rp[:, 0, :], ALU.mult)
                        P.tt(t2[0:48, :], pb_[0:48, 0:TW], rp[:, 1, :], ALU.mult)
                        P.tt(qo[0:48, :], t1[0:48, :], t2[0:48, :], ALU.add)
                P.dma(dv(self.qt[h * 128:(h + 1) * 128, t * TW:(t + 1) * TW]), qo[:, :], q="pool")

    def phase_ffn_up(self, l, L, hT, TW, NT, hook=None):
        P, A = self.P, self.A
        stg = [A.tile([128, 8, 128], F32, "wstg") for _ in range(2)]
        WG = [A.tile([128, 8, 128], BF16, "wg") for _ in range(2)]
        WU = [A.tile([128, 8, 128], BF16, "wu") for _ in range(2)]
        GB = [A.tile([128, L + 2], BF16, "gb") for _ in range(2)]
        UB = [A.tile([128, L + 2], BF16, "ub") for _ in range(2)]
        for r in GB + UB:
            P.memset(r[:, 0:1], 0.0)
            P.memset(r[:, L + 1:L + 2], 0.0)
        DGG = A.tile([128, 3, 128], BF16, "dgg")
        DGU = A.tile([128, 3, 128], BF16, "dgu")
        SGT = [A.tile([128, TW], F32, "sgt") for _ in range(2)]
        AO = [A.tile([128, TW], BF16, "ao") for _ in range(3)]
        n = 0
        for cg in range(22):
            wg, wu, gb, ub = WG[cg % 2], WU[cg % 2], GB[cg % 2], UB[cg % 2]
            self.load_wg(wg, self.wup[l, cg], stg[0], "dve")
            self.load_wg(wu, self.wup[l, 22 + cg], stg[1], "act")
            if hook is not None:
                hook(2)
            for t in range(NT):
                pg = self.ps.get()
                pu = self.ps.get()
                for k in range(8):
                    P.mm(pg[:, 0:TW], wg[:, k, :], hT[t][:, k, :], start=(k == 0), stop=(k == 7))
                for k in range(8):
                    P.mm(pu[:, 0:TW], wu[:, k, :], hT[t][:, k, :], start=(k == 0), stop=(k == 7))
                P.cp(gb[:, 1 + t * TW:1 + (t + 1) * TW], pg[:, 0:TW], e="act")
                P.cp(ub[:, 1 + t * TW:1 + (t + 1) * TW], pu[:, 0:TW], e="dve")
            for j in range(3):
                P.ts(DGG[:, j, :], self.IDB[:, :], self.pkc(PK_FCONV + cg * 3 + j), None, op0=ALU.mult)
                P.ts(DGU[:, j, :], self.IDB[:, :], self.pkc(PK_FCONV + (22 + cg) * 3 + j), None, op0=ALU.mult)
            for t in range(NT):
                pg = self.ps.get()
                pu = self.ps.get()
                for j in range(3):
                    P.mm(pg[:, 0:TW], DGG[:, j, :], gb[:, t * TW + j:t * TW + j + TW], start=(j == 0), stop=(j == 2))
                for j in range(3):
                    P.mm(pu[:, 0:TW], DGU[:, j, :], ub[:, t * TW + j:t * TW + j + TW], start=(j == 0), stop=(j == 2))
                sg = SGT[n % 2]
                ao = AO[n % 3]
                n += 1
                P.act(sg[:, :], pg[:, 0:TW], AF.Silu)
                P.tt(ao[:, :], pu[:, 0:TW], sg[:, :], ALU.mult)
                P.dma(dv(self.actT[cg * 128:(cg + 1) * 128, t * TW:(t + 1) * TW]), ao[:, :], q="pool")

    def phase_attn(self, l, L, is_dec, TW, NT, NK):
        P, A = self.P, self.A
        NKT = NK // 128
        scale = 96.0 ** -0.5
        CKVB = A.tile([128, NK], BF16, "ckvb")
        KRB = A.tile([48, NK], BF16, "krb")
        P.dma(CKVB[:, :], dv(self.ckvd[:, 0:NK]))
        P.dma(KRB[:, :], dv(self.krd[:, 0:NK]))
        WUK = A.tile([128, 1, 1024], BF16, "wuk")
        WUV = A.tile([128, 1, 512], BF16, "wuv")
        self.load_w_bf16(WUK, self.wuk[l], 1, 1024)
        self.load_w_bf16(WUV, self.wuv[l], 1, 512)
        KT = [A.tile([128, NK], BF16, "kt") for _ in range(2)]
        VA = [A.tile([128, NKT, 128], BF16, "va") for _ in range(2)]
        for va in VA:
            P.cp(va[:, :, 64:128], sub(self.ONESB, self.ONESB.t[:, :].unsqueeze(1).to_broadcast([128, NKT, 64])), e="pool")
        QT = [A.tile([128, L], BF16, "qth") for _ in range(2)]
        PT = [A.tile([128, TW], BF16, "pt") for _ in range(8)]
        REC = [A.tile([64, TW], F32, "rec") for _ in range(2)]
        OT = [A.tile([64, TW], BF16, "ot") for _ in range(2)]
        npt = 0
        pend = []
        LAG = 4

        def drain(n):
            while len(pend) > n:
                pend.pop(0)()

        for h in range(8):
            kt, va, qt = KT[h % 2], VA[h % 2], QT[h % 2]
            while pend and pend[0].head <= h - 2:
                pend.pop(0)()
            P.dma(qt[:, :], dv(self.qt[h * 128:(h + 1) * 128, 0:L]))
            for c0 in range(0, NK, 512):
                cw = min(512, NK - c0)
                pt = self.ps.get()
                P.mm(pt[:, 0:cw], WUK[:, 0, h * 128:(h + 1) * 128], CKVB[:, c0:c0 + cw], start=True, stop=False)
                P.mm(pt[:, 0:cw], self.IDB[0:48, :], KRB[:, c0:c0 + cw], start=False, stop=True)
                P.cp(kt[:, c0:c0 + cw], pt[:, 0:cw], e="dve")
            for k0 in range(0, NKT, 8):
                kn = min(8, NKT - k0)
                pt = self.ps.get()
                for kk in range(kn):
                    P.mm(pt[:, kk * 64:(kk + 1) * 64], CKVB[:, (k0 + kk) * 128:(k0 + kk + 1) * 128],
                         WUV[:, 0, h * 64:(h + 1) * 64])
                P.cp(va[:, k0:k0 + kn, 0:64], sub(pt, pt.t[:, 0:kn * 64].rearrange("p (k v) -> p k v", v=64)), e="dve")
            for t in range(NT):
                po = self.ps.reserve()
                for kti in range(NKT):
                    ps_ = self.ps.get()
                    P.mm(ps_[:, 0:TW], kt[:, kti * 128:(kti + 1) * 128], qt[:, t * TW:(t + 1) * TW])
                    pt_ = PT[npt % len(PT)]
                    npt += 1
                    P.act(pt_[:, :], ps_[:, 0:TW], AF.Exp, scale=scale)

                    def pv(po=po, kti=kti, pt_=pt_, va=va, t=t, h=h):
                        P.mm(po[:, 0:TW], va[:, kti, :], pt_[:, :], start=(kti == 0), stop=(kti == NKT - 1))
                        if kti == NKT - 1:
                            rec = REC[(h * NT + t) % 2]
                            P.recip(rec[:, :], po[64:128, 0:TW])
                            ot = OT[(h * NT + t) % 2]
                            P.tt(ot[:, :], po[0:64, 0:TW], rec[:, :], ALU.mult)
                            self.ps.free(po)
                            P.dma(dv(self.catd[256 + h * 64:256 + (h + 1) * 64, t * TW:(t + 1) * TW]), ot[:, :], q="pool")

                    pv.head = h
                    pend.append(pv)
                    drain(LAG)
        drain(0)

    def scan_tiles(self):
        import os
        A = self.A
        f = lambda shape, name: A.tile(shape, F32, name)
        h = lambda shape, name: A.tile(shape, BF16, name)
        mode = os.environ.get("K_DI", "r32")
        self.dI = {"f32": F32, "r32": F32R}.get(mode, BF16)
        self.hl = (mode == "hl")
        i_ = lambda shape, name: A.tile(shape, self.dI, name)
        T = {}
        T["FMS"] = [{k: [h([128, 128], "fm" + k) for _ in range(2)] for k in ("q", "k", "v", "x", "b", "c")} for _ in range(2)]
        T["TM"] = h([128, 4, 256], "tm")
        T["GC"], T["EG"], T["EGL"] = f([128, 8], "gc"), f([128, 8], "eg"), f([128, 8], "egl")
        T["NB"], T["BEG"] = f([128, 4], "nbeta"), f([128, 4], "beg")
        T["LAB"] = [f([128, 128], "lab") for _ in range(8)]
        T["EGRW"] = [f([128, 4, 128], "egrw") for _ in range(2)]
        T["DECT"] = [f([128, 4, 128], "dect") for _ in range(2)]
        T["DECS"] = f([128, 4, 128], "decs")
        T["MP"] = [i_([128, 4, 3 if self.hl else 2, 128], "mp") for _ in range(2)]
        T["AT"] = [i_([128, 4, 128], "at") for _ in range(2)]
        T["R"] = f([128, 4, 128], "r") if (self.hl or self.dI == F32R) else i_([128, 4, 128], "r")
        if self.hl:
            T["PF"] = f([128, 4, 128], "pf")
            T["PHL"] = h([128, 4, 2, 128], "phl")
        kv_ = f if (self.hl or self.dI == F32R) else i_
        T["KBG"], T["VB"] = kv_([128, 4, 64], "kbg"), kv_([128, 4, 64], "vb")
        T["WT"] = h([64, 4, 128], "wt")
        T["U"] = f([64, 2, 4, 64], "u")
        T["ATT"] = h([64, 2, 4, 64], "att")
        T["QDT"] = h([64, 4, 128], "qdt")
        T["KDEC"] = h([64, 2, 4, 64], "kdec")
        T["SCT"] = h([64, 2, 4, 64], "sct")
        T["XDT"] = h([64, 2, 4, 64], "xdt")
        T["BDEC"] = h([64, 2, 4, 128], "bdec")
        T["CDT"] = h([128, 4, 128], "cdt")
        T["VN"] = h([64, 4, 64], "vn")
        T["SG"] = [f([64, 4, 64], "sg") for _ in range(2)]
        T["SGB"] = [h([64, 4, 64], "sgb") for _ in range(2)]
        T["SS"] = [f([128, 4, 64], "ss") for _ in range(2)]
        T["SSB"] = [h([128, 4, 64], "ssb") for _ in range(2)]
        T["STMP"] = f([64, 4, 128], "stmp")
        return T

    def scan_dir(self, d, T, l, sq, L, is_dec, NB, ABT, OGt, OYt, obufs, touched):
        P = self.P
        ID = self.cst(C_ID)
        IDB = self.IDB
        rows = {"q": 0, "k": 256, "v": 512, "x": 1280, "b": 1536, "c": 1792}
        TRI = self.cst(C_TRIF + d)
        TRIS = self.cst(C_TRISF + d)
        b4 = lambda blk: sub(self.CON, self.CON.t[:, blk * 128:(blk + 1) * 128].unsqueeze(1).to_broadcast([128, 4, 128]))
        TRI4, TRIS4 = b4(C_TRIF + d), b4(C_TRISF + d)
        IDB4 = sub(IDB, IDB.t[:, :].unsqueeze(1).to_broadcast([128, 4, 128]))
        ID4 = b4(C_ID)
        FMS, TM = T["FMS"], T["TM"]
        GC, EG, EGL, NB_, BEG, LAB = T["GC"], T["EG"], T["EGL"], T["NB"], T["BEG"], T["LAB"]
        EGRW, DECT, DECS, MP, AT, R = T["EGRW"], T["DECT"], T["DECS"], T["MP"], T["AT"], T["R"]
        KBG, VB, WT, U, ATT, QDT, KDEC = T["KBG"], T["VB"], T["WT"], T["U"], T["ATT"], T["QDT"], T["KDEC"]
        SCT, XDT, BDEC, CDT, VN = T["SCT"], T["XDT"], T["BDEC"], T["CDT"], T["VN"]
        SG, SGB, SS, SSB, STMP = T["SG"], T["SGB"], T["SS"], T["SSB"], T["STMP"]
        h4 = lambda pt, n=128: sub(pt, pt.t[:, 0:4 * n].rearrange("p (h i) -> p h i", h=4))
        f32v = (lambda v: V(v.ap.bitcast(F32), v.buf)) if self.dI == F32R else (lambda v: v)
        sgi, ssi = 0, 0
        st = {"ssi": 0}
        import os
        scut = int(os.environ.get("K_SCUT", "100000"))
        MASKE = os.environ.get('K_MASKE', 'dve')
        self._ny = 0
        if is_dec:
            P.dma(SG[0][:, :, :], dv(self.stg[l, d].rearrange("h k v -> k h v")))
            P.dma(STMP[:, :, :], dv(self.sts[l, d].rearrange("h p n -> p h n")))
            pt = self.ps.get()
            for h in range(4):
                P.tr(pt[:, h * 64:(h + 1) * 64], STMP[:, h, :], self.cst(C_ID, 0, 64, 0, 64))
            P.cp(SS[0][:, :, :], h4(pt, 64))
        else:
            P.memset(SG[0][:, :, :], 0.0)
            P.memset(SS[0][:, :, :], 0.0)
        P.cp(SGB[0][:, :, :], SG[0][:, :, :], e="pool")
        P.cp(SSB[0][:, :, :], SS[0][:, :, :], e="pool")
        self._ny += 1
        if self._ny >= scut:
            return
        yield
        blocks = list(range(NB)) if d == 0 else list(range(NB - 1, -1, -1))

        def load_fm(bi):
            fm = FMS[bi % 2]
            tk = slice(blocks[bi] * 128, (blocks[bi] + 1) * 128)
            for k in fm:
                for g in range(2):
                    P.dma(fm[k][g][:, :], dv(self.pa[rows[k] + g * 128:rows[k] + (g + 1) * 128, tk]))

        load_fm(0)
        for bi, b in enumerate(blocks):
            tok = slice(b * 128, (b + 1) * 128)
            FM = FMS[bi % 2]
            if bi + 1 < len(blocks):
                load_fm(bi + 1)
            skip = os.environ.get('K_SKIP', '')
            for half, keys in enumerate((("k", "v"), ("x", "b"))):
                if 'tm' in skip:
                    break
                pt = self.ps.get()
                ptb = pt.t[:, :].bitcast(BF16)
                for i, k in enumerate(keys):
                    for g in range(2):
                        c0 = i * 256 + g * 128
                        P.tr(sub(pt, ptb[:, c0:c0 + 128]), FM[k][g][:, :], IDB[:, :])
                P.ts(sub(TM, TM.t[:, half * 2:half * 2 + 2, :].rearrange("p a c -> p (a c)")), sub(pt, ptb[:, 0:512]),
                     1.0, None, op0=ALU.mult)
            KT4 = sub(TM, TM.t[:, 0, :].rearrange("p (h v) -> p h v", h=4))
            VT4 = sub(TM, TM.t[:, 1, :].rearrange("p (h v) -> p h v", h=4))
            XT4 = sub(TM, TM.t[:, 2, :].rearrange("p (h v) -> p h v", h=4))
            BT = sub(TM, TM.t[:, 3, :])
            lasel = sub(ABT, ABT.t[:, b, 0:16].rearrange("p (t d h) -> p t d h", t=2, d=2)[:, :, d, :])
            pt = self.ps.get()
            P.mm(sub(pt, pt.t[:, 0:8].rearrange("p (t h) -> p t h", t=2)), TRI, lasel)
            P.mm(sub(pt, pt.t[:, 8:16].rearrange("p (t h) -> p t h", t=2)), TRIS, lasel)
            P.cp(GC[:, :], pt[:, 0:8])
            P.act(EG[:, :], pt[:, 0:8], AF.Exp)
            P.act(EGL[:, :], pt[:, 8:16], AF.Exp)
            beta = ABT[:, b, 32 + d * 4:36 + d * 4]
            dtc = ABT[:, b, 72 + d * 4:76 + d * 4]
            P.ts(NB_[:, :], beta, -1.0, None, op0=ALU.mult)
            P.tt(BEG[:, :], beta, EG[:, 0:4], ALU.mult)
            for ty in range(2):
                if 'lab' in skip:
                    break
                for h in range(4):
                    P.ts(LAB[ty * 4 + h][:, :], self.cst(C_ONES), ABT[:, b, ty * 8 + d * 4 + h:ty * 8 + d * 4 + h + 1],
                         None, op0=ALU.mult)
            self._ny += 1
            if self._ny >= scut:
                return
            yield
            pg = [self.ps.get(), self.ps.get()]
            for ty in range(2):
                for h in range(4):
                    P.mm(pg[ty][:, h * 128:(h + 1) * 128], LAB[ty * 4 + h][:, :], TRI)
            for h in range(4):
                P.ts(DECS[:, h, :], pg[0][:, h * 128:(h + 1) * 128], GC[:, h:h + 1], self.ZERO1[:, :], op0=ALU.subtract, op1=ALU.max)
            P.act(DECS[:, :, :], DECS[:, :, :], AF.Exp, scale=-1.0)
            P.tt(DECS[:, :, :], DECS[:, :, :], TRIS4, ALU.mult, e=MASKE)
            for ty in range(2):
                for h in range(4):
                    P.ts(DECT[ty][:, h, :], pg[ty][:, h * 128:(h + 1) * 128], GC[:, ty * 4 + h:ty * 4 + h + 1], self.ZERO1[:, :],
                         op0=ALU.subtract, op1=ALU.min)
                P.act(EGRW[ty][:, :, :], h4(pg[ty]), AF.Exp, after=[DECT[ty][:, :, :]] + ([DECS[:, :, :]] if ty == 0 else []))
                P.act(DECT[ty][:, :, :], DECT[ty][:, :, :], AF.Exp)
                P.tt(DECT[ty][:, :, :], DECT[ty][:, :, :], TRI4, ALU.mult, e=MASKE)
            self._ny += 1
            if self._ny >= scut:
                return
            yield
            mp, at = MP[0], AT[0]
            pkk = self.ps.get()
            for h in range(4):
                kf = FM["k"][h // 2][(h % 2) * 64:(h % 2) * 64 + 64, :]
                P.mm(pkk[:, h * 128:(h + 1) * 128], kf, kf)
            for h in range(4):
                P.stt(mp[:, h, 0, :], pkk[:, h * 128:(h + 1) * 128], NB_[:, h:h + 1], DECS[:, h, :], ALU.mult, ALU.mult)
            if self.dI == F32R:
                P.cp(mp[:, :, 1, :], ID4)
            else:
                P.cp(mp[:, :, 1, :], IDB4, e="pool")
            if self.hl:
                P.memset(mp[:, :, 2, :], 0.0)
                P.cp(T["PF"][:, :, :], ID4, e="pool")
            pt = self.ps.get()
            if self.dI == BF16:
                ptb = pt.t[:, :].bitcast(BF16)
                for h in range(4):
                    P.tr(sub(pt, ptb[:, h * 128:(h + 1) * 128]), mp[:, h, 0, :], IDB[:, :])
                P.cp(at[:, :, :], sub(pt, ptb[:, 0:512].rearrange("p (h i) -> p h i", h=4)), e="act")
            else:
                for h in range(4):
                    P.tr(pt[:, h * 128:(h + 1) * 128], f32v(mp[:, h, 0, :]), ID)
                P.cp(at[:, :, :], h4(pt), e="act")
            pqk = self.ps.get()
            for h in range(4):
                r0 = (h % 2) * 64
                P.mm(pqk[:, h * 128:(h + 1) * 128], FM["k"][h // 2][r0:r0 + 64, :], FM["q"][h // 2][r0:r0 + 64, :])
            pbc = self.ps.get()
            for gr in range(2):
                P.mm(pbc[:, gr * 128:(gr + 1) * 128], FM["b"][gr][:, :], FM["c"][gr][:, :])
            for c in range(2):
                cs = slice(c * 64, c * 64 + 64)
                P.tt(ATT[:, c, :, :], sub(pqk, pqk.t[cs, :].rearrange("p (h i) -> p h i", h=4)[:, :, cs]),
                     DECT[0][cs, :, cs], ALU.mult)
                for gr in range(2):
                    P.tt(SCT[:, c, gr * 2:gr * 2 + 2, :],
                         sub(pbc, pbc.t[cs, gr * 128 + c * 64:gr * 128 + c * 64 + 64].unsqueeze(1).to_broadcast([64, 2, 64])),
                         DECT[1][cs, gr * 2:gr * 2 + 2, cs], ALU.mult)
            self._ny += 1
            if self._ny >= scut:
                return
            yield
            def ssd_step(c):
                cs = slice(c * 64, c * 64 + 64)
                il = c * 64 + (63 if d == 0 else 0)
                ctok = slice(b * 128 + c * 64, b * 128 + c * 64 + 64)
                ck = b * 2 + c
                first = ("y", ck) not in touched
                touched.add(("y", ck))
                oyv = V(OYt.t[:, :, ctok], obufs[1][ck])
                ssi_ = st["ssi"]
                S, Sn, Sb, Sbn = SS[ssi_], SS[1 - ssi_], SSB[ssi_], SSB[1 - ssi_]
                st["ssi"] = 1 - ssi_
                po = self.ps.get()
                for h in range(4):
                    r0 = (h % 2) * 64
                    oreg = po[r0:r0 + 64, (h // 2) * 64:(h // 2) * 64 + 64]
                    P.mm(oreg, Sb[:, h, :], CDT[:, h, cs], start=True, stop=False)
                    P.mm(oreg, XDT[:, c, h, :], SCT[:, c, h, :], start=False, stop=True)
                pk_ = self.ps.get()
                for h in range(4):
                    P.mm(pk_[:, h * 64:(h + 1) * 64], BDEC[:, c, h, :], XDT[:, c, h, :])
                for h in range(4):
                    P.stt(Sn[:, h, :], S[:, h, :], EGRW[1][:, h, il:il + 1], pk_[:, h * 64:(h + 1) * 64], ALU.mult, ALU.add)
                P.cp(Sbn[:, :, :], Sn[:, :, :], e="pool")
                posrc = sub(po, po.t[:, 0:128].rearrange("p (g i) -> p g i", g=2))
                if first:
                    P.cp(oyv, posrc, e="act")
                else:
                    P.tt(oyv, posrc, oyv, ALU.add)

            corder = (0, 1) if d == 0 else (1, 0)
            cur = 0
            for lev in range(6):
                mp, at = MP[cur], AT[cur]
                mpn, atn = MP[1 - cur], AT[1 - cur]
                lastlev = (lev == 5)
                pM = [self.ps.get(), self.ps.get()]
                for h in range(4):
                    reg0 = (h % 2) * 256
                    if self.hl:
                        P.mm(pM[h // 2][:, reg0:reg0 + 256], at[:, h, :],
                             sub(mp, mp.t[:, h, 0:2, :].rearrange("p a i -> p (a i)")), start=True, stop=False)
                        P.mm(pM[h // 2][:, reg0 + 128:reg0 + 256], at[:, h, :], mp[:, h, 2, :], start=False, stop=True)
                    else:
                        P.mm(pM[h // 2][:, reg0:reg0 + 256], at[:, h, :],
                             sub(mp, mp.t[:, h, :, :].rearrange("p a i -> p (a i)")))
                if not lastlev:
                    pA = self.ps.get()
                    for h in range(4):
                        P.mm(pA[:, h * 128:(h + 1) * 128], mp[:, h, 0, :], at[:, h, :])
                    P.cp(atn[:, :, :], h4(pA), e="act")
                for hp in range(2):
                    src = pM[hp].t[:, :].rearrange("p (h a i) -> p h a i", h=2, a=2)
                    if not lastlev:
                        P.cp(mpn[:, hp * 2:hp * 2 + 2, 0, :], sub(pM[hp], src[:, :, 0, :]), e="act")
                    if self.hl:
                        PF = T["PF"]
                        P.tt(PF[:, hp * 2:hp * 2 + 2, :], sub(pM[hp], src[:, :, 1, :]), PF[:, hp * 2:hp * 2 + 2, :], ALU.add)
                    else:
                        P.tt(mpn[:, hp * 2:hp * 2 + 2, 1, :], sub(pM[hp], src[:, :, 1, :]), f32v(mp[:, hp * 2:hp * 2 + 2, 1, :]), ALU.add)
                if self.hl and not lastlev:
                    hi = mpn[:, :, 1, :]
                    lo = mpn[:, :, 2, :]
                    P.cp(hi, T["PF"][:, :, :], e="pool")
                    P.tt(lo, T["PF"][:, :, :], hi, ALU.subtract)
                cur = 1 - cur
                if lev == 0:
                    P.tt(KBG[:, :, :], KT4, sub(BEG, BEG.t[:, :].unsqueeze(2).to_broadcast([128, 4, 64])), ALU.mult)
                    P.tt(VB[:, :, :], VT4, sub(ABT, beta.ap.unsqueeze(2).to_broadcast([128, 4, 64])), ALU.mult)
                    for h in range(4):
                        r0 = (h % 2) * 64
                        P.tt(QDT[:, h, :], FM["q"][h // 2][r0:r0 + 64, :], EGRW[0][r0:r0 + 64, h, :], ALU.mult,
                             e=("pool" if h % 2 else "dve"))
                if lev == 1:
                    for c in range(2):
                        cs = slice(c * 64, c * 64 + 64)
                        P.tt(KDEC[:, c, :, :], sub(TM, KT4.ap[cs, :, :]),
                             sub(EGL, EGL.t[cs, 0:4].unsqueeze(2).to_broadcast([64, 4, 64])), ALU.mult)
                        P.tt(XDT[:, c, :, :], sub(TM, XT4.ap[cs, :, :]),
                             sub(ABT, dtc.ap[cs, :].unsqueeze(2).to_broadcast([64, 4, 64])), ALU.mult)
                if lev == 2:
                    for c in range(2):
                        cs = slice(c * 64, c * 64 + 64)
                        for h in range(4):
                            gr = h // 2
                            P.ts(BDEC[:, c, h, :], sub(TM, BT.ap[cs, gr * 128:(gr + 1) * 128]), EGL[cs, 4 + h:5 + h], None,
                                 op0=ALU.mult, e=("pool" if h % 2 else "dve"))
                if lev == 3:
                    for h in range(4):
                        P.tt(CDT[:, h, :], FM["c"][h // 2][:, :], EGRW[1][:, h, :], ALU.mult, e=("pool" if h % 2 else "dve"))
                if lev == 4:
                    ssd_step(corder[0])
                if lev == 5:
                    ssd_step(corder[1])
                self._ny += 1
                if self._ny >= scut:
                    return
                yield
            mp = MP[cur]
            pt = self.ps.get()
            if self.hl:
                for h in range(4):
                    P.tr(pt[:, h * 128:(h + 1) * 128], T["PF"][:, h, :], ID)
                P.cp(R[:, :, :], h4(pt), e="act")
            elif self.dI == BF16:
                ptb = pt.t[:, :].bitcast(BF16)
                for h in range(4):
                    P.tr(sub(pt, ptb[:, h * 128:(h + 1) * 128]), mp[:, h, 1, :], IDB[:, :])
                P.cp(R[:, :, :], sub(pt, ptb[:, 0:512].rearrange("p (h i) -> p h i", h=4)), e="act")
            else:
                for h in range(4):
                    P.tr(pt[:, h * 128:(h + 1) * 128], f32v(mp[:, h, 1, :]), ID)
                P.cp(R[:, :, :], h4(pt), e="act")
            self._ny += 1
            if self._ny >= scut:
                return
            yield
            pt = self.ps.get()
            for h in range(4):
                if False:
                    pass
                else:
                    P.mm(pt[0:64, h * 128:(h + 1) * 128], KBG[:, h, :], R[:, h, :])
            P.cp(WT[:, :, :], sub(pt, pt.t[0:64, :].rearrange("p (h i) -> p h i", h=4)), e="act")
            pt = self.ps.get()
            for c in range(2):
                cs = slice(c * 64, c * 64 + 64)
                for h in range(4):
                    oreg = pt[0:64, (c * 4 + h) * 64:(c * 4 + h + 1) * 64]
                    if False:
                        pass
                    else:
                        P.mm(oreg, R[cs, h, cs], VB[cs, h, :])
            P.cp(U[:, :, :, :], sub(pt, pt.t[0:64, :].rearrange("p (c h v) -> p c h v", c=2, h=4)))
            self._ny += 1
            if self._ny >= scut:
                return
            yield
            for c in ((0, 1) if d == 0 else (1, 0)):
                cs = slice(c * 64, c * 64 + 64)
                il = c * 64 + (63 if d == 0 else 0)
                ctok = slice(b * 128 + c * 64, b * 128 + c * 64 + 64)
                ck = b * 2 + c
                first = ck not in touched
                touched.add(ck)
                ogv = V(OGt.t[:, :, ctok], obufs[0][ck])
                oyv = V(OYt.t[:, :, ctok], obufs[1][ck])
                S, Sn, Sb, Sbn = SG[sgi], SG[1 - sgi], SGB[sgi], SGB[1 - sgi]
                sgi = 1 - sgi
                pw = self.ps.get()
                for h in range(4):
                    P.mm(pw[0:64, h * 64:(h + 1) * 64], WT[:, h, cs], Sb[:, h, :])
                P.tt(VN[:, :, :], U[:, c, :, :], sub(pw, pw.t[0:64, 0:256].rearrange("p (h v) -> p h v", h=4)), ALU.subtract)
                self._ny += 1
                if self._ny >= scut:
                    return
                yield
                pk_ = self.ps.get()
                for h in range(4):
                    P.mm(pk_[0:64, h * 64:(h + 1) * 64], KDEC[:, c, h, :], VN[:, h, :])
                po = self.ps.get()
                for h in range(4):
                    r0 = (h % 2) * 64
                    oreg = po[r0:r0 + 64, (h // 2) * 64:(h // 2) * 64 + 64]
                    P.mm(oreg, Sb[:, h, :], QDT[:, h, cs], start=True, stop=False)
                    P.mm(oreg, VN[:, h, :], ATT[:, c, h, :], start=False, stop=True)
                for h in range(4):
                    P.stt(Sn[:, h, :], S[:, h, :], EGRW[0][0:64, h, il:il + 1], pk_[0:64, h * 64:(h + 1) * 64], ALU.mult, ALU.add)
                P.cp(Sbn[:, :, :], Sn[:, :, :], e="pool")
                posrc = sub(po, po.t[:, 0:128].rearrange("p (g i) -> p g i", g=2))
                if first:
                    P.cp(ogv, posrc, e="act")
                else:
                    P.tt(ogv, posrc, ogv, ALU.add)
                self._ny += 1
                if self._ny >= scut:
                    return
                yield
        if not is_dec:
            P.dma(dv(self.osg[sq, l, d].rearrange("h k v -> k h v")), SG[sgi][:, :, :], q="pool")
            pt = self.ps.get()
            for h in range(4):
                P.tr(pt[0:64, h * 128:(h + 1) * 128], SS[st["ssi"]][:, h, :], ID)
            P.cp(STMP[:, :, :], sub(pt, pt.t[0:64, :].rearrange("p (h n) -> p h n", h=4)))
            P.dma(dv(self.oss[sq, l, d].rearrange("h p n -> p h n")), STMP[:, :, :], q="pool")

    def phase_scan(self, l, sq, L, is_dec, TW, NT, NB, ABT, OG, OY):
        P, A = self.P, self.A
        m_sc = A.mark()
        obufs = [[Buf() for _ in range(2 * NB)] for _ in range(2)]
        touched = set()
        if is_dec:
            tsets = [self.scan_tiles(), self.scan_tiles()]
            groups = [[0, 1]]
        else:
            ts1 = self.scan_tiles()
            tsets = [ts1, ts1]
            groups = [[0], [1]]
        for grp in groups:
            gens = [self.scan_dir(d, tsets[d], l, sq, L, is_dec, NB, ABT, OG, OY, obufs, touched) for d in grp]
            while gens:
                for g in list(gens):
                    try:
                        next(g)
                    except StopIteration:
                        gens.remove(g)
        self.barrier()
        A.release(m_sc)
        T_ = lambda shape, name: A.tile(shape, F32, name)
        GT = [A.tile([128, 2, TW], BF16, "gt") for _ in range(3)]
        TA = T_([128, 2, TW], "ta")
        TB = T_([128, 2, TW], "tb")
        RS = T_([128, TW], "rs")
        OB = [A.tile([128, 2, TW], BF16, "ob") for _ in range(2)]
        import os
        for t in range(NT):
            if 'epi' in os.environ.get('K_SKIP', ''):
                break
            ts_ = slice(t * TW, (t + 1) * TW)
            g = GT[0]
            P.dma(g[:, :, :], dv(self.pa[768:1024, ts_].rearrange("(g p) t -> p g t", p=128)))
            P.tt(TA[:, :, :], OG[:, :, ts_], OG[:, :, ts_], ALU.mult)
            ob = OB[0]
            for gi in range(2):
                pt = self.ps.get()
                P.mm(pt[:, 0:TW], self.cst(C_ONESBD), TA[:, gi, :])
                self.rstd_from(RS[:, :], pt[:, 0:TW], 64.0)
                P.stt(TB[:, gi, :], OG[:, gi, ts_], self.pkc(PK_GDNN), RS[:, :], ALU.mult, ALU.mult)
            P.tt(ob[:, :, :], TB[:, :, :], g[:, :, :], ALU.mult)
            P.dma(dv(self.catd[0:256, ts_].rearrange("(g p) t -> p g t", p=128)), ob[:, :, :], q="pool")
            z = GT[1]
            P.dma(z[:, :, :], dv(self.pa[1024:1280, ts_].rearrange("(g p) t -> p g t", p=128)))
            xs = GT[2]
            P.dma(xs[:, :, :], dv(self.pa[1280:1536, ts_].rearrange("(g p) t -> p g t", p=128)))
            for gi in range(2):
                P.stt(TA[:, gi, :], xs[:, gi, :], self.pkc(PK_SSDD + gi), OY[:, gi, ts_], ALU.mult, ALU.add)
            P.tt(TA[:, :, :], TA[:, :, :], z[:, :, :], ALU.mult)
            P.tt(TB[:, :, :], TA[:, :, :], TA[:, :, :], ALU.mult)
            pt = self.ps.get()
            for gi in range(2):
                P.mm(pt[:, 0:TW], self.cst(C_ONES), TB[:, gi, :], start=(gi == 0), stop=(gi == 1))
            self.rstd_from(RS[:, :], pt[:, 0:TW], 256.0)
            ob = OB[1]
            for gi in range(2):
                P.stt(ob[:, gi, :], TA[:, gi, :], self.pkc(PK_SSDN + gi), RS[:, :], ALU.mult, ALU.mult)
            P.dma(dv(self.catd[768:1024, ts_].rearrange("(g p) t -> p g t", p=128)), ob[:, :, :], q="pool")


def _consts():
    C = np.zeros((128, NCONST * 128), np.float32)
    idx = np.arange(128)
    sc = (idx[:, None] // 64) == (idx[None, :] // 64)
    t = idx[:, None]
    i = idx[None, :]

    def put(blk, m):
        C[:, blk * 128:(blk + 1) * 128] = m.astype(np.float32)

    put(C_ID, np.eye(128))
    for d in range(2):
        before = (t < i) if d == 0 else (t > i)
        after = (t > i) if d == 0 else (t < i)
        tri = sc & (before | (t == i))
        put(C_TRIF + d, tri)
        put(C_TRISF + d, sc & after)
        put(C_NEGTF + d, np.where(tri, 0.0, -30000.0))
        put(C_POSSF + d, np.where(sc & after, 0.0, 30000.0))
    put(C_ONESBD, sc)
    put(C_ONES, np.ones((128, 128)))
    return C


def _rope_table(L):
    tt = np.arange(L)
    r = (tt // GRID_W).astype(np.float32)
    col = (tt % GRID_W).astype(np.float32)
    nf = 8
    inv = (np.float32(10000.0) ** (-np.arange(nf, dtype=np.float32) / np.float32(nf))).astype(np.float32)
    ang = np.concatenate([r[:, None] * inv, col[:, None] * inv], axis=-1).astype(np.float32)
    cos, sin = np.cos(ang).astype(np.float32), np.sin(ang).astype(np.float32)
    R = np.zeros((48, 2, L), np.float32)
    R[0:16, 0] = cos.T
    R[32:48, 0] = cos.T
    R[0:16, 1] = -sin.T
    R[32:48, 1] = sin.T
    return R


def _fm(v, n):
    return np.ascontiguousarray(np.asarray(v, np.float32).reshape(n, 128).T)


def _prep_shared(inp, depth, L):
    f = lambda k: np.asarray(inp[k], np.float32)
    w_in = f("w_in")
    win = np.zeros((depth, D, NG_IN * 128), np.float32)
    win[:, :, 0:768] = w_in[:, :, 0:768]
    win[:, :, 768:1024] = w_in[:, :, 768:1024]
    win[:, :, G_CQ * 128:G_CQ * 128 + 256] = w_in[:, :, 1040:1296]
    win[:, :, G_CKV * 128:G_CKV * 128 + 128] = w_in[:, :, 1296:1424]
    win[:, :, G_KR * 128 + 0:G_KR * 128 + 16] = w_in[:, :, 1424:1440]
    win[:, :, G_KR * 128 + 32:G_KR * 128 + 48] = w_in[:, :, 1440:1456]
    win[:, :, G_KRB * 128 + 0:G_KRB * 128 + 16] = w_in[:, :, 1440:1456]
    win[:, :, G_KRB * 128 + 32:G_KRB * 128 + 48] = w_in[:, :, 1424:1440]
    win[:, :, G_AB * 128 + 0:G_AB * 128 + 8] = w_in[:, :, 1024:1032]
    win[:, :, G_AB * 128 + 8:G_AB * 128 + 16] = w_in[:, :, 2480:2488]
    win[:, :, G_AB * 128 + 32:G_AB * 128 + 40] = w_in[:, :, 1032:1040]
    win[:, :, G_Z * 128:G_Z * 128 + 256] = w_in[:, :, 1456:1712]
    win[:, :, G_XS * 128:G_XS * 128 + 768] = w_in[:, :, 1712:2480]
    pk = np.zeros((depth, 128, NPK), np.float32)
    p = np.arange(128)
    for l in range(depth):
        pk[l, :, PK_NMP:PK_NMP + 8] = _fm(f("norm_mix_pre")[l], 8)
        pk[l, :, PK_NMO:PK_NMO + 8] = _fm(f("norm_mix_post")[l], 8)
        pk[l, :, PK_NFP:PK_NFP + 8] = _fm(f("norm_ffn_pre")[l], 8)
        pk[l, :, PK_NFO:PK_NFO + 8] = _fm(f("norm_ffn_post")[l], 8)
        pk[l, :, PK_BADA:PK_BADA + 48] = _fm(f("b_ada")[l], 48)
        for i in range(6):
            pk[l, :, PK_GCONV + i * 5:PK_GCONV + i * 5 + 5] = f("gdn_conv")[l][:, i * 128:(i + 1) * 128].T
            pk[l, :, PK_SCONV + i * 5:PK_SCONV + i * 5 + 5] = f("ssd_conv")[l][:, i * 128:(i + 1) * 128].T
            pk[l, :, PK_SCB + i] = f("ssd_conv_b")[l][i * 128:(i + 1) * 128]
        for cg in range(44):
            pk[l, :, PK_FCONV + cg * 3:PK_FCONV + cg * 3 + 3] = f("ffn_conv")[l][:, cg * 128:(cg + 1) * 128].T
        pk[l, :, PK_QN:PK_QN + 2] = _fm(f("mla_q_norm")[l], 2)
        pk[l, :, PK_KVN] = f("mla_kv_norm")[l]
        pk[l, :, PK_GDNN] = f("gdn_norm")[l][p % 64]
        pk[l, :, PK_SSDN:PK_SSDN + 2] = _fm(f("ssd_norm")[l], 2)
        for gi in range(2):
            pk[l, :, PK_SSDD + gi] = f("ssd_d")[l][2 * gi + p // 64]
        pk[l, 0:8, PK_ALOG] = f("gdn_a_log")[l].reshape(8)
        pk[l, 8:16, PK_ALOG] = f("ssd_a_log")[l].reshape(8)
        pk[l, 0:8, PK_DTB] = f("gdn_dt_bias")[l].reshape(8)
        pk[l, 8:16, PK_DTB] = f("ssd_dt_bias")[l].reshape(8)
    w_uq = f("mla_w_uq")
    wuq = np.zeros((depth, 256, 8 * 128), np.float32)
    wuqb = np.zeros((depth, 256, 8 * 64), np.float32)
    w_ukv = f("mla_w_ukv")
    wuk = np.zeros((depth, 128, 8 * 128), np.float32)
    wuv = np.zeros((depth, 128, 8 * 64), np.float32)
    for h in range(8):
        x1 = w_uq[:, :, h * 96 + 64:h * 96 + 80]
        x2 = w_uq[:, :, h * 96 + 80:h * 96 + 96]
        wuq[:, :, h * 128 + 0:h * 128 + 16] = x1
        wuq[:, :, h * 128 + 32:h * 128 + 48] = x2
        wuq[:, :, h * 128 + 64:h * 128 + 128] = w_uq[:, :, h * 96:h * 96 + 64]
        wuqb[:, :, h * 64 + 0:h * 64 + 16] = x2
        wuqb[:, :, h * 64 + 32:h * 64 + 48] = x1
        wuk[:, :, h * 128 + 64:h * 128 + 128] = w_ukv[:, :, h * 128:h * 128 + 64]
        wuv[:, :, h * 64:(h + 1) * 64] = w_ukv[:, :, h * 128 + 64:h * 128 + 128]
    def grp(w):
        dd, _, gc = w.shape
        g = gc // 128
        return np.ascontiguousarray(w.reshape(dd, 8, 128, g, 128).transpose(0, 3, 2, 1, 4)).reshape(dd, g, 128, 1024)
    return {"wada": grp(f("w_ada")), "win": grp(win), "pk": pk, "wuq": wuq, "wuqb": wuqb, "wuk": wuk, "wuv": wuv,
            "wout": f("w_out"), "wup": grp(f("ffn_w_up")), "wdn": f("ffn_w_down"), "consts": _consts(),
            "rope": _rope_table(L)}


def _in_maps(inp, ncores, depth, npr, L):
    sh = _prep_shared(inp, depth, L)
    f = lambda k: np.asarray(inp[k], np.float32)
    ndec = f("x_sample").shape[0]
    ckr = f("cache_mla_krope")
    ckr48 = np.zeros(ckr.shape[:-1] + (48,), np.float32)
    ckr48[..., 0:16] = ckr[..., 0:16]
    ckr48[..., 32:48] = ckr[..., 16:32]
    maps = []
    for c in range(ncores):
        b = c % ndec
        m = dict(sh)
        m["xp"] = np.ascontiguousarray(f("x_prompt")[c * npr:(c + 1) * npr])
        m["xs"] = np.ascontiguousarray(f("x_sample")[b])
        m["cckv"] = np.ascontiguousarray(f("cache_mla_ckv")[b])
        m["ckr"] = np.ascontiguousarray(ckr48[b])
        m["stg"] = np.ascontiguousarray(f("state_gdn")[b])
        m["sts"] = np.ascontiguousarray(f("state_ssd")[b])
        ct = np.zeros((128, 8, 2), np.float32)
        ct[:, :, 0] = f("c")[b].reshape(8, 128).T
        ct[:, :, 1] = f("c_ctx").reshape(8, 128).T
        m["condT"] = ct
        maps.append(m)
    return maps


def run(inp, ncores=NCORES, depth=DEPTH, seq=SEQ, dec_seq=DEC_SEQ, past=PAST, npr=NPR, debug=None, stop=99, only=None):
    bld = Builder(depth=depth, seq=seq, dec_seq=dec_seq, past=past, npr=npr, debug=debug, stop=stop, only=only)
    maps = _in_maps(inp, ncores, depth, npr, dec_seq)
    res = run_bass_kernel_spmd(bld.nc, maps, core_ids=list(range(ncores)))
    return res.results, bld


def kernel(**inputs):
    res, _ = run(inputs)
    ndec = DEC_BATCH
    y_prompt = np.concatenate([res[c]["yp"] for c in range(NCORES)], axis=0).astype(np.float32)
    y_sample = np.stack([res[b]["ys"] for b in range(ndec)], axis=0).astype(np.float32)
    ckv = np.concatenate([res[c]["ockv"] for c in range(NCORES)], axis=0).astype(np.float32)
    kr = np.concatenate([res[c]["okr"] for c in range(NCORES)], axis=0).astype(np.float32)
    sg = np.concatenate([res[c]["osg"] for c in range(NCORES)], axis=0).astype(np.float32)
    ss = np.concatenate([res[c]["oss"] for c in range(NCORES)], axis=0).astype(np.float32)
    return (y_prompt, y_sample, ckv, kr, sg, ss)
```
